# Optimizing a Trainium2 kernel written in Bass

```python
import jax, jax.numpy as jnp
from jax import lax
import numpy as np

D_MODEL = 4096
BATCH = 4
SEQ = 2048
DEPTH = 2

N_A_LAYERS = DEPTH // 2
N_B_LAYERS = DEPTH - N_A_LAYERS
D_FF = 11008
LRU_WIDTH = D_MODEL
LRU_BLOCKS = 16
LRU_BLOCK_DIM = LRU_WIDTH // LRU_BLOCKS
CONV_WIDTH = 4
LRU_C = 8.0
N_HEADS = 32
QK_NOPE_DIM = 128
QK_ROPE_DIM = 64
QK_HEAD_DIM = QK_NOPE_DIM + QK_ROPE_DIM
V_HEAD_DIM = 128
Q_LORA_RANK = 1024
KV_LORA_RANK = 512
ROPE_THETA = 10000.0
Q_BLOCK = 128
NORM_EPS = 1e-6
N_SUBLAYER_NORMS = 6
MAX_POS_OFFSET = 4096

kernel_name = "yoco_rglru_mla_macaron_sandwich"


def _rmsnorm(x, g):
    x32 = x.astype(jnp.float32)
    y = x32 * lax.rsqrt(jnp.mean(x32 * x32, axis=-1, keepdims=True) + NORM_EPS)
    return (y * g.astype(jnp.float32)).astype(x.dtype)


def _swiglu(hn, w_gate_up, w_down):
    gu = hn @ w_gate_up
    return (jax.nn.silu(gu[..., :D_FF]) * gu[..., D_FF:]) @ w_down


def _rope(x, cos, sin):
    half = x.shape[-1] // 2
    x1, x2 = x[..., :half], x[..., half:]
    return jnp.concatenate([x1 * cos - x2 * sin, x2 * cos + x1 * sin], axis=-1)


def _linear_combine(c1, c2):
    a1, b1 = c1
    a2, b2 = c2
    return a1 * a2, a2 * b1 + b2


def _rglru_mixer(hn, w_in, conv_w, conv_b, w_ga, b_ga, w_gx, b_gx, lam, w_out):
    b, s, _ = hn.shape
    proj = hn @ w_in
    gate_branch = jax.nn.gelu(proj[..., :LRU_WIDTH], approximate=True)
    xb = proj[..., LRU_WIDTH:]
    xp = jnp.pad(xb, ((0, 0), (CONV_WIDTH - 1, 0), (0, 0)))
    xc = conv_b + sum(xp[:, k:k + s] * conv_w[k] for k in range(CONV_WIDTH))
    xg = xc.reshape(b, s, LRU_BLOCKS, LRU_BLOCK_DIM)
    r = jax.nn.sigmoid(jnp.einsum('bsgi,gij->bsgj', xg, w_ga) + b_ga).reshape(b, s, LRU_WIDTH)
    i = jax.nn.sigmoid(jnp.einsum('bsgi,gij->bsgj', xg, w_gx) + b_gx).reshape(b, s, LRU_WIDTH)
    log_a = (-LRU_C * r.astype(jnp.float32)) * jax.nn.softplus(-lam.astype(jnp.float32))
    a = jnp.exp(log_a)
    mult = jnp.sqrt(-jnp.expm1(2.0 * log_a))
    u = mult * (i * xc).astype(jnp.float32)
    _, h_seq = lax.associative_scan(_linear_combine, (a, u), axis=1)
    y = h_seq.astype(hn.dtype) * gate_branch
    return y @ w_out


def _shared_kv(h, cos, sin, kv_norm_in, kv_w_down, kv_latent_norm, kv_w_up):
    b, s, _ = h.shape
    hn = _rmsnorm(h, kv_norm_in)
    ckr = hn @ kv_w_down
    c_kv = _rmsnorm(ckr[..., :KV_LORA_RANK], kv_latent_norm)
    k_rope = _rope(ckr[..., KV_LORA_RANK:], cos, sin)
    kv = (c_kv @ kv_w_up).reshape(b, s, N_HEADS, QK_NOPE_DIM + V_HEAD_DIM)
    return kv[..., :QK_NOPE_DIM], k_rope, kv[..., QK_NOPE_DIM:]


def _causal_block_attention(q_nope, q_rope, k_nope, k_rope, v):
    b, s, h, _ = q_nope.shape
    n_blocks = s // Q_BLOCK
    scale = QK_HEAD_DIM ** -0.5
    qn = q_nope.reshape(b, n_blocks, Q_BLOCK, h, QK_NOPE_DIM).transpose(1, 0, 2, 3, 4)
    qr = q_rope.reshape(b, n_blocks, Q_BLOCK, h, QK_ROPE_DIM).transpose(1, 0, 2, 3, 4)
    k_idx = jnp.arange(s)
    neg = jnp.finfo(jnp.float32).min

    def one_block(args):
        qn_blk, qr_blk, blk = args
        scores = (jnp.einsum('bqhd,bkhd->bhqk', qn_blk, k_nope, preferred_element_type=jnp.float32)
                  + jnp.einsum('bqhr,bkr->bhqk', qr_blk, k_rope, preferred_element_type=jnp.float32))
        q_idx = blk * Q_BLOCK + jnp.arange(Q_BLOCK)
        mask = k_idx[None, :] <= q_idx[:, None]
        scores = jnp.where(mask, scores * scale, neg)
        p = jax.nn.softmax(scores, axis=-1).astype(v.dtype)
        return jnp.einsum('bhqk,bkhd->bqhd', p, v)

    out = lax.map(one_block, (qn, qr, jnp.arange(n_blocks)))
    return out.transpose(1, 0, 2, 3, 4).reshape(b, s, h * V_HEAD_DIM)


def _mla_mixer(hn, cos, sin, w_dq, q_norm, w_uq, w_o, k_nope, k_rope, v):
    b, s, _ = hn.shape
    q = (_rmsnorm(hn @ w_dq, q_norm) @ w_uq).reshape(b, s, N_HEADS, QK_HEAD_DIM)
    q_nope = q[..., :QK_NOPE_DIM]
    q_rope = _rope(q[..., QK_NOPE_DIM:], cos[:, :, None, :], sin[:, :, None, :])
    return _causal_block_attention(q_nope, q_rope, k_nope, k_rope, v) @ w_o


def setup_inputs(seed: int = 0) -> dict:
    key = jax.random.key(seed)
    ks = jax.random.split(key, 24)
    f32 = jnp.float32

    def nrm(k, shape, fan_in):
        return jax.random.normal(k, shape, f32) * (fan_in ** -0.5)

    def gain(k, shape):
        return 1.0 + 0.02 * jax.random.normal(k, shape, f32)

    def bias(k, shape):
        return 0.01 * jax.random.normal(k, shape, f32)

    x = jax.random.normal(ks[0], (BATCH, SEQ, D_MODEL), f32)
    offset = jax.random.randint(ks[1], (BATCH, 1), 0, MAX_POS_OFFSET, dtype=jnp.int32)
    positions = offset + jnp.arange(SEQ, dtype=jnp.int32)[None, :]
    a0 = jax.random.uniform(ks[10], (N_A_LAYERS, LRU_WIDTH), f32, 0.9, 0.999)
    s0 = a0 ** (1.0 / LRU_C)
    lru_lambda = jnp.log(s0) - jnp.log1p(-s0)
    return {
        "x": x,
        "positions": positions,
        "norm_gains": gain(ks[2], (DEPTH, N_SUBLAYER_NORMS, D_MODEL)),
        "ffn_w_gate_up": nrm(ks[3], (DEPTH, 2, D_MODEL, 2 * D_FF), D_MODEL),
        "ffn_w_down": nrm(ks[4], (DEPTH, 2, D_FF, D_MODEL), D_FF),
        "lru_w_in": nrm(ks[5], (N_A_LAYERS, D_MODEL, 2 * LRU_WIDTH), D_MODEL),
        "lru_conv_w": nrm(ks[6], (N_A_LAYERS, CONV_WIDTH, LRU_WIDTH), CONV_WIDTH),
        "lru_conv_b": bias(ks[7], (N_A_LAYERS, LRU_WIDTH)),
        "lru_w_gate_a": nrm(ks[8], (N_A_LAYERS, LRU_BLOCKS, LRU_BLOCK_DIM, LRU_BLOCK_DIM), LRU_BLOCK_DIM),
        "lru_b_gate_a": bias(ks[9], (N_A_LAYERS, LRU_BLOCKS, LRU_BLOCK_DIM)),
        "lru_w_gate_x": nrm(ks[11], (N_A_LAYERS, LRU_BLOCKS, LRU_BLOCK_DIM, LRU_BLOCK_DIM), LRU_BLOCK_DIM),
        "lru_b_gate_x": bias(ks[12], (N_A_LAYERS, LRU_BLOCKS, LRU_BLOCK_DIM)),
        "lru_lambda": lru_lambda,
        "lru_w_out": nrm(ks[13], (N_A_LAYERS, LRU_WIDTH, D_MODEL), LRU_WIDTH),
        "kv_norm_in": gain(ks[14], (D_MODEL,)),
        "kv_w_down": nrm(ks[15], (D_MODEL, KV_LORA_RANK + QK_ROPE_DIM), D_MODEL),
        "kv_latent_norm": gain(ks[16], (KV_LORA_RANK,)),
        "kv_w_up": nrm(ks[17], (KV_LORA_RANK, N_HEADS * (QK_NOPE_DIM + V_HEAD_DIM)), KV_LORA_RANK),
        "mla_w_dq": nrm(ks[18], (N_B_LAYERS, D_MODEL, Q_LORA_RANK), D_MODEL),
        "mla_q_norm": gain(ks[19], (N_B_LAYERS, Q_LORA_RANK)),
        "mla_w_uq": nrm(ks[20], (N_B_LAYERS, Q_LORA_RANK, N_HEADS * QK_HEAD_DIM), Q_LORA_RANK),
        "mla_w_o": nrm(ks[21], (N_B_LAYERS, N_HEADS * V_HEAD_DIM, D_MODEL), N_HEADS * V_HEAD_DIM),
    }


def reference(x, positions, norm_gains, ffn_w_gate_up, ffn_w_down, lru_w_in, lru_conv_w, lru_conv_b,
              lru_w_gate_a, lru_b_gate_a, lru_w_gate_x, lru_b_gate_x, lru_lambda, lru_w_out,
              kv_norm_in, kv_w_down, kv_latent_norm, kv_w_up, mla_w_dq, mla_q_norm, mla_w_uq, mla_w_o):
    inv_freq = ROPE_THETA ** (-jnp.arange(0, QK_ROPE_DIM, 2, dtype=jnp.float32) / QK_ROPE_DIM)
    ang = positions.astype(jnp.float32)[..., None] * inv_freq
    cos = jnp.cos(ang).astype(x.dtype)
    sin = jnp.sin(ang).astype(x.dtype)

    h = x
    k_nope = k_rope = v = None
    for layer in range(DEPTH):
        g = norm_gains[layer]
        h = h + 0.5 * _rmsnorm(_swiglu(_rmsnorm(h, g[0]), ffn_w_gate_up[layer, 0], ffn_w_down[layer, 0]), g[1])
        if layer < N_A_LAYERS:
            la = layer
            mix = _rglru_mixer(_rmsnorm(h, g[2]), lru_w_in[la], lru_conv_w[la], lru_conv_b[la],
                               lru_w_gate_a[la], lru_b_gate_a[la], lru_w_gate_x[la], lru_b_gate_x[la],
                               lru_lambda[la], lru_w_out[la])
        else:
            if layer == N_A_LAYERS:
                k_nope, k_rope, v = _shared_kv(h, cos, sin, kv_norm_in, kv_w_down, kv_latent_norm, kv_w_up)
            lb = layer - N_A_LAYERS
            mix = _mla_mixer(_rmsnorm(h, g[2]), cos, sin, mla_w_dq[lb], mla_q_norm[lb], mla_w_uq[lb],
                             mla_w_o[lb], k_nope, k_rope, v)
        h = h + _rmsnorm(mix, g[3])
        h = h + 0.5 * _rmsnorm(_swiglu(_rmsnorm(h, g[4]), ffn_w_gate_up[layer, 1], ffn_w_down[layer, 1]), g[5])
    return h
```

```python
import contextlib
import numpy as np
import concourse.bass as bass
import concourse.mybir as mybir
from concourse.bass_utils import run_bass_kernel_spmd

F32 = mybir.dt.float32
BF16 = mybir.dt.bfloat16
I32 = mybir.dt.int32
AF = mybir.ActivationFunctionType
ALU = mybir.AluOpType
AX = mybir.AxisListType
NCORES = 8
T = 1024
TH = 512
NEG = -1000.0
NEGB = -80.0


class Cfg:
    def __init__(self, D=4096, F=11008, NH=32, QL=1024, KL=512, NBLK=16):
        self.D, self.F, self.NH, self.QL, self.KL, self.NBLK = D, F, NH, QL, KL, NBLK
        self.ND = D // 128
        self.NF = F // 128
        base = self.NF // 3
        rem = self.NF - 3 * base
        self.PARTS = []
        o = 0
        for i in range(3):
            n = base + (1 if i < rem else 0)
            self.PARTS.append((o, n))
            o += n
        self.NFH = max(n for _, n in self.PARTS)
        self.KQ = QL // 128
        self.KK = KL // 128


class Buf:
    __slots__ = ("name", "last_w", "readers")

    def __init__(self, name=""):
        self.name = name
        self.last_w = None
        self.readers = []


class Op:
    __slots__ = ("eng", "fn", "idx", "dma", "deps", "signaled", "count", "dsem", "dval", "extra")

    def __init__(self, eng, fn, idx, dma):
        self.eng, self.fn, self.idx, self.dma = eng, fn, idx, dma
        self.deps = []
        self.signaled = False
        self.count = 0
        self.dsem = None
        self.dval = 0
        self.extra = []


ENGS = ("tensor", "vector", "scalar", "gpsimd", "sync")
NDSEM = 6


def _freeze(fn):
    import types
    if getattr(fn, "__closure__", None) is None:
        return fn
    cells = []
    for c in fn.__closure__:
        try:
            cells.append(types.CellType(c.cell_contents))
        except ValueError:
            cells.append(c)
    g = types.FunctionType(fn.__code__, fn.__globals__, fn.__name__, fn.__defaults__, tuple(cells))
    g.__kwdefaults__ = fn.__kwdefaults__
    return g


class Stage:
    def __init__(self, nc, name):
        self.nc = nc
        self.name = name
        self.ops = {e: [] for e in ENGS}
        self.ndma = {e: 0 for e in ENGS}

    def add(self, eng, fn, reads=(), writes=(), dma=False, extra=()):
        op = Op(eng, _freeze(fn), len(self.ops[eng]), dma)
        op.extra = list(extra)
        deps = {}

        def dep(o):
            if o is None or o is op:
                return
            if o.dma:
                deps[("d", id(o))] = o
            else:
                if o.eng == "tensor" and eng == "tensor" and not dma:
                    return
                k = ("c", o.eng)
                if k not in deps or deps[k].idx < o.idx:
                    deps[k] = o

        for b in reads:
            dep(b.last_w)
        for b in writes:
            dep(b.last_w)
            for r in b.readers:
                dep(r)
        op.deps = list(deps.values())
        for o in op.deps:
            o.signaled = True
        for b in reads:
            if not dma:
                b.readers = [r for r in b.readers if r.dma or r.eng != eng]
            b.readers.append(op)
        for b in writes:
            b.last_w = op
            b.readers = []
        if dma:
            n = self.ndma[eng]
            self.ndma[eng] = n + 1
            op.dsem = n % NDSEM
            op.dval = 16 * (n // NDSEM + 1)
        self.ops[eng].append(op)
        return op

    SEMS = None

    def emit(self):
        nc = self.nc
        with contextlib.ExitStack() as es:
            csem, dsem_all, extra_clear = Stage.SEMS
            dsem = {e: dsem_all[e] for e in ENGS if self.ndma[e] > 0}
            allsems = list(csem.values()) + [x for e in ENGS for x in dsem_all[e]] + list(extra_clear)
            extra_clear.clear()
            with nc.Block(f"{self.name}_clr") as cb:
                def clr(eng):
                    for sm_ in allsems:
                        eng.sem_clear(sm_)
                cb.vector(clr)
            for e in ENGS:
                c = 0
                for op in self.ops[e]:
                    if op.signaled and not op.dma:
                        c += 1
                        op.count = c
            block = es.enter_context(nc.Block(f"{self.name}"))

            def body(ename):
                def f(eng):
                    seen = {}

                    def wait(sem, val):
                        k = id(sem)
                        if seen.get(k, 0) >= val:
                            return
                        seen[k] = val
                        eng.wait_ge(sem, val)

                    for op in self.ops[ename]:
                        for (s, v) in op.extra:
                            wait(s, v)
                        for d in op.deps:
                            if d.dma:
                                wait(dsem[d.eng][d.dsem], d.dval)
                            else:
                                wait(csem[d.eng], d.count)
                        if op.dma:
                            if op.dval > 16:
                                wait(dsem[ename][op.dsem], op.dval - 16)
                            op.fn(eng).then_inc(dsem[ename][op.dsem], 16)
                        else:
                            ins = op.fn(eng)
                            if op.signaled:
                                ins.then_inc(csem[ename], 1)
                    if ename in dsem:
                        n = self.ndma[ename]
                        for i in range(NDSEM):
                            cnt = len(range(i, n, NDSEM))
                            if cnt:
                                wait(dsem[ename][i], 16 * cnt)
                return f

            for e in ENGS:
                if self.ops[e]:
                    getattr(block, e)(body(e))


def pack_cols(W, col_lists, wc):
    K, N = W.shape
    KT = K // 128
    nch = len(col_lists)
    idx = np.zeros((nch, wc), np.int64)
    valid = np.zeros((nch, wc), bool)
    for c, cl in enumerate(col_lists):
        idx[c, :len(cl)] = cl
        valid[c, :len(cl)] = True
    Wr = W.reshape(KT, 128, N)
    out = Wr[:, :, idx.reshape(-1)].reshape(KT, 128, nch, wc)
    if not valid.all():
        out = np.where(valid[None, None], out, np.float32(0))
    out = np.ascontiguousarray(out.transpose(2, 1, 0, 3)).reshape(nch, 128, KT * wc)
    npad = (-nch) % NCORES
    if npad:
        out = np.concatenate([out, np.zeros((npad, 128, KT * wc), np.float32)], 0)
    return out


def weight_groups(cfg, inp):
    D, F, NH = cfg.D, cfg.F, cfg.NH
    g = {}
    ar = np.arange

    def ffn(q):
        l, s = q // 2, q % 2
        Wgu = np.asarray(inp["ffn_w_gate_up"][l, s])
        Wdn = np.asarray(inp["ffn_w_down"][l, s])
        for hf, (p0, pn) in enumerate(cfg.PARTS):
            cl = []
            for j in range(pn):
                f0 = (p0 + j) * 128
                cl.append(ar(f0, f0 + 128))
                cl.append(ar(F + f0, F + f0 + 128))
            g[f"gu{q}{hf}"] = pack_cols(Wgu, cl, 128)
            r0 = p0 * 128
            g[f"dn{q}{hf}"] = pack_cols(Wdn[r0:r0 + pn * 128], [ar(i * 128, i * 128 + 128) for i in range(cfg.ND)], 128)

    ffn(0)
    g["win"] = pack_cols(np.asarray(inp["lru_w_in"][0]), [ar(i * 128, i * 128 + 128) for i in range(2 * cfg.ND)], 128)
    wa, wx = np.asarray(inp["lru_w_gate_a"][0]), np.asarray(inp["lru_w_gate_x"][0])
    ch = []
    for b in range(cfg.NBLK):
        for w in (wa, wx):
            for nt in range(2):
                ch.append(pack_cols(w[b], [ar(nt * 128, nt * 128 + 128)], 128)[0])
    gt = np.stack(ch)
    npad = (-len(ch)) % NCORES
    if npad:
        gt = np.concatenate([gt, np.zeros((npad,) + gt.shape[1:], np.float32)])
    g["gates"] = gt
    g["wout"] = pack_cols(np.asarray(inp["lru_w_out"][0]), [ar(i * 128, i * 128 + 128) for i in range(cfg.ND)], 128)
    ffn(1)
    ffn(2)
    KL = cfg.KL
    cl = [ar(i * 128, i * 128 + 128) for i in range(cfg.KK)]
    cl.append(ar(KL, KL + 64))
    cl.append(np.concatenate([ar(KL + 32, KL + 64), ar(KL, KL + 32)]))
    g["kvd"] = pack_cols(np.asarray(inp["kv_w_down"]), cl, 128)
    g["wdq"] = pack_cols(np.asarray(inp["mla_w_dq"][0]), [ar(i * 128, i * 128 + 128) for i in range(cfg.KQ)], 128)
    cl = []
    for h in range(NH):
        b0 = h * 192
        cl.append(ar(b0, b0 + 128))
        cl.append(ar(b0 + 128, b0 + 192))
        cl.append(np.concatenate([ar(b0 + 160, b0 + 192), ar(b0 + 128, b0 + 160)]))
    g["wuq"] = pack_cols(np.asarray(inp["mla_w_uq"][0]), cl, 128)
    cl = []
    for h in range(NH):
        cl.append(ar(h * 256, h * 256 + 128))
        cl.append(ar(h * 256 + 128, h * 256 + 256))
    g["kvu"] = pack_cols(np.asarray(inp["kv_w_up"]), cl, 128)
    g["wo"] = pack_cols(np.asarray(inp["mla_w_o"][0]), [ar(i * 128, i * 128 + 128) for i in range(cfg.ND)], 128)
    ffn(3)
    return g


def small_vectors(cfg, inp):
    cols = {}
    tabs = []

    def put(name, v):
        v = np.asarray(v, np.float32).reshape(-1)
        n = (len(v) + 127) // 128
        vv = np.zeros(n * 128, np.float32)
        vv[:len(v)] = v
        cols[name] = sum(t.shape[1] for t in tabs)
        tabs.append(vv.reshape(n, 128).T)

    ng = np.asarray(inp["norm_gains"])
    for l in range(2):
        for i in range(6):
            put(f"g{l}{i}", ng[l, i])
    for k in range(4):
        put(f"cw{k}", np.asarray(inp["lru_conv_w"])[0, k])
    put("cb", np.asarray(inp["lru_conv_b"])[0])
    put("bga", np.asarray(inp["lru_b_gate_a"])[0])
    put("bgx", np.asarray(inp["lru_b_gate_x"])[0])
    put("lam", np.asarray(inp["lru_lambda"])[0])
    put("kvn", inp["kv_norm_in"])
    put("kln", inp["kv_latent_norm"])
    put("qn", np.asarray(inp["mla_q_norm"])[0])
    return np.ascontiguousarray(np.concatenate(tabs, 1)), cols


def build(cfg, gshapes, vcols, NV):
    nc = bass.Bass("TRN2", target_bir_lowering=False)
    ND, NFH, NH, KQ, KK = cfg.ND, cfg.NFH, cfg.NH, cfg.KQ, cfg.KK
    D = cfg.D
    dt_ = nc.dram_tensor
    _uid = [0]

    def sbt(name, shape, dtype):
        _uid[0] += 1
        return nc.sbuf_tensor(f"{name}_{_uid[0]}", shape, dtype)

    def pst(name, shape, dtype):
        _uid[0] += 1
        return nc.psum_tensor(f"{name}_{_uid[0]}", shape, dtype)
    xT = dt_("xT", [ND, 128, T], F32, kind="ExternalInput")
    outT = dt_("outT", [ND, 128, T], F32, kind="ExternalOutput")
    vecs_d = dt_("vecs", [128, NV], F32, kind="ExternalInput")
    pos_d = dt_("pos", [64, T], I32, kind="ExternalInput")
    CSd = dt_("CSd", [64, T], F32)
    SNd = dt_("SNd", [64, T], F32)
    CQd = dt_("CQd", [cfg.KQ * 128, T], BF16)
    invf_d = dt_("invf", [64, 1], F32, kind="ExternalInput")
    flag_d = dt_("flag", [128, 2], F32, kind="ExternalInput")
    ident_d = dt_("ident", [128, 128], F32, kind="ExternalInput")
    mask_d = dt_("cmask", [128, 128], F32, kind="ExternalInput")
    wsh = {n: dt_(f"w_{n}", [s[0] // NCORES, 128, s[2]], F32, kind="ExternalInput") for n, s in gshapes.items()}
    CHMAX = 512 * 1024
    gn_ = list(gshapes.keys())
    ginfo = {}
    for n in gn_:
        nl = gshapes[n][0] // NCORES
        ce = 128 * gshapes[n][2]
        wpa = max(1, CHMAX // ce)
        ginfo[n] = (nl, ce, wpa)
    ib_t = {n: dt_(f"ib_{n}", [ginfo[n][0], ginfo[n][1]], BF16) for n in gn_}
    o4_t = {n: dt_(f"o4_{n}", [4 * ginfo[n][0], ginfo[n][1]], BF16) for n in gn_}
    o8_t = {n: dt_(f"o8_{n}", [8 * ginfo[n][0], ginfo[n][1]], BF16) for n in gn_}

    def ib_chunk(n, cl):
        nl, ce, wpa = ginfo[n]
        return ib_t[n][cl].rearrange("(p f) -> p f", p=128)

    def ag_chunks(n):
        nl, ce, wpa = ginfo[n]
        out = []
        a0 = 0
        while a0 < nl:
            w = min(wpa, nl - a0)
            out.append((a0, w))
            a0 += w
        return out

    def ob_chunk(n, c):
        nl, ce, wpa = ginfo[n]
        r, cl = c // nl, c % nl
        a = cl // wpa
        a0 = a * wpa
        w = min(wpa, nl - a0)
        z = w * ce
        q, h, i2 = r // 4, (r % 4) // 2, r % 2
        row = 8 * a0 + h * 4 * w + q * 2 * w + i2 * w + (cl - a0)
        return o8_t[n][row].rearrange("(p f) -> p f", p=128), 2 * a + h + 1
    Hd = dt_("Hd", [ND, 128, T], F32)
    Yd = dt_("Yd", [ND, 128, T], F32)
    GBd = dt_("GBd", [ND, 128, T], F32)
    XBd = dt_("XBd", [ND, 128, T], F32)
    Ad = dt_("Ad", [ND, 128, T], F32)
    Ud = dt_("Ud", [ND, 128, T], F32)
    halo_i = dt_("halo_i", [ND * 128, 4], F32)
    halo_o = dt_("halo_o", [2 * ND * 128, 4], F32)
    car_i = dt_("car_i", [ND * 128, 1], F32)
    car_o = dt_("car_o", [2 * ND * 128, 1], F32)
    KVR = KK * 128 + 64
    kv_i = dt_("kv_i", [KVR, T], BF16)
    kv_o = dt_("kv_o", [2 * KVR, T], BF16)
    PAIRS = [[0, 1], [2, 3], [4, 5], [6, 7]]
    gnames = list(gshapes.keys())

    with contextlib.ExitStack() as top:
        E = top.enter_context
        ag_sem = {n: E(nc.semaphore(f"ag_{n}")) for n in gnames}
        g_csem = {e: E(nc.semaphore(f"c_{e}")) for e in ENGS}
        g_dsem = {e: [E(nc.semaphore(f"d_{e}{i}")) for i in range(NDSEM)] for e in ENGS}
        x_sem = [E(nc.semaphore(f"xs{i}")) for i in range(4)]
        Stage.SEMS = (g_csem, g_dsem, list(ag_sem.values()) + x_sem)
        HN = E(sbt("HN", [128, ND, T], BF16))
        RSTD = E(sbt("RSTD", [128, T], F32))
        RSTDY = E(sbt("RSTDY", [128, T], F32))
        VEC = E(sbt("VEC", [128, NV], F32))
        ONES = E(sbt("ONES", [128, 128], F32))
        IDB = E(sbt("IDB", [128, 128], BF16))
        IDF = E(sbt("IDF", [128, 128], F32))
        CMASK = E(sbt("CMASK", [128, 128], F32))
        FLAG = E(sbt("FLAG", [128, 2], F32))
        CL = E(sbt("CL", [128, ND], F32))
        CL2 = E(sbt("CL2", [128, ND], F32))
        PS = [None] * 8

        PSB = {}

        def alloc_ps(es, n=8):
            PSB.clear()
            for i in range(8):
                PSB[i] = Buf()
            for i in range(n):
                PS[i] = es.enter_context(pst(f"PS{i}", [128, TH], F32))

        def vcol(name, i=0):
            c = vcols[name] + i
            return VEC[:, c:c + 1]

        def stage_prep():
            st = Stage(nc, "prep")
            with contextlib.ExitStack() as es:
                FM = max(s[2] for s in gshapes.values())
                stg = [es.enter_context(sbt(f"stg{i}", [128, FM], F32)) for i in range(2)]
                stb = [es.enter_context(sbt(f"stb{i}", [128, FM], BF16)) for i in range(2)]
                tmp = es.enter_context(sbt("ptmp", [128, 2 * ND], F32))
                ang = es.enter_context(sbt("ang", [64, T], F32))
                kf = es.enter_context(sbt("kf", [64, T], F32))
                ki = es.enter_context(sbt("ki", [64, T], I32))
                invf = es.enter_context(sbt("invf_s", [64, 1], F32))
                CS = es.enter_context(sbt("CSp", [64, T], F32))
                SN = es.enter_context(sbt("SNp", [64, T], F32))
                rr = es.enter_context(sbt("rrp", [64, T], F32))
                bstg = [Buf() for _ in range(2)]
                bstb = [Buf() for _ in range(2)]
                bmisc = Buf()
                bib = {n: Buf() for n in gnames}
                st.add("scalar", lambda e: e.dma_start(out=VEC[:], in_=vecs_d[:, :]), writes=[bmisc], dma=True)
                st.add("scalar", lambda e: e.dma_start(out=IDF[:], in_=ident_d[:, :]), writes=[bmisc], dma=True)
                st.add("scalar", lambda e: e.dma_start(out=CMASK[:], in_=mask_d[:, :]), writes=[bmisc], dma=True)
                st.add("scalar", lambda e: e.dma_start(out=FLAG[:], in_=flag_d[:, :]), writes=[bmisc], dma=True)
                st.add("scalar", lambda e: e.dma_start(out=ki[:], in_=pos_d[:, :]), writes=[bmisc], dma=True)
                st.add("scalar", lambda e: e.activation(out=ang[:], in_=ki[:], func=AF.Copy), reads=[bmisc], writes=[bmisc])
                st.add("scalar", lambda e: e.dma_start(out=invf[:], in_=invf_d[:, :]), writes=[bmisc], dma=True)
                st.add("vector", lambda e: e.memset(ONES[:], 1.0), writes=[bmisc])
                st.add("vector", lambda e: e.tensor_copy(out=IDB[:], in_=IDF[:]), reads=[bmisc], writes=[bmisc])
                lam = VEC[:, vcols["lam"]:vcols["lam"] + ND]
                st.add("scalar", lambda e: e.activation(out=tmp[:, 0:ND], in_=lam, func=AF.Exp, scale=-1.0), reads=[bmisc], writes=[bmisc])
                st.add("scalar", lambda e: e.activation(out=tmp[:, ND:2 * ND], in_=tmp[:, 0:ND], func=AF.Ln, bias=1.0), reads=[bmisc], writes=[bmisc])
                st.add("vector", lambda e: e.tensor_scalar(out=CL[:], in0=tmp[:, ND:2 * ND], scalar1=-8.0, scalar2=None, op0=ALU.mult), reads=[bmisc], writes=[bmisc])
                st.add("vector", lambda e: e.tensor_scalar(out=CL2[:], in0=tmp[:, ND:2 * ND], scalar1=-16.0, scalar2=None, op0=ALU.mult), reads=[bmisc], writes=[bmisc])
                TWO_PI = 2.0 * np.pi
                c1 = float(np.float32(6.28125))
                c2 = float(np.float32(TWO_PI - 6.28125))
                c3 = float(TWO_PI - 6.28125 - float(np.float32(TWO_PI - 6.28125)))
                st.add("vector", lambda e: e.tensor_scalar(out=ang[:], in0=ang[:], scalar1=invf[:, 0:1], scalar2=None, op0=ALU.mult), reads=[bmisc], writes=[bmisc])
                st.add("vector", lambda e: e.tensor_scalar(out=kf[:], in0=ang[:], scalar1=float(1.0 / TWO_PI), scalar2=None, op0=ALU.mult), reads=[bmisc], writes=[bmisc])
                MAGIC = 12582912.0
                st.add("vector", lambda e: e.tensor_scalar(out=kf[:], in0=kf[:], scalar1=MAGIC, scalar2=None, op0=ALU.add), reads=[bmisc], writes=[bmisc])
                st.add("vector", lambda e: e.tensor_scalar(out=kf[:], in0=kf[:], scalar1=-MAGIC, scalar2=None, op0=ALU.add), reads=[bmisc], writes=[bmisc])
                for cc_ in (c1, c2, c3):
                    st.add("vector", lambda e, cc_=cc_: e.tensor_scalar(out=rr[:], in0=kf[:], scalar1=float(-cc_), scalar2=None, op0=ALU.mult), reads=[bmisc], writes=[bmisc])
                    st.add("vector", lambda e: e.tensor_tensor(out=ang[:], in0=ang[:], in1=rr[:], op=ALU.add), reads=[bmisc], writes=[bmisc])
                PI = float(np.pi)

                def wrap(dst, src, shift):
                    st.add("vector", lambda e: e.tensor_scalar(out=dst[:], in0=src[:], scalar1=float(shift), scalar2=None, op0=ALU.add), reads=[bmisc], writes=[bmisc])
                    st.add("vector", lambda e: e.tensor_scalar(out=kf[:], in0=dst[:], scalar1=PI, scalar2=-2 * PI, op0=ALU.is_gt, op1=ALU.mult), reads=[bmisc], writes=[bmisc])
                    st.add("vector", lambda e: e.tensor_tensor(out=dst[:], in0=dst[:], in1=kf[:], op=ALU.add), reads=[bmisc], writes=[bmisc])
                    st.add("vector", lambda e: e.tensor_scalar(out=kf[:], in0=dst[:], scalar1=-PI, scalar2=2 * PI, op0=ALU.is_lt, op1=ALU.mult), reads=[bmisc], writes=[bmisc])
                    st.add("vector", lambda e: e.tensor_tensor(out=dst[:], in0=dst[:], in1=kf[:], op=ALU.add), reads=[bmisc], writes=[bmisc])
                    st.add("vector", lambda e: e.tensor_scalar(out=dst[:], in0=dst[:], scalar1=3.1415925, scalar2=-3.1415925, op0=ALU.min, op1=ALU.max), reads=[bmisc], writes=[bmisc])

                CL_ = 3.1415925
                st.add("vector", lambda e: e.tensor_scalar(out=rr[:], in0=ang[:], scalar1=CL_, scalar2=-CL_, op0=ALU.min, op1=ALU.max), reads=[bmisc], writes=[bmisc])
                st.add("scalar", lambda e: e.activation(out=SN[:], in_=rr[:], func=AF.Sin), reads=[bmisc], writes=[bmisc])
                st.add("vector", lambda e: e.tensor_scalar(out=kf[:], in0=rr[:], scalar1=-1.0, scalar2=None, op0=ALU.mult), reads=[bmisc], writes=[bmisc])
                st.add("vector", lambda e: e.tensor_tensor(out=kf[:], in0=kf[:], in1=rr[:], op=ALU.max), reads=[bmisc], writes=[bmisc])
                st.add("vector", lambda e: e.tensor_scalar(out=kf[:], in0=kf[:], scalar1=-1.0, scalar2=PI / 2, op0=ALU.mult, op1=ALU.add), reads=[bmisc], writes=[bmisc])
                st.add("scalar", lambda e: e.activation(out=CS[:], in_=kf[:], func=AF.Sin), reads=[bmisc], writes=[bmisc])
                st.add("vector", lambda e: e.tensor_scalar(out=SN[0:32, :], in0=SN[0:32, :], scalar1=-1.0, scalar2=None, op0=ALU.mult), reads=[bmisc], writes=[bmisc])
                st.add("scalar", lambda e: e.dma_start(out=CSd[:, :], in_=CS[:]), reads=[bmisc], writes=[Buf()], dma=True)
                st.add("scalar", lambda e: e.dma_start(out=SNd[:, :], in_=SN[:]), reads=[bmisc], writes=[Buf()], dma=True)
                k = 0
                for n in gnames:
                    nl = gshapes[n][0] // NCORES
                    Fc = gshapes[n][2]
                    for c in range(nl):
                        s = k % 2
                        st.add("sync", lambda e, n=n, c=c, s=s, Fc=Fc: e.dma_start(out=stg[s][:, 0:Fc], in_=wsh[n][c]), writes=[bstg[s]], dma=True)
                        ce = ("vector", "scalar", "gpsimd")[k % 3]
                        if ce == "scalar":
                            st.add(ce, lambda e, s=s, Fc=Fc: e.activation(out=stb[s][:, 0:Fc], in_=stg[s][:, 0:Fc], func=AF.Copy), reads=[bstg[s]], writes=[bstb[s]])
                        else:
                            st.add(ce, lambda e, s=s, Fc=Fc: e.tensor_copy(out=stb[s][:, 0:Fc], in_=stg[s][:, 0:Fc]), reads=[bstg[s]], writes=[bstb[s]])
                        st.add("sync", lambda e, n=n, c=c, s=s, Fc=Fc: e.dma_start(out=ib_chunk(n, c), in_=stb[s][:, 0:Fc]), reads=[bstb[s]], writes=[bib[n]], dma=True)
                        k += 1
                    QUADS = [[0, 1, 2, 3], [4, 5, 6, 7]]
                    XP4 = [[0, 4], [1, 5], [2, 6], [3, 7]]
                    nl_, ce_, wpa_ = ginfo[n]
                    first_cc = True
                    import os as _o2
                    if gnames.index(n) >= int(_o2.environ.get('KAGN', '999')):
                        continue
                    for (a0, w) in ag_chunks(n):
                        z = w * ce_

                        def cc1(e, n=n, a0=a0, w=w):
                            ins = e.collective_compute("AllGather", ALU.bypass, replica_groups=QUADS,
                                                       ins=[ib_t[n][a0:a0 + w].opt()], outs=[o4_t[n][4 * a0:4 * a0 + 4 * w].opt()])
                            ins.then_inc(x_sem[3], 1)
                            return ins
                        st.add("gpsimd", cc1, reads=[bib[n]] if first_cc else [])
                        first_cc = False
                        for h in range(2):
                            def cc2(e, n=n, a0=a0, w=w, h=h):
                                ins = e.collective_compute("AllGather", ALU.bypass, replica_groups=XP4,
                                                           ins=[o4_t[n][4 * a0 + h * 2 * w:4 * a0 + (h + 1) * 2 * w].opt()],
                                                           outs=[o8_t[n][8 * a0 + h * 4 * w:8 * a0 + (h + 1) * 4 * w].opt()])
                                ins.then_inc(ag_sem[n], 1)
                                return ins
                            st.add("gpsimd", cc2)
                import os
                if os.environ.get("KWAITAG"):
                    st.add("gpsimd", lambda e: e.memset(tmp[:, 0:1], 0.0), extra=[(ag_sem[n], 2 * len(ag_chunks(n))) for n in gnames])
                st.emit()

        class Ring:
            def __init__(self, es, name, n, width, dtype=BF16):
                self.t = [es.enter_context(sbt(f"{name}{i}", [128, width], dtype)) for i in range(n)]
                self.b = [Buf() for _ in range(n)]
                self.k = 0

            def nxt(self):
                i = self.k % len(self.t)
                self.k += 1
                return self.t[i], self.b[i]

        class Banks:
            def __init__(self, ids):
                self.ids = ids
                self.b = {i: PSB[i] for i in ids}
                self.k = 0

            def nxt(self):
                i = self.ids[self.k % len(self.ids)]
                self.k += 1
                return PS[i], self.b[i]

        def load_w(st, name, c, wt, wb, ncols, first):
            src, need = ob_chunk(name, c)
            st.add("sync", lambda e: e.dma_start(out=wt[:, 0:ncols], in_=src), writes=[wb], dma=True, extra=[(ag_sem[name], need)])

        def linear(st, name, chunks, KT, rhs, rhs_bufs, ring, banks, evac, M=128, nth=2, first_flag=[True]):
            Fc = gshapes[name][2]
            wcw = Fc // KT
            for i, c in enumerate(chunks):
                wt, wb = ring.nxt()
                load_w(st, name, c, wt, wb, Fc, i == 0)
                pss = [banks.nxt() for _ in range(nth)]
                for kt in range(KT):
                    for th in range(nth):
                        ps, pb = pss[th]
                        r_ap = rhs(kt, th)
                        st.add("tensor", lambda e, ps=ps, wt=wt, kt=kt, th=th, r_ap=r_ap: e.matmul(ps[0:M, :], wt[:, kt * wcw:kt * wcw + M], r_ap, start=(kt == 0), stop=(kt == KT - 1)),
                               reads=[wb] + rhs_bufs(kt, th), writes=[pb])
                for th in range(nth):
                    evac(i, c, th, pss[th][0], pss[th][1])

        def rstd_from(st, ps, pb, out_ap, outbuf, nfeat, tmp, tb, mul=1.0):
            st.add("scalar", lambda e: e.activation(out=tmp, in_=ps, func=AF.Sqrt, scale=1.0 / nfeat, bias=1e-6), reads=[pb], writes=[tb])
            st.add("vector", lambda e: e.reciprocal(out=out_ap, in_=tmp), reads=[tb], writes=[outbuf])
            if mul != 1.0:
                st.add("vector", lambda e: e.tensor_scalar(out=out_ap, in0=out_ap, scalar1=float(mul), scalar2=None, op0=ALU.mult), reads=[outbuf], writes=[outbuf])

        def stage_epi(name, src, ysrc, gcoef, gout, hdst, ymul_applied=True):
            st = Stage(nc, name)
            with contextlib.ExitStack() as es:
                alloc_ps(es)
                HNEW = es.enter_context(sbt("HNEW", [128, ND, TH], F32))
                ystg = Ring(es, "ystg", 3, TH, F32)
                sq = Ring(es, "sq", 2, TH, F32)
                rt = es.enter_context(sbt("rt", [128, TH], F32))
                rtb = Buf()
                bh = [Buf() for _ in range(ND)]
                bhn = Buf()
                brs = Buf()
                bd = Buf()
                banks = Banks([0, 1])
                for th in range(2):
                    tsl = slice(th * TH, (th + 1) * TH)
                    ps, pb = banks.nxt()
                    for dt in range(ND):
                        st.add("sync", lambda e, dt=dt, tsl=tsl: e.dma_start(out=HNEW[:, dt, :], in_=src[dt][:, tsl]), writes=[bh[dt]], dma=True)
                        if ysrc is not None:
                            yt, yb = ystg.nxt()
                            st.add("scalar", lambda e, dt=dt, tsl=tsl, yt=yt: e.dma_start(out=yt[:], in_=ysrc[dt][:, tsl]), writes=[yb], dma=True)
                            st.add("gpsimd", lambda e, yt=yt, tsl=tsl: e.tensor_tensor(out=yt[:], in0=yt[:], in1=RSTDY[:, tsl], op=ALU.mult), reads=[yb], writes=[yb])
                            st.add("vector", lambda e, dt=dt, yt=yt: e.scalar_tensor_tensor(out=HNEW[:, dt, :], in0=yt[:], scalar=vcol(gcoef, dt), in1=HNEW[:, dt, :], op0=ALU.mult, op1=ALU.add),
                                   reads=[yb, bh[dt]], writes=[bh[dt]])
                        if hdst is not None and (ysrc is not None or hdst is not src):
                            st.add("scalar", lambda e, dt=dt, tsl=tsl: e.dma_start(out=hdst[dt][:, tsl], in_=HNEW[:, dt, :]), reads=[bh[dt]], writes=[bd], dma=True)
                        if gout is not None:
                            qt, qb = sq.nxt()
                            st.add("scalar", lambda e, dt=dt, qt=qt: e.activation(out=qt[:], in_=HNEW[:, dt, :], func=AF.Square), reads=[bh[dt]], writes=[qb])
                            st.add("tensor", lambda e, dt=dt, qt=qt, ps=ps: e.matmul(ps[:, :], ONES[:, :], qt[:], start=(dt == 0), stop=(dt == ND - 1)), reads=[qb], writes=[pb])
                    if gout is not None:
                        rstd_from(st, ps[:, :], pb, RSTD[:, tsl], brs, float(D), rt[:], rtb)
                        for dt in range(ND):
                            st.add("vector", lambda e, dt=dt, tsl=tsl: e.scalar_tensor_tensor(out=HN[:, dt, tsl], in0=HNEW[:, dt, :], scalar=vcol(gout, dt), in1=RSTD[:, tsl], op0=ALU.mult, op1=ALU.mult),
                                   reads=[bh[dt], brs], writes=[bhn])
                st.emit()

        def stage_renorm(name, gout):
            st = Stage(nc, name)
            with contextlib.ExitStack() as es:
                hs = Ring(es, "hs", 3, T, F32)
                bhn = Buf()
                for dt in range(ND):
                    ht, hb = hs.nxt()
                    st.add("sync", lambda e, dt=dt, ht=ht: e.dma_start(out=ht[:], in_=Hd[dt]), writes=[hb], dma=True)
                    st.add("vector", lambda e, dt=dt, ht=ht: e.scalar_tensor_tensor(out=HN[:, dt, :], in0=ht[:], scalar=vcol(gout, dt), in1=RSTD[:, :], op0=ALU.mult, op1=ALU.mult),
                           reads=[hb], writes=[bhn])
                st.emit()

        class YOut:
            def __init__(self, st, es, mul):
                self.st = st
                self.ystg = Ring(es, "yo", 3, TH, F32)
                self.sq = Ring(es, "ysq", 2, TH, F32)
                self.ssb = Banks([6, 7])
                self.ss = [self.ssb.nxt() for _ in range(2)]
                self.rt = es.enter_context(sbt("yrt", [128, TH], F32))
                self.rtb = Buf()
                self.mul = mul
                self.bd = Buf()
                self.brs = Buf()

            def evac(self, i, dt, th, ps, pb, ntot, pre=None):
                st = self.st
                tsl = slice(th * TH, (th + 1) * TH)
                yt, yb = self.ystg.nxt()
                if pre is None:
                    st.add("vector", lambda e: e.tensor_copy(out=yt[:], in_=ps[:, :]), reads=[pb], writes=[yb])
                else:
                    pt, ptb = pre
                    st.add("vector", lambda e: e.tensor_tensor(out=yt[:], in0=ps[:, :], in1=pt[:], op=ALU.add), reads=[pb, ptb], writes=[yb])
                st.add("scalar", lambda e: e.dma_start(out=Yd[dt][:, tsl], in_=yt[:]), reads=[yb], writes=[self.bd], dma=True)
                qt, qb = self.sq.nxt()
                st.add("scalar", lambda e: e.activation(out=qt[:], in_=yt[:], func=AF.Square), reads=[yb], writes=[qb])
                sps, spb = self.ss[th]
                st.add("tensor", lambda e: e.matmul(sps[:, :], ONES[:, :], qt[:], start=(i == 0), stop=(i == ntot - 1)), reads=[qb], writes=[spb])
                if i == ntot - 1:
                    rstd_from(st, sps[:, :], spb, RSTDY[:, tsl], self.brs, float(D), self.rt[:], self.rtb, mul=self.mul)

        def stage_ffn(q):
            st = Stage(nc, f"ffn{q}")
            with contextlib.ExitStack() as es:
                alloc_ps(es)
                ACT = es.enter_context(sbt("ACT", [128, NFH, T], BF16))
                ring = Ring(es, "wr", 3, max(ND, NFH) * 128)
                sg = Ring(es, "sg", 3, TH, F32)
                yo = YOut(st, es, 0.5)
                yp = Ring(es, "yp", 3, TH, F32)
                bact = [Buf() for _ in range(NFH)]
                bhn = Buf()
                bdy = [[Buf() for _ in range(2)] for _ in range(ND)]
                NP = len(cfg.PARTS)
                for hf, (p0, pn) in enumerate(cfg.PARTS):
                    banks = Banks([0, 1, 2, 3, 4, 5])
                    sgs = {}

                    def evac_gu(i, c, th, ps, pb):
                        j = i // 2
                        if i % 2 == 0:
                            t_, b_ = sg.nxt()
                            sgs[(j, th)] = (t_, b_)
                            st.add("scalar", lambda e: e.activation(out=t_[:], in_=ps[:, :], func=AF.Silu), reads=[pb], writes=[b_])
                        else:
                            t_, b_ = sgs.pop((j, th))
                            st.add("vector", lambda e: e.tensor_tensor(out=ACT[:, j, th * TH:(th + 1) * TH], in0=ps[:, :], in1=t_[:], op=ALU.mult), reads=[pb, b_], writes=[bact[j]])

                    linear(st, f"gu{q}{hf}", list(range(2 * pn)), ND, lambda kt, th: HN[:, kt, th * TH:(th + 1) * TH], lambda kt, th: [bhn], ring, banks, evac_gu)

                    def evac_dn(i, c, th, ps, pb):
                        tsl = slice(th * TH, (th + 1) * TH)
                        if hf == 0:
                            yt, yb = yo.ystg.nxt()
                            st.add("vector", lambda e: e.tensor_copy(out=yt[:], in_=ps[:, :]), reads=[pb], writes=[yb])
                            st.add("scalar", lambda e: e.dma_start(out=Yd[c][:, tsl], in_=yt[:]), reads=[yb], writes=[bdy[c][th]], dma=True)
                        elif hf < NP - 1:
                            pt, ptb = yp.nxt()
                            st.add("scalar", lambda e: e.dma_start(out=pt[:], in_=Yd[c][:, tsl]), reads=[bdy[c][th]], writes=[ptb], dma=True)
                            yt, yb = yo.ystg.nxt()
                            st.add("vector", lambda e: e.tensor_tensor(out=yt[:], in0=ps[:, :], in1=pt[:], op=ALU.add), reads=[pb, ptb], writes=[yb])
                            st.add("scalar", lambda e: e.dma_start(out=Yd[c][:, tsl], in_=yt[:]), reads=[yb], writes=[bdy[c][th]], dma=True)
                        else:
                            pt, ptb = yp.nxt()
                            st.add("scalar", lambda e: e.dma_start(out=pt[:], in_=Yd[c][:, tsl]), reads=[bdy[c][th]], writes=[ptb], dma=True)
                            yo.evac(i, c, th, ps, pb, ND, pre=(pt, ptb))

                    import os as _o3
                    _dbg = int(_o3.environ.get("KFFNDBG", "9"))
                    if _dbg == 1 or (_dbg == 2 and hf > 0):
                        break
                    linear(st, f"dn{q}{hf}", list(range(ND)), pn, lambda kt, th: ACT[:, kt, th * TH:(th + 1) * TH], lambda kt, th: [bact[kt]], ring, Banks([0, 1, 2, 3]), evac_dn)
                st.emit()

        def exchange(st, src_writes, ib, ob, sem, after_bufs):
            def cc(e):
                ins = e.collective_compute("AllGather", ALU.bypass, replica_groups=PAIRS, ins=[ib.ap().opt()], outs=[ob.ap().opt()])
                ins.then_inc(sem)
                return ins
            st.add("gpsimd", cc, reads=after_bufs)

        def stage_lru_a():
            st = Stage(nc, "lruA")
            with contextlib.ExitStack() as es:
                ring = Ring(es, "wr", 3, ND * 128)
                alloc_ps(es)
                og = Ring(es, "og", 4, TH, F32)
                HAL = es.enter_context(sbt("HAL", [128, ND, 4], F32))
                bhal = Buf()
                bhn = Buf()
                bd = Buf()
                banks = Banks([0, 1, 2, 3, 4, 5])
                st.add("vector", lambda e: e.memset(HAL[:], 0.0), writes=[bhal])

                def evac(i, c, th, ps, pb):
                    tsl = slice(th * TH, (th + 1) * TH)
                    t_, b_ = og.nxt()
                    if c < ND:
                        st.add("scalar", lambda e: e.activation(out=t_[:], in_=ps[:, :], func=AF.Gelu_apprx_tanh), reads=[pb], writes=[b_])
                        st.add("scalar", lambda e: e.dma_start(out=GBd[c][:, tsl], in_=t_[:]), reads=[b_], writes=[bd], dma=True)
                    else:
                        ct = c - ND
                        st.add("vector", lambda e: e.tensor_copy(out=t_[:], in_=ps[:, :]), reads=[pb], writes=[b_])
                        st.add("scalar", lambda e: e.dma_start(out=XBd[ct][:, tsl], in_=t_[:]), reads=[b_], writes=[bd], dma=True)
                        if th == 1:
                            st.add("vector", lambda e: e.tensor_copy(out=HAL[:, ct, 0:3], in_=t_[:, TH - 3:TH]), reads=[b_], writes=[bhal])

                linear(st, "win", list(range(2 * ND)), ND, lambda kt, th: HN[:, kt, th * TH:(th + 1) * TH], lambda kt, th: [bhn], ring, banks, evac)
                bx = Buf()
                st.add("scalar", lambda e: e.dma_start(out=halo_i.ap().rearrange("(c p) k -> p c k", p=128), in_=HAL[:]), reads=[bhal], writes=[bx], dma=True)
                exchange(st, None, halo_i, halo_o, x_sem[0], [bx])
                st.emit()

        def stage_lru_b():
            st = Stage(nc, "lruB")
            NBLK = cfg.NBLK
            with contextlib.ExitStack() as es:
                ring = Ring(es, "wr", 4, 256)
                alloc_ps(es)
                XP = Ring(es, "xp", 4, T + 4, F32)
                XC = Ring(es, "xc", 4, T, F32)
                XCB = Ring(es, "xcb", 4, T, BF16)
                RR = Ring(es, "rr", 3, T, F32)
                II = Ring(es, "ii", 3, T, F32)
                AA = Ring(es, "aa", 3, T, F32)
                MM = Ring(es, "mm", 3, T, F32)
                UU = Ring(es, "uu", 3, T, F32)
                HH = Ring(es, "hh", 2, T, F32)
                HAL = es.enter_context(sbt("HALp", [128, ND, 4], F32))
                FIN = es.enter_context(sbt("FIN", [128, ND, 4], F32))
                bhal = Buf()
                bfin = Buf()
                bd = Buf()
                banks = Banks([0, 1, 2, 3, 4, 5, 6, 7])
                st.add("scalar", lambda e: e.dma_start(out=HAL[:], in_=halo_o.ap()[0:ND * 128, :].rearrange("(c p) k -> p c k", p=128)), writes=[bhal], dma=True, extra=[(x_sem[0], 1)])
                st.add("vector", lambda e: e.tensor_scalar(out=HAL[:], in0=HAL[:], scalar1=FLAG[:, 0:1], scalar2=None, op0=ALU.mult), reads=[bhal], writes=[bhal])
                st.add("vector", lambda e: e.memset(FIN[:], 0.0), writes=[bfin])
                first = [True]
                import os as _o4
                KLB = int(_o4.environ.get("KLB", "9"))
                for blk in range(NBLK if KLB > 1 else 0):
                    xcb = []
                    xcs = []
                    for s in range(2):
                        ct = blk * 2 + s
                        xp, xpb = XP.nxt()
                        st.add("sync", lambda e, xp=xp, ct=ct: e.dma_start(out=xp[:, 4:T + 4], in_=XBd[ct]), writes=[xpb], dma=True)
                        st.add("vector", lambda e, xp=xp, ct=ct: e.tensor_copy(out=xp[:, 1:4], in_=HAL[:, ct, 0:3]), reads=[bhal], writes=[xpb])
                        xc, xcbuf = XC.nxt()
                        st.add("gpsimd", lambda e, xp=xp, xc=xc, ct=ct: e.tensor_scalar(out=xc[:], in0=xp[:, 1:T + 1], scalar1=vcol("cw0", ct), scalar2=vcol("cb", ct), op0=ALU.mult, op1=ALU.add), reads=[xpb], writes=[xcbuf])
                        for k in range(1, 4):
                            st.add("vector", lambda e, xp=xp, xc=xc, ct=ct, k=k: e.scalar_tensor_tensor(out=xc[:], in0=xp[:, 1 + k:T + 1 + k], scalar=vcol(f"cw{k}", ct), in1=xc[:], op0=ALU.mult, op1=ALU.add), reads=[xpb, xcbuf], writes=[xcbuf])
                        xb_, xbb = XCB.nxt()
                        st.add("scalar", lambda e, xb_=xb_, xc=xc: e.activation(out=xb_[:], in_=xc[:], func=AF.Copy), reads=[xcbuf], writes=[xbb])
                        xcb.append((xb_, xbb))
                        xcs.append((xc, xcbuf))
                    res = {}
                    if KLB == 2:
                        continue

                    def evac(i, c, th, ps, pb, blk=blk):
                        gi = (c // 2) % 2
                        nt = c % 2
                        ct = blk * 2 + nt
                        key = (gi, nt)
                        if key not in res:
                            res[key] = (RR if gi == 0 else II).nxt()
                        t_, b_ = res[key]
                        st.add("vector", lambda e: e.tensor_scalar(out=t_[:, th * TH:(th + 1) * TH], in0=ps[:, :], scalar1=vcol("bga" if gi == 0 else "bgx", ct), scalar2=None, op0=ALU.add), reads=[pb], writes=[b_])
                        st.add("scalar", lambda e: e.activation(out=t_[:, th * TH:(th + 1) * TH], in_=t_[:, th * TH:(th + 1) * TH], func=AF.Sigmoid), reads=[b_], writes=[b_])

                    linear(st, "gates", [blk * 4 + i for i in range(4)], 2, lambda kt, th: xcb[kt][0][:, th * TH:(th + 1) * TH], lambda kt, th: [xcb[kt][1]], ring, banks, evac, first_flag=first)
                    if KLB == 3:
                        continue
                    for nt in range(2):
                        ct = blk * 2 + nt
                        r_, rb = res[(0, nt)]
                        i_, ib_ = res[(1, nt)]
                        xc, xcbuf = xcs[nt]
                        a_, ab = AA.nxt()
                        m_, mb = MM.nxt()
                        u_, ub = UU.nxt()
                        h_, hb = HH.nxt()
                        st.add("vector", lambda e, m_=m_, r_=r_, ct=ct: e.tensor_scalar(out=m_[:], in0=r_[:], scalar1=CL[:, ct:ct + 1], scalar2=None, op0=ALU.mult), reads=[rb], writes=[mb])
                        st.add("scalar", lambda e, a_=a_, m_=m_: e.activation(out=a_[:], in_=m_[:], func=AF.Exp), reads=[mb], writes=[ab])
                        st.add("scalar", lambda e, m_=m_: e.activation(out=m_[:], in_=m_[:], func=AF.Exp, scale=2.0), reads=[mb], writes=[mb])
                        st.add("scalar", lambda e, m_=m_: e.activation(out=m_[:], in_=m_[:], func=AF.Sqrt, scale=-1.0, bias=1.0), reads=[mb], writes=[mb])
                        st.add("gpsimd", lambda e, u_=u_, i_=i_, xc=xc: e.tensor_tensor(out=u_[:], in0=i_[:], in1=xc[:], op=ALU.mult), reads=[ib_, xcbuf], writes=[ub])
                        st.add("vector", lambda e, u_=u_, m_=m_: e.tensor_tensor(out=u_[:], in0=u_[:], in1=m_[:], op=ALU.mult), reads=[ub, mb], writes=[ub])
                        st.add("scalar", lambda e, a_=a_, ct=ct: e.dma_start(out=Ad[ct], in_=a_[:]), reads=[ab], writes=[bd], dma=True)
                        st.add("scalar", lambda e, u_=u_, ct=ct: e.dma_start(out=Ud[ct], in_=u_[:]), reads=[ub], writes=[bd], dma=True)
                        if KLB == 4:
                            continue
                        st.add("vector", lambda e, h_=h_, a_=a_, u_=u_: e.tensor_tensor_scan(out=h_[:, 0:TH], data0=a_[:, 0:TH], data1=u_[:, 0:TH], initial=0.0, op0=ALU.mult, op1=ALU.add), reads=[ab, ub], writes=[hb])
                        st.add("vector", lambda e, h_=h_, a_=a_, u_=u_: e.tensor_tensor_scan(out=h_[:, TH:T], data0=a_[:, TH:T], data1=u_[:, TH:T], initial=h_[:, TH - 1:TH], op0=ALU.mult, op1=ALU.add), reads=[ab, ub, hb], writes=[hb])
                        st.add("scalar", lambda e, h_=h_, ct=ct: e.dma_start(out=car_i.ap().rearrange("(c p) k -> p c k", p=128)[:, ct, 0:1], in_=h_[:, T - 1:T], allow_slow_non_contiguous=True), reads=[hb], writes=[bfin], dma=True)
                exchange(st, None, car_i, car_o, x_sem[1], [bfin])
                st.emit()

        def stage_lru_c():
            st = Stage(nc, "lruC")
            with contextlib.ExitStack() as es:
                AA = Ring(es, "aa", 3, T, F32)
                UU = Ring(es, "uu", 3, T, F32)
                GG = Ring(es, "gg", 3, T, F32)
                HH = Ring(es, "hh", 3, T, F32)
                alloc_ps(es)
                CAR = es.enter_context(sbt("CAR", [128, ND, 1], F32))
                ring = Ring(es, "wr", 3, ND * 128)
                yo = YOut(st, es, 1.0)
                bcar = Buf()
                byb = [Buf() for _ in range(ND)]
                st.add("scalar", lambda e: e.dma_start(out=CAR[:], in_=car_o.ap()[0:ND * 128, :].rearrange("(c p) k -> p c k", p=128), allow_slow_non_contiguous=True), writes=[bcar], dma=True, extra=[(x_sem[1], 1)])
                st.add("vector", lambda e: e.tensor_scalar(out=CAR[:], in0=CAR[:], scalar1=FLAG[:, 0:1], scalar2=None, op0=ALU.mult), reads=[bcar], writes=[bcar])
                for ct in range(ND):
                    a_, ab = AA.nxt()
                    u_, ub = UU.nxt()
                    g_, gb = GG.nxt()
                    h_, hb = HH.nxt()
                    st.add("sync", lambda e, a_=a_, ct=ct: e.dma_start(out=a_[:], in_=Ad[ct]), writes=[ab], dma=True)
                    st.add("sync", lambda e, u_=u_, ct=ct: e.dma_start(out=u_[:], in_=Ud[ct]), writes=[ub], dma=True)
                    st.add("sync", lambda e, g_=g_, ct=ct: e.dma_start(out=g_[:], in_=GBd[ct]), writes=[gb], dma=True)
                    st.add("vector", lambda e, h_=h_, a_=a_, u_=u_, ct=ct: e.tensor_tensor_scan(out=h_[:, 0:TH], data0=a_[:, 0:TH], data1=u_[:, 0:TH], initial=CAR[:, ct, 0:1], op0=ALU.mult, op1=ALU.add), reads=[ab, ub, bcar], writes=[hb])
                    st.add("vector", lambda e, h_=h_, a_=a_, u_=u_: e.tensor_tensor_scan(out=h_[:, TH:T], data0=a_[:, TH:T], data1=u_[:, TH:T], initial=h_[:, TH - 1:TH], op0=ALU.mult, op1=ALU.add), reads=[ab, ub, hb], writes=[hb])
                    st.add("gpsimd", lambda e, h_=h_, g_=g_, ct=ct: e.tensor_tensor(out=HN[:, ct, :], in0=h_[:], in1=g_[:], op=ALU.mult), reads=[hb, gb], writes=[byb[ct]])

                def evac(i, c, th, ps, pb):
                    yo.evac(i, c, th, ps, pb, ND)

                linear(st, "wout", list(range(ND)), ND, lambda kt, th: HN[:, kt, th * TH:(th + 1) * TH], lambda kt, th: [byb[kt]], ring, Banks([0, 1, 2, 3]), evac)
                st.emit()

        def rope_apply(st, out_ap, outbuf, x_t, xb, xs_t, xsb, np_, csl, tmp_ring, CS, SN, bcs):
            t1, b1 = tmp_ring.nxt()
            st.add("vector", lambda e: e.tensor_tensor(out=t1[0:np_, :], in0=x_t, in1=CS[0:np_, csl], op=ALU.mult), reads=[xb, bcs], writes=[b1])
            t2, b2 = tmp_ring.nxt()
            st.add("gpsimd", lambda e: e.tensor_tensor(out=t2[0:np_, :], in0=xs_t, in1=SN[0:np_, csl], op=ALU.mult), reads=[xsb, bcs], writes=[b2])
            st.add("vector", lambda e: e.tensor_tensor(out=out_ap, in0=t1[0:np_, :], in1=t2[0:np_, :], op=ALU.add), reads=[b1, b2], writes=[outbuf])

        def stage_kv():
            st = Stage(nc, "kv")
            with contextlib.ExitStack() as es:
                ring = Ring(es, "wr", 3, ND * 128)
                alloc_ps(es)
                CF = es.enter_context(sbt("CF", [128, KK + 2, T], F32))
                CB = es.enter_context(sbt("CB", [128, KK + 1, T], BF16))
                sq = Ring(es, "sq", 2, TH, F32)
                tr = Ring(es, "tr", 2, T, F32)
                CS = es.enter_context(sbt("CS", [64, T], F32))
                SN = es.enter_context(sbt("SN", [64, T], F32))
                bcs = Buf()
                st.add("scalar", lambda e: e.dma_start(out=CS[:], in_=CSd[:, :]), writes=[bcs], dma=True)
                st.add("scalar", lambda e: e.dma_start(out=SN[:], in_=SNd[:, :]), writes=[bcs], dma=True)
                rt = es.enter_context(sbt("rt", [128, TH], F32))
                RC = es.enter_context(sbt("RC", [128, T], F32))
                rtb, brc = Buf(), Buf()
                bcf = [Buf() for _ in range(KK + 2)]
                bcb = Buf()
                bhn = Buf()
                ssb = Banks([6, 7])
                ss = [ssb.nxt() for _ in range(2)]

                def evac(i, c, th, ps, pb):
                    tsl = slice(th * TH, (th + 1) * TH)
                    st.add("vector", lambda e: e.tensor_copy(out=CF[:, c, tsl], in_=ps[:, :]), reads=[pb], writes=[bcf[c]])
                    if c < KK:
                        qt, qb = sq.nxt()
                        st.add("scalar", lambda e: e.activation(out=qt[:], in_=CF[:, c, tsl], func=AF.Square), reads=[bcf[c]], writes=[qb])
                        sps, spb = ss[th]
                        st.add("tensor", lambda e: e.matmul(sps[:, :], ONES[:, :], qt[:], start=(c == 0), stop=(c == KK - 1)), reads=[qb], writes=[spb])
                        if c == KK - 1:
                            rstd_from(st, sps[:, :], spb, RC[:, tsl], brc, float(cfg.KL), rt[:], rtb)

                linear(st, "kvd", list(range(KK + 2)), ND, lambda kt, th: HN[:, kt, th * TH:(th + 1) * TH], lambda kt, th: [bhn], ring, Banks([0, 1, 2, 3]), evac)
                for c in range(KK):
                    st.add("vector", lambda e, c=c: e.scalar_tensor_tensor(out=CB[:, c, :], in0=CF[:, c, :], scalar=vcol("kln", c), in1=RC[:, :], op0=ALU.mult, op1=ALU.mult), reads=[bcf[c], brc], writes=[bcb])
                rope_apply(st, CB[0:64, KK, :], bcb, CF[0:64, KK, :], bcf[KK], CF[0:64, KK + 1, :], bcf[KK + 1], 64, slice(0, T), tr, CS, SN, bcs)
                bx = Buf()
                for c in range(KK):
                    st.add("scalar", lambda e, c=c: e.dma_start(out=kv_i[c * 128:(c + 1) * 128, :], in_=CB[:, c, :]), reads=[bcb], writes=[bx], dma=True)
                st.add("scalar", lambda e: e.dma_start(out=kv_i[KK * 128:KK * 128 + 64, :], in_=CB[0:64, KK, :]), reads=[bcb], writes=[bx], dma=True)
                exchange(st, None, kv_i, kv_o, x_sem[2], [bx])
                st.emit()

        def stage_q():
            st = Stage(nc, "qlat")
            with contextlib.ExitStack() as es:
                alloc_ps(es)
                ring = Ring(es, "wr", 3, ND * 128)
                CQF = es.enter_context(sbt("CQF", [128, KQ, T], F32))
                CQ = es.enter_context(sbt("CQ", [128, KQ, T], BF16))
                RQ = es.enter_context(sbt("RQ", [128, T], F32))
                rt = es.enter_context(sbt("rt", [128, TH], F32))
                sq = Ring(es, "sq", 2, TH, F32)
                rtb, brq = Buf(), Buf()
                bhn = Buf()
                bcq = Buf()
                bcqf = [Buf() for _ in range(KQ)]
                ssb = Banks([4, 5])
                ss = [ssb.nxt() for _ in range(2)]

                def evac_cq(i, c, th, ps, pb):
                    tsl = slice(th * TH, (th + 1) * TH)
                    st.add("vector", lambda e: e.tensor_copy(out=CQF[:, c, tsl], in_=ps[:, :]), reads=[pb], writes=[bcqf[c]])
                    qt, qb = sq.nxt()
                    st.add("scalar", lambda e: e.activation(out=qt[:], in_=CQF[:, c, tsl], func=AF.Square), reads=[bcqf[c]], writes=[qb])
                    sps, spb = ss[th]
                    st.add("tensor", lambda e: e.matmul(sps[:, :], ONES[:, :], qt[:], start=(c == 0), stop=(c == KQ - 1)), reads=[qb], writes=[spb])
                    if c == KQ - 1:
                        rstd_from(st, sps[:, :], spb, RQ[:, tsl], brq, float(cfg.QL), rt[:], rtb)

                linear(st, "wdq", list(range(KQ)), ND, lambda kt, th: HN[:, kt, th * TH:(th + 1) * TH], lambda kt, th: [bhn], ring, Banks([0, 1, 2, 3]), evac_cq)
                for c in range(KQ):
                    st.add("vector", lambda e, c=c: e.scalar_tensor_tensor(out=CQ[:, c, :], in0=CQF[:, c, :], scalar=vcol("qn", c), in1=RQ[:, :], op0=ALU.mult, op1=ALU.mult), reads=[bcqf[c], brq], writes=[bcq])
                    st.add("scalar", lambda e, c=c: e.dma_start(out=CQd[c * 128:(c + 1) * 128, :], in_=CQ[:, c, :]), reads=[bcq], writes=[Buf()], dma=True)
                st.emit()

        def stage_attn():
            st = Stage(nc, "attn")
            scale = 192.0 ** -0.5
            with contextlib.ExitStack() as es:
                alloc_ps(es, 6)
                PT = [es.enter_context(pst(f"PT{i}", [128, 1024], BF16)) for i in range(2)]
                ringq = Ring(es, "wq", 4, KQ * 128)
                ringk = Ring(es, "wk", 3, KK * 128)
                CQ = es.enter_context(sbt("CQ", [128, KQ, T], BF16))
                CKV = es.enter_context(sbt("CKV", [128, KK, 2 * T], BF16))
                KR = es.enter_context(sbt("KR", [64, 2 * T], BF16))
                CS = es.enter_context(sbt("CS", [64, T], F32))
                SN = es.enter_context(sbt("SN", [64, T], F32))
                QN = Ring(es, "qn", 2, T, BF16)
                QRF = Ring(es, "qrf", 2, T, F32)
                QR = Ring(es, "qr", 2, T, BF16)
                tr = Ring(es, "tr", 2, T, F32)
                KN = Ring(es, "kn", 2, 2 * T, BF16)
                VV = Ring(es, "vv", 2, 16 * 128, BF16)
                PP = Ring(es, "pp", 2, 2 * T, BF16)
                PTS = Ring(es, "pts", 2, 2 * T, BF16)
                OS = Ring(es, "os", 2, 128, BF16)
                SM = Ring(es, "sm", 4, 8, F32)
                ET = Ring(es, "et", 3, TH, F32)
                bpt = [Buf(), Buf()]
                bhn = Buf()
                bcq = Buf()
                bckv = Buf()
                bcs = Buf()
                bao = [Buf() for _ in range(NH)]
                st.add("scalar", lambda e: e.dma_start(out=CS[:], in_=CSd[:, :]), writes=[bcs], dma=True)
                st.add("scalar", lambda e: e.dma_start(out=SN[:], in_=SNd[:, :]), writes=[bcs], dma=True)
                for c in range(KQ):
                    st.add("scalar", lambda e, c=c: e.dma_start(out=CQ[:, c, :], in_=CQd[c * 128:(c + 1) * 128, :]), writes=[bcq], dma=True)
                KVR_ = KK * 128 + 64
                for c in range(KK):
                    st.add("scalar", lambda e, c=c: e.dma_start(out=CKV[:, c, 0:T], in_=kv_o[c * 128:(c + 1) * 128, :]), writes=[bckv], dma=True, extra=[(x_sem[2], 1)])
                    st.add("scalar", lambda e, c=c: e.dma_start(out=CKV[:, c, T:2 * T], in_=kv_i[c * 128:(c + 1) * 128, :]), writes=[bckv], dma=True)
                st.add("scalar", lambda e: e.dma_start(out=KR[:, 0:T], in_=kv_o[KK * 128:KK * 128 + 64, :]), writes=[bckv], dma=True, extra=[(x_sem[2], 1)])
                st.add("scalar", lambda e: e.dma_start(out=KR[:, T:2 * T], in_=kv_i[KK * 128:KK * 128 + 64, :]), writes=[bckv], dma=True)
                bS = Banks([0, 1, 2, 3])
                SB = [bS.nxt() for _ in range(4)]
                bO = Banks([4, 5])
                bproj = bS
                for h in range(NH):
                    qn_t, qn_b = QN.nxt()
                    qrf = {}

                    def evac_q(i, c, th, ps, pb):
                        tsl = slice(th * TH, (th + 1) * TH)
                        kind = c % 3
                        if kind == 0:
                            st.add("scalar", lambda e: e.activation(out=qn_t[:, tsl], in_=ps[:, :], func=AF.Copy), reads=[pb], writes=[qn_b])
                        else:
                            if kind not in qrf:
                                qrf[kind] = QRF.nxt()
                            t_, b_ = qrf[kind]
                            st.add("vector", lambda e: e.tensor_copy(out=t_[0:64, tsl], in_=ps[0:64, :]), reads=[pb], writes=[b_])

                    linear(st, "wuq", [3 * h, 3 * h + 1, 3 * h + 2], KQ, lambda kt, th: CQ[:, kt, th * TH:(th + 1) * TH], lambda kt, th: [bcq], ringq, bproj, evac_q)
                    qr_t, qr_b = QR.nxt()
                    rope_apply(st, qr_t[0:64, :], qr_b, qrf[1][0][0:64, :], qrf[1][1], qrf[2][0][0:64, :], qrf[2][1], 64, slice(0, T), tr, CS, SN, bcs)
                    kn_t, kn_b = KN.nxt()
                    wt, wb = ringk.nxt()
                    load_w(st, "kvu", 2 * h, wt, wb, KK * 128, h == 0)
                    for kb in range(4):
                        ps, pb = bproj.nxt()
                        for kt in range(KK):
                            st.add("tensor", lambda e, ps=ps, wt=wt, kt=kt, kb=kb: e.matmul(ps[:, :], wt[:, kt * 128:(kt + 1) * 128], CKV[:, kt, kb * TH:(kb + 1) * TH], start=(kt == 0), stop=(kt == KK - 1)), reads=[wb, bckv], writes=[pb])
                        st.add("scalar", lambda e, ps=ps, kb=kb, kn_t=kn_t: e.activation(out=kn_t[:, kb * TH:(kb + 1) * TH], in_=ps[:, :], func=AF.Copy), reads=[pb], writes=[kn_b])
                    vv_t, vv_b = VV.nxt()
                    wt2, wb2 = ringk.nxt()
                    load_w(st, "kvu", 2 * h + 1, wt2, wb2, KK * 128, False)
                    for g4 in range(4):
                        ps, pb = bproj.nxt()
                        for j in range(4):
                            kt16 = g4 * 4 + j
                            for kt in range(KK):
                                st.add("tensor", lambda e, ps=ps, wt2=wt2, kt=kt, kt16=kt16, j=j: e.matmul(ps[:, j * 128:(j + 1) * 128], CKV[:, kt, kt16 * 128:(kt16 + 1) * 128], wt2[:, kt * 128:(kt + 1) * 128], start=(kt == 0), stop=(kt == KK - 1)), reads=[wb2, bckv], writes=[pb])
                        st.add("vector", lambda e, ps=ps, g4=g4, vv_t=vv_t: e.tensor_copy(out=vv_t[:, g4 * 512:(g4 + 1) * 512], in_=ps[:, :]), reads=[pb], writes=[vv_b])
                    for qi in range(8):
                        qsl = slice(qi * 128, (qi + 1) * 128)
                        nown = (qi + 1) * 128
                        blocks = [(0, 0, 512), (1, 512, 512)]
                        o0 = 0
                        while o0 < nown:
                            w = min(512, nown - o0)
                            blocks.append((2 + o0 // 512, T + o0, w))
                            o0 += 512
                        for (bi, k0, w) in blocks:
                            ps, pb = SB[bi]
                            st.add("tensor", lambda e, ps=ps, k0=k0, w=w: e.matmul(ps[:, 0:w], qn_t[:, qsl], kn_t[:, k0:k0 + w], start=True, stop=False), reads=[qn_b, kn_b], writes=[pb])
                            st.add("tensor", lambda e, ps=ps, k0=k0, w=w: e.matmul(ps[:, 0:w], qr_t[0:64, qsl], KR[0:64, k0:k0 + w], start=False, stop=True), reads=[qr_b, bckv], writes=[pb])
                        dbi = 2 + (qi * 128) // 512
                        dps, dpb = SB[dbi]
                        dc = (qi * 128) % 512
                        st.add("vector", lambda e, dps=dps, dc=dc: e.tensor_tensor(out=dps[:, dc:dc + 128], in0=dps[:, dc:dc + 128], in1=CMASK[:, :], op=ALU.add), reads=[dpb], writes=[dpb])
                        sm, smb = SM.nxt()
                        nblk = len(blocks)
                        for ii, (bi, k0, w) in enumerate(blocks):
                            ps, pb = SB[bi]
                            st.add("vector", lambda e, ps=ps, w=w, ii=ii, sm=sm: e.reduce_max(out=sm[:, ii:ii + 1], in_=ps[:, 0:w], axis=AX.X), reads=[pb], writes=[smb])
                        st.add("vector", lambda e, sm=sm, nblk=nblk: e.reduce_max(out=sm[:, 4:5], in_=sm[:, 0:nblk], axis=AX.X), reads=[smb], writes=[smb])
                        st.add("vector", lambda e, sm=sm: e.tensor_scalar(out=sm[:, 5:6], in0=sm[:, 4:5], scalar1=-scale, scalar2=None, op0=ALU.mult), reads=[smb], writes=[smb])
                        st.add("vector", lambda e, sm=sm: e.tensor_tensor(out=sm[:, 6:7], in0=sm[:, 5:6], in1=FLAG[:, 1:2], op=ALU.add), reads=[smb], writes=[smb])
                        pp, ppb = PP.nxt()
                        sm2, sm2b = SM.nxt()
                        for ii, (bi, k0, w) in enumerate(blocks):
                            ps, pb = SB[bi]
                            bcol = 6 if bi < 2 else 5
                            et, etb = ET.nxt()
                            st.add("vector", lambda e, ps=ps, w=w, sm=sm, bcol=bcol, et=et: e.tensor_scalar(out=et[:, 0:w], in0=ps[:, 0:w], scalar1=float(scale), scalar2=sm[:, bcol:bcol + 1], op0=ALU.mult, op1=ALU.add), reads=[pb, smb], writes=[etb])
                            st.add("scalar", lambda e, w=w, k0=k0, ii=ii, pp=pp, sm2=sm2, et=et: e.activation(out=pp[:, k0:k0 + w], in_=et[:, 0:w], func=AF.Exp, accum_out=sm2[:, ii:ii + 1]),
                                   reads=[etb], writes=[ppb, sm2b])
                        st.add("vector", lambda e, sm2=sm2, nblk=nblk: e.reduce_sum(out=sm2[:, 4:5], in_=sm2[:, 0:nblk], axis=AX.X), reads=[sm2b], writes=[sm2b])
                        st.add("vector", lambda e, sm2=sm2: e.reciprocal(out=sm2[:, 5:6], in_=sm2[:, 4:5]), reads=[sm2b], writes=[sm2b])
                        ktiles = list(range(8)) + [8 + i for i in range(qi + 1)]
                        pts, ptsb = PTS.nxt()
                        for g8 in range(0, len(ktiles), 8):
                            grp = ktiles[g8:g8 + 8]
                            pt = PT[(g8 // 8) % 2]
                            ptb = bpt[(g8 // 8) % 2]
                            for j, ktl in enumerate(grp):
                                st.add("tensor", lambda e, pt=pt, j=j, ktl=ktl, pp=pp: e.transpose(pt[:, j * 128:(j + 1) * 128], pp[:, ktl * 128:(ktl + 1) * 128], IDB[:, :]), reads=[ppb], writes=[ptb])
                            n_ = len(grp) * 128
                            k00 = grp[0] * 128
                            st.add("vector" if (g8 // 8) % 2 == 0 else "scalar",
                                   (lambda e, pt=pt, n_=n_, k00=k00, pts=pts: e.tensor_copy(out=pts[:, k00:k00 + n_], in_=pt[:, 0:n_])) if (g8 // 8) % 2 == 0 else
                                   (lambda e, pt=pt, n_=n_, k00=k00, pts=pts: e.activation(out=pts[:, k00:k00 + n_], in_=pt[:, 0:n_], func=AF.Copy)),
                                   reads=[ptb], writes=[ptsb])
                        ops_, opb = bO.nxt()
                        for ii, ktl in enumerate(ktiles):
                            st.add("tensor", lambda e, ops_=ops_, ktl=ktl, ii=ii, pts=pts, vv_t=vv_t, nk=len(ktiles): e.matmul(ops_[:, 0:128], pts[:, ktl * 128:(ktl + 1) * 128], vv_t[:, ktl * 128:(ktl + 1) * 128], start=(ii == 0), stop=(ii == nk - 1)),
                                   reads=[ptsb, vv_b], writes=[opb])
                        os_, osb = OS.nxt()
                        st.add("vector", lambda e, ops_=ops_, os_=os_, sm2=sm2: e.tensor_scalar(out=os_[:, :], in0=ops_[:, 0:128], scalar1=sm2[:, 5:6], scalar2=None, op0=ALU.mult), reads=[opb, sm2b], writes=[osb])
                        pt = PT[0]
                        st.add("tensor", lambda e, pt=pt, os_=os_: e.transpose(pt[:, 0:128], os_[:, :], IDB[:, :]), reads=[osb], writes=[bpt[0]])
                        st.add("vector", lambda e, pt=pt, h=h, qsl=qsl: e.tensor_copy(out=HN[:, h, qsl], in_=pt[:, 0:128]), reads=[bpt[0]], writes=[bao[h], bhn])
                st.emit()
            return

        def stage_wo():
            st = Stage(nc, "wo")
            with contextlib.ExitStack() as es:
                alloc_ps(es)
                ring = Ring(es, "wr", 3, ND * 128)
                yo = YOut(st, es, 1.0)
                bao = Buf()

                def evac(i, c, th, ps, pb):
                    yo.evac(i, c, th, ps, pb, ND)

                linear(st, "wo", list(range(ND)), NH, lambda kt, th: HN[:, kt, th * TH:(th + 1) * TH], lambda kt, th: [bao], ring, Banks([0, 1, 2, 3]), evac)
                st.emit()

        xs = [xT[i] for i in range(ND)]
        hs = [Hd[i] for i in range(ND)]
        ys = [Yd[i] for i in range(ND)]
        os_ = [outT[i] for i in range(ND)]
        import os
        nst = int(os.environ.get("KSTAGES", "99"))
        plan = [
            lambda: stage_prep(),
            lambda: stage_epi("n0", xs, None, None, "g00", None),
            lambda: stage_ffn(0),
            lambda: stage_epi("e0", xs, ys, "g01", "g02", hs),
            lambda: stage_lru_a(),
            lambda: stage_lru_b(),
            lambda: stage_lru_c(),
            lambda: stage_epi("e1", hs, ys, "g03", "g04", hs),
            lambda: stage_ffn(1),
            lambda: stage_epi("e2", hs, ys, "g05", "g10", hs),
            lambda: stage_ffn(2),
            lambda: stage_epi("e3", hs, ys, "g11", "kvn", hs),
            lambda: stage_kv(),
            lambda: stage_renorm("rn", "g12"),
            lambda: stage_q(),
            lambda: stage_attn(),
            lambda: stage_wo(),
            lambda: stage_epi("e4", hs, ys, "g13", "g14", hs),
            lambda: stage_ffn(3),
            lambda: stage_epi("e5", hs, ys, "g15", None, os_),
        ]
        for f in plan[:nst]:
            f()
        if os.environ.get("KDUMP"):
            kd = os.environ["KDUMP"]
            st = Stage(nc, "dump")
            if kd == "CS":
                with contextlib.ExitStack() as es:
                    rg = Ring(es, "dmp", 2, T, F32)
                    for dt, srcd in enumerate([CSd, SNd]):
                        t_, b_ = rg.nxt()
                        st.add("vector", lambda e, t_=t_: e.memset(t_[:], 0.0), writes=[b_])
                        st.add("sync", lambda e, t_=t_, srcd=srcd: e.dma_start(out=t_[0:64, :], in_=srcd[:, :]), reads=[b_], writes=[b_], dma=True)
                        st.add("sync", lambda e, dt=dt, t_=t_: e.dma_start(out=outT[dt], in_=t_[:]), reads=[b_], writes=[Buf()], dma=True)
                    st.emit()
            if kd in ("KV", "CQ"):
                with contextlib.ExitStack() as es:
                    rb_ = Ring(es, "dmpb", 2, T, BF16)
                    rg = Ring(es, "dmp", 2, T, F32)
                    srcb = kv_i if kd == "KV" else CQd
                    nrow = srcb.shape[0]
                    for dt in range(min(ND, (nrow + 127) // 128)):
                        r1 = min(nrow, (dt + 1) * 128)
                        np_ = r1 - dt * 128
                        tb, bb = rb_.nxt()
                        t_, b_ = rg.nxt()
                        st.add("vector", lambda e, t_=t_: e.memset(t_[:], 0.0), writes=[b_])
                        st.add("sync", lambda e, dt=dt, tb=tb, np_=np_, r1=r1: e.dma_start(out=tb[0:np_, :], in_=srcb[dt * 128:r1, :]), writes=[bb], dma=True)
                        st.add("vector", lambda e, t_=t_, tb=tb, np_=np_: e.tensor_copy(out=t_[0:np_, :], in_=tb[0:np_, :]), reads=[bb, b_], writes=[b_])
                        st.add("sync", lambda e, dt=dt, t_=t_: e.dma_start(out=outT[dt], in_=t_[:]), reads=[b_], writes=[Buf()], dma=True)
                    st.emit()
            src_t = {"H": Hd, "Y": Yd}.get(kd)
            with contextlib.ExitStack() as es:
                if src_t is None:
                    es.close()
                rg = Ring(es, "dmp", 2, T, F32)
                for dt in range(ND if src_t is not None else 0):
                    t_, b_ = rg.nxt()
                    st.add("sync", lambda e, dt=dt, t_=t_: e.dma_start(out=t_[:], in_=src_t[dt]), writes=[b_], dma=True)
                    st.add("sync", lambda e, dt=dt, t_=t_: e.dma_start(out=outT[dt], in_=t_[:]), reads=[b_], writes=[Buf()], dma=True)
                if src_t is not None:
                    st.emit()
    return nc


_CACHE = {}


def run(cfg, inp):
    groups = weight_groups(cfg, inp)
    vecs, vcols = small_vectors(cfg, inp)
    gshapes = {n: a.shape for n, a in groups.items()}
    key = (cfg.D, cfg.F, cfg.NH)
    if key not in _CACHE:
        _CACHE[key] = build(cfg, gshapes, vcols, vecs.shape[1])
    nc = _CACHE[key]
    x = np.asarray(inp["x"], np.float32)
    pos = np.asarray(inp["positions"])
    B = x.shape[0]
    invf = (10000.0 ** (-np.arange(0, 64, 2, dtype=np.float32) / 64.0)).astype(np.float32)
    invf = np.concatenate([invf, invf]).reshape(64, 1)
    ident = np.eye(128, dtype=np.float32)
    cmask = np.where(np.arange(128)[None, :] <= np.arange(128)[:, None], 0.0, NEG).astype(np.float32)
    in_maps = []
    for c in range(NCORES):
        b, half = c // 2, c % 2
        xs = x[b, half * T:(half + 1) * T, :]
        xT = np.ascontiguousarray(xs.T).reshape(cfg.ND, 128, T)
        p = np.asarray(pos[b, half * T:(half + 1) * T], np.int32)
        m = {"xT": xT, "vecs": vecs, "pos": np.ascontiguousarray(np.broadcast_to(p[None, :], (64, T))),
             "invf": invf, "ident": ident, "cmask": cmask,
             "flag": np.ascontiguousarray(np.broadcast_to(np.array([[float(half), 0.0 if half else NEGB]], np.float32), (128, 2)))}
        for n, a in groups.items():
            nl = a.shape[0] // NCORES
            m[f"w_{n}"] = a[c * nl:(c + 1) * nl]
        in_maps.append(m)
    res = run_bass_kernel_spmd(nc, in_maps, core_ids=list(range(NCORES)))
    out = np.zeros((B, 2 * T, cfg.D), np.float32)
    for c in range(NCORES):
        b, half = c // 2, c % 2
        o = res.results[c]["outT"].reshape(cfg.D, T)
        out[b, half * T:(half + 1) * T, :] = o.T
    return out


def kernel(**inputs):
    return run(Cfg(), inputs)
```

```python
import contextlib
import numpy as np
import concourse.bass as bass
import concourse.mybir as mybir
from concourse.bass_utils import run_bass_kernel_spmd

F32 = mybir.dt.float32
BF16 = mybir.dt.bfloat16
I32 = mybir.dt.int32
AF = mybir.ActivationFunctionType
ALU = mybir.AluOpType
AX = mybir.AxisListType
NCORES = 8
T = 1024
TH = 512
NEG = -1000.0
NEGB = -80.0


class Cfg:
    def __init__(self, D=4096, F=11008, NH=32, QL=1024, KL=512, NBLK=16):
        self.D, self.F, self.NH, self.QL, self.KL, self.NBLK = D, F, NH, QL, KL, NBLK
        self.ND = D // 128
        self.NF = F // 128
        base = self.NF // 3
        rem = self.NF - 3 * base
        self.PARTS = []
        o = 0
        for i in range(3):
            n = base + (1 if i < rem else 0)
            self.PARTS.append((o, n))
            o += n
        self.NFH = max(n for _, n in self.PARTS)
        self.KQ = QL // 128
        self.KK = KL // 128


class Buf:
    __slots__ = ("name", "last_w", "readers")

    def __init__(self, name=""):
        self.name = name
        self.last_w = None
        self.readers = []


class Op:
    __slots__ = ("eng", "fn", "idx", "dma", "deps", "signaled", "count", "dsem", "dval", "extra")

    def __init__(self, eng, fn, idx, dma):
        self.eng, self.fn, self.idx, self.dma = eng, fn, idx, dma
        self.deps = []
        self.signaled = False
        self.count = 0
        self.dsem = None
        self.dval = 0
        self.extra = []


ENGS = ("tensor", "vector", "scalar", "gpsimd", "sync")
NDSEM = 6


def _freeze(fn):
    import types
    if getattr(fn, "__closure__", None) is None:
        return fn
    cells = []
    for c in fn.__closure__:
        try:
            cells.append(types.CellType(c.cell_contents))
        except ValueError:
            cells.append(c)
    g = types.FunctionType(fn.__code__, fn.__globals__, fn.__name__, fn.__defaults__, tuple(cells))
    g.__kwdefaults__ = fn.__kwdefaults__
    return g


class Stage:
    def __init__(self, nc, name):
        self.nc = nc
        self.name = name
        self.ops = {e: [] for e in ENGS}
        self.ndma = {e: 0 for e in ENGS}

    def add(self, eng, fn, reads=(), writes=(), dma=False, extra=()):
        op = Op(eng, _freeze(fn), len(self.ops[eng]), dma)
        op.extra = list(extra)
        deps = {}

        def dep(o):
            if o is None or o is op:
                return
            if o.dma:
                deps[("d", id(o))] = o
            else:
                if o.eng == "tensor" and eng == "tensor" and not dma:
                    return
                k = ("c", o.eng)
                if k not in deps or deps[k].idx < o.idx:
                    deps[k] = o

        for b in reads:
            dep(b.last_w)
        for b in writes:
            dep(b.last_w)
            for r in b.readers:
                dep(r)
        op.deps = list(deps.values())
        for o in op.deps:
            o.signaled = True
        for b in reads:
            if not dma:
                b.readers = [r for r in b.readers if r.dma or r.eng != eng]
            b.readers.append(op)
        for b in writes:
            b.last_w = op
            b.readers = []
        if dma:
            n = self.ndma[eng]
            self.ndma[eng] = n + 1
            op.dsem = n % NDSEM
            op.dval = 16 * (n // NDSEM + 1)
        self.ops[eng].append(op)
        return op

    SEMS = None

    def emit(self):
        nc = self.nc
        with contextlib.ExitStack() as es:
            csem, dsem_all, extra_clear = Stage.SEMS
            dsem = {e: dsem_all[e] for e in ENGS if self.ndma[e] > 0}
            allsems = list(csem.values()) + [x for e in ENGS for x in dsem_all[e]] + list(extra_clear)
            extra_clear.clear()
            with nc.Block(f"{self.name}_clr") as cb:
                def clr(eng):
                    for sm_ in allsems:
                        eng.sem_clear(sm_)
                cb.vector(clr)
            for e in ENGS:
                c = 0
                for op in self.ops[e]:
                    if op.signaled and not op.dma:
                        c += 1
                        op.count = c
            block = es.enter_context(nc.Block(f"{self.name}"))

            def body(ename):
                def f(eng):
                    seen = {}

                    def wait(sem, val):
                        k = id(sem)
                        if seen.get(k, 0) >= val:
                            return
                        seen[k] = val
                        eng.wait_ge(sem, val)

                    for op in self.ops[ename]:
                        for (s, v) in op.extra:
                            wait(s, v)
                        for d in op.deps:
                            if d.dma:
                                wait(dsem[d.eng][d.dsem], d.dval)
                            else:
                                wait(csem[d.eng], d.count)
                        if op.dma:
                            if op.dval > 16:
                                wait(dsem[ename][op.dsem], op.dval - 16)
                            op.fn(eng).then_inc(dsem[ename][op.dsem], 16)
                        else:
                            ins = op.fn(eng)
                            if op.signaled:
                                ins.then_inc(csem[ename], 1)
                    if ename in dsem:
                        n = self.ndma[ename]
                        for i in range(NDSEM):
                            cnt = len(range(i, n, NDSEM))
                            if cnt:
                                wait(dsem[ename][i], 16 * cnt)
                return f

            for e in ENGS:
                if self.ops[e]:
                    getattr(block, e)(body(e))


def pack_cols(W, col_lists, wc):
    K, N = W.shape
    KT = K // 128
    nch = len(col_lists)
    idx = np.zeros((nch, wc), np.int64)
    valid = np.zeros((nch, wc), bool)
    for c, cl in enumerate(col_lists):
        idx[c, :len(cl)] = cl
        valid[c, :len(cl)] = True
    Wr = W.reshape(KT, 128, N)
    out = Wr[:, :, idx.reshape(-1)].reshape(KT, 128, nch, wc)
    if not valid.all():
        out = np.where(valid[None, None], out, np.float32(0))
    out = np.ascontiguousarray(out.transpose(2, 1, 0, 3)).reshape(nch, 128, KT * wc)
    npad = (-nch) % NCORES
    if npad:
        out = np.concatenate([out, np.zeros((npad, 128, KT * wc), np.float32)], 0)
    return out


def weight_groups(cfg, inp):
    D, F, NH = cfg.D, cfg.F, cfg.NH
    g = {}
    ar = np.arange

    def ffn(q):
        l, s = q // 2, q % 2
        Wgu = np.asarray(inp["ffn_w_gate_up"][l, s])
        Wdn = np.asarray(inp["ffn_w_down"][l, s])
        for hf, (p0, pn) in enumerate(cfg.PARTS):
            cl = []
            for j in range(pn):
                f0 = (p0 + j) * 128
                cl.append(ar(f0, f0 + 128))
                cl.append(ar(F + f0, F + f0 + 128))
            g[f"gu{q}{hf}"] = pack_cols(Wgu, cl, 128)
            r0 = p0 * 128
            g[f"dn{q}{hf}"] = pack_cols(Wdn[r0:r0 + pn * 128], [ar(i * 128, i * 128 + 128) for i in range(cfg.ND)], 128)

    ffn(0)
    g["win"] = pack_cols(np.asarray(inp["lru_w_in"][0]), [ar(i * 128, i * 128 + 128) for i in range(2 * cfg.ND)], 128)
    wa, wx = np.asarray(inp["lru_w_gate_a"][0]), np.asarray(inp["lru_w_gate_x"][0])
    ch = []
    for b in range(cfg.NBLK):
        for w in (wa, wx):
            for nt in range(2):
                ch.append(pack_cols(w[b], [ar(nt * 128, nt * 128 + 128)], 128)[0])
    gt = np.stack(ch)
    npad = (-len(ch)) % NCORES
    if npad:
        gt = np.concatenate([gt, np.zeros((npad,) + gt.shape[1:], np.float32)])
    g["gates"] = gt
    g["wout"] = pack_cols(np.asarray(inp["lru_w_out"][0]), [ar(i * 128, i * 128 + 128) for i in range(cfg.ND)], 128)
    ffn(1)
    ffn(2)
    KL = cfg.KL
    cl = [ar(i * 128, i * 128 + 128) for i in range(cfg.KK)]
    cl.append(ar(KL, KL + 64))
    cl.append(np.concatenate([ar(KL + 32, KL + 64), ar(KL, KL + 32)]))
    g["kvd"] = pack_cols(np.asarray(inp["kv_w_down"]), cl, 128)
    g["wdq"] = pack_cols(np.asarray(inp["mla_w_dq"][0]), [ar(i * 128, i * 128 + 128) for i in range(cfg.KQ)], 128)
    cl = []
    for h in range(NH):
        b0 = h * 192
        cl.append(ar(b0, b0 + 128))
        cl.append(ar(b0 + 128, b0 + 192))
        cl.append(np.concatenate([ar(b0 + 160, b0 + 192), ar(b0 + 128, b0 + 160)]))
    g["wuq"] = pack_cols(np.asarray(inp["mla_w_uq"][0]), cl, 128)
    cl = []
    for h in range(NH):
        cl.append(ar(h * 256, h * 256 + 128))
        cl.append(ar(h * 256 + 128, h * 256 + 256))
    g["kvu"] = pack_cols(np.asarray(inp["kv_w_up"]), cl, 128)
    g["wo"] = pack_cols(np.asarray(inp["mla_w_o"][0]), [ar(i * 128, i * 128 + 128) for i in range(cfg.ND)], 128)
    ffn(3)
    return g


def small_vectors(cfg, inp):
    cols = {}
    tabs = []

    def put(name, v):
        v = np.asarray(v, np.float32).reshape(-1)
        n = (len(v) + 127) // 128
        vv = np.zeros(n * 128, np.float32)
        vv[:len(v)] = v
        cols[name] = sum(t.shape[1] for t in tabs)
        tabs.append(vv.reshape(n, 128).T)

    ng = np.asarray(inp["norm_gains"])
    for l in range(2):
        for i in range(6):
            put(f"g{l}{i}", ng[l, i])
    for k in range(4):
        put(f"cw{k}", np.asarray(inp["lru_conv_w"])[0, k])
    put("cb", np.asarray(inp["lru_conv_b"])[0])
    put("bga", np.asarray(inp["lru_b_gate_a"])[0])
    put("bgx", np.asarray(inp["lru_b_gate_x"])[0])
    put("lam", np.asarray(inp["lru_lambda"])[0])
    put("kvn", inp["kv_norm_in"])
    put("kln", inp["kv_latent_norm"])
    put("qn", np.asarray(inp["mla_q_norm"])[0])
    return np.ascontiguousarray(np.concatenate(tabs, 1)), cols


def build(cfg, gshapes, vcols, NV):
    nc = bass.Bass("TRN2", target_bir_lowering=False)
    ND, NFH, NH, KQ, KK = cfg.ND, cfg.NFH, cfg.NH, cfg.KQ, cfg.KK
    D = cfg.D
    dt_ = nc.dram_tensor
    _uid = [0]

    def sbt(name, shape, dtype):
        _uid[0] += 1
        return nc.sbuf_tensor(f"{name}_{_uid[0]}", shape, dtype)

    def pst(name, shape, dtype):
        _uid[0] += 1
        return nc.psum_tensor(f"{name}_{_uid[0]}", shape, dtype)
    xT = dt_("xT", [ND, 128, T], F32, kind="ExternalInput")
    outT = dt_("outT", [ND, 128, T], F32, kind="ExternalOutput")
    vecs_d = dt_("vecs", [128, NV], F32, kind="ExternalInput")
    pos_d = dt_("pos", [64, T], I32, kind="ExternalInput")
    CSd = dt_("CSd", [64, T], F32)
    SNd = dt_("SNd", [64, T], F32)
    CQd = dt_("CQd", [cfg.KQ * 128, T], BF16)
    invf_d = dt_("invf", [64, 1], F32, kind="ExternalInput")
    flag_d = dt_("flag", [128, 2], F32, kind="ExternalInput")
    ident_d = dt_("ident", [128, 128], F32, kind="ExternalInput")
    mask_d = dt_("cmask", [128, 128], F32, kind="ExternalInput")
    wsh = {n: dt_(f"w_{n}", [s[0] // NCORES, 128, s[2]], F32, kind="ExternalInput") for n, s in gshapes.items()}
    CHMAX = 512 * 1024
    gn_ = list(gshapes.keys())
    ginfo = {}
    for n in gn_:
        nl = gshapes[n][0] // NCORES
        ce = 128 * gshapes[n][2]
        wpa = max(1, CHMAX // ce)
        ginfo[n] = (nl, ce, wpa)
    ib_t = {n: dt_(f"ib_{n}", [ginfo[n][0], ginfo[n][1]], BF16) for n in gn_}
    o4_t = {n: dt_(f"o4_{n}", [4 * ginfo[n][0], ginfo[n][1]], BF16) for n in gn_}
    o8_t = {n: dt_(f"o8_{n}", [8 * ginfo[n][0], ginfo[n][1]], BF16) for n in gn_}

    def ib_chunk(n, cl):
        nl, ce, wpa = ginfo[n]
        return ib_t[n][cl].rearrange("(p f) -> p f", p=128)

    def ag_chunks(n):
        nl, ce, wpa = ginfo[n]
        out = []
        a0 = 0
        while a0 < nl:
            w = min(wpa, nl - a0)
            out.append((a0, w))
            a0 += w
        return out

    def ob_chunk(n, c):
        nl, ce, wpa = ginfo[n]
        r, cl = c // nl, c % nl
        a = cl // wpa
        a0 = a * wpa
        w = min(wpa, nl - a0)
        z = w * ce
        q, h, i2 = r // 4, (r % 4) // 2, r % 2
        row = 8 * a0 + h * 4 * w + q * 2 * w + i2 * w + (cl - a0)
        return o8_t[n][row].rearrange("(p f) -> p f", p=128), 2 * a + h + 1
    Hd = dt_("Hd", [ND, 128, T], F32)
    Yd = dt_("Yd", [ND, 128, T], F32)
    GBd = dt_("GBd", [ND, 128, T], F32)
    XBd = dt_("XBd", [ND, 128, T], F32)
    Ad = dt_("Ad", [ND, 128, T], F32)
    Ud = dt_("Ud", [ND, 128, T], F32)
    halo_i = dt_("halo_i", [ND * 128, 4], F32)
    halo_o = dt_("halo_o", [2 * ND * 128, 4], F32)
    car_i = dt_("car_i", [ND * 128, 1], F32)
    car_o = dt_("car_o", [2 * ND * 128, 1], F32)
    KVR = KK * 128 + 64
    kv_i = dt_("kv_i", [KVR, T], BF16)
    kv_o = dt_("kv_o", [2 * KVR, T], BF16)
    PAIRS = [[0, 1], [2, 3], [4, 5], [6, 7]]
    gnames = list(gshapes.keys())

    with contextlib.ExitStack() as top:
        E = top.enter_context
        ag_sem = {n: E(nc.semaphore(f"ag_{n}")) for n in gnames}
        g_csem = {e: E(nc.semaphore(f"c_{e}")) for e in ENGS}
        g_dsem = {e: [E(nc.semaphore(f"d_{e}{i}")) for i in range(NDSEM)] for e in ENGS}
        x_sem = [E(nc.semaphore(f"xs{i}")) for i in range(4)]
        Stage.SEMS = (g_csem, g_dsem, list(ag_sem.values()) + x_sem)
        HN = E(sbt("HN", [128, ND, T], BF16))
        RSTD = E(sbt("RSTD", [128, T], F32))
        RSTDY = E(sbt("RSTDY", [128, T], F32))
        VEC = E(sbt("VEC", [128, NV], F32))
        ONES = E(sbt("ONES", [128, 128], F32))
        IDB = E(sbt("IDB", [128, 128], BF16))
        IDF = E(sbt("IDF", [128, 128], F32))
        CMASK = E(sbt("CMASK", [128, 128], F32))
        FLAG = E(sbt("FLAG", [128, 2], F32))
        CL = E(sbt("CL", [128, ND], F32))
        CL2 = E(sbt("CL2", [128, ND], F32))
        PS = [None] * 8

        PSB = {}

        def alloc_ps(es, n=8):
            PSB.clear()
            for i in range(8):
                PSB[i] = Buf()
            for i in range(n):
                PS[i] = es.enter_context(pst(f"PS{i}", [128, TH], F32))

        def vcol(name, i=0):
            c = vcols[name] + i
            return VEC[:, c:c + 1]

        QUADS = [[0, 1, 2, 3], [4, 5, 6, 7]]
        XP4 = [[0, 4], [1, 5], [2, 6], [3, 7]]
        GATHER_IN_PREP = [n for n in gnames if n.startswith("gu0") or n.startswith("dn0")]
        rest = [n for n in gnames if n not in GATHER_IN_PREP]
        GATHER_AT_FFN = {
            0: [n for n in rest if n in ("win", "gates", "wout") or n.startswith("gu1") or n.startswith("dn1")],
            1: [n for n in rest if n.startswith("gu2") or n.startswith("dn2")],
            2: [n for n in rest if n in ("kvd", "wdq", "wuq", "kvu", "wo") or n.startswith("gu3") or n.startswith("dn3")],
            3: [],
        }
        assert sorted(GATHER_IN_PREP + sum(GATHER_AT_FFN.values(), [])) == sorted(gnames)

        def issue_gather(st, n, deps):
            first_cc = True
            for (a0, w) in ag_chunks(n):
                def cc1(e, n=n, a0=a0, w=w):
                    ins = e.collective_compute("AllGather", ALU.bypass, replica_groups=QUADS,
                                               ins=[ib_t[n][a0:a0 + w].opt()], outs=[o4_t[n][4 * a0:4 * a0 + 4 * w].opt()])
                    ins.then_inc(x_sem[3], 1)
                    return ins
                st.add("gpsimd", cc1, reads=deps if first_cc else [])
                first_cc = False
                for h in range(2):
                    def cc2(e, n=n, a0=a0, w=w, h=h):
                        ins = e.collective_compute("AllGather", ALU.bypass, replica_groups=XP4,
                                                   ins=[o4_t[n][4 * a0 + h * 2 * w:4 * a0 + (h + 1) * 2 * w].opt()],
                                                   outs=[o8_t[n][8 * a0 + h * 4 * w:8 * a0 + (h + 1) * 4 * w].opt()])
                        ins.then_inc(ag_sem[n], 1)
                        return ins
                    st.add("gpsimd", cc2)

        def stage_prep():
            st = Stage(nc, "prep")
            with contextlib.ExitStack() as es:
                FM = max(s[2] for s in gshapes.values())
                stg = [es.enter_context(sbt(f"stg{i}", [128, FM], F32)) for i in range(2)]
                stb = [es.enter_context(sbt(f"stb{i}", [128, FM], BF16)) for i in range(2)]
                tmp = es.enter_context(sbt("ptmp", [128, 2 * ND], F32))
                ang = es.enter_context(sbt("ang", [64, T], F32))
                kf = es.enter_context(sbt("kf", [64, T], F32))
                ki = es.enter_context(sbt("ki", [64, T], I32))
                invf = es.enter_context(sbt("invf_s", [64, 1], F32))
                CS = es.enter_context(sbt("CSp", [64, T], F32))
                SN = es.enter_context(sbt("SNp", [64, T], F32))
                rr = es.enter_context(sbt("rrp", [64, T], F32))
                bstg = [Buf() for _ in range(2)]
                bstb = [Buf() for _ in range(2)]
                bmisc = Buf()
                bib = {n: Buf() for n in gnames}
                st.add("scalar", lambda e: e.dma_start(out=VEC[:], in_=vecs_d[:, :]), writes=[bmisc], dma=True)
                st.add("scalar", lambda e: e.dma_start(out=IDF[:], in_=ident_d[:, :]), writes=[bmisc], dma=True)
                st.add("scalar", lambda e: e.dma_start(out=CMASK[:], in_=mask_d[:, :]), writes=[bmisc], dma=True)
                st.add("scalar", lambda e: e.dma_start(out=FLAG[:], in_=flag_d[:, :]), writes=[bmisc], dma=True)
                st.add("scalar", lambda e: e.dma_start(out=ki[:], in_=pos_d[:, :]), writes=[bmisc], dma=True)
                st.add("scalar", lambda e: e.activation(out=ang[:], in_=ki[:], func=AF.Copy), reads=[bmisc], writes=[bmisc])
                st.add("scalar", lambda e: e.dma_start(out=invf[:], in_=invf_d[:, :]), writes=[bmisc], dma=True)
                st.add("vector", lambda e: e.memset(ONES[:], 1.0), writes=[bmisc])
                st.add("vector", lambda e: e.tensor_copy(out=IDB[:], in_=IDF[:]), reads=[bmisc], writes=[bmisc])
                lam = VEC[:, vcols["lam"]:vcols["lam"] + ND]
                st.add("scalar", lambda e: e.activation(out=tmp[:, 0:ND], in_=lam, func=AF.Exp, scale=-1.0), reads=[bmisc], writes=[bmisc])
                st.add("scalar", lambda e: e.activation(out=tmp[:, ND:2 * ND], in_=tmp[:, 0:ND], func=AF.Ln, bias=1.0), reads=[bmisc], writes=[bmisc])
                st.add("vector", lambda e: e.tensor_scalar(out=CL[:], in0=tmp[:, ND:2 * ND], scalar1=-8.0, scalar2=None, op0=ALU.mult), reads=[bmisc], writes=[bmisc])
                st.add("vector", lambda e: e.tensor_scalar(out=CL2[:], in0=tmp[:, ND:2 * ND], scalar1=-16.0, scalar2=None, op0=ALU.mult), reads=[bmisc], writes=[bmisc])
                TWO_PI = 2.0 * np.pi
                c1 = float(np.float32(6.28125))
                c2 = float(np.float32(TWO_PI - 6.28125))
                c3 = float(TWO_PI - 6.28125 - float(np.float32(TWO_PI - 6.28125)))
                st.add("vector", lambda e: e.tensor_scalar(out=ang[:], in0=ang[:], scalar1=invf[:, 0:1], scalar2=None, op0=ALU.mult), reads=[bmisc], writes=[bmisc])
                st.add("vector", lambda e: e.tensor_scalar(out=kf[:], in0=ang[:], scalar1=float(1.0 / TWO_PI), scalar2=None, op0=ALU.mult), reads=[bmisc], writes=[bmisc])
                MAGIC = 12582912.0
                st.add("vector", lambda e: e.tensor_scalar(out=kf[:], in0=kf[:], scalar1=MAGIC, scalar2=None, op0=ALU.add), reads=[bmisc], writes=[bmisc])
                st.add("vector", lambda e: e.tensor_scalar(out=kf[:], in0=kf[:], scalar1=-MAGIC, scalar2=None, op0=ALU.add), reads=[bmisc], writes=[bmisc])
                for cc_ in (c1, c2, c3):
                    st.add("vector", lambda e, cc_=cc_: e.tensor_scalar(out=rr[:], in0=kf[:], scalar1=float(-cc_), scalar2=None, op0=ALU.mult), reads=[bmisc], writes=[bmisc])
                    st.add("vector", lambda e: e.tensor_tensor(out=ang[:], in0=ang[:], in1=rr[:], op=ALU.add), reads=[bmisc], writes=[bmisc])
                PI = float(np.pi)

                def wrap(dst, src, shift):
                    st.add("vector", lambda e: e.tensor_scalar(out=dst[:], in0=src[:], scalar1=float(shift), scalar2=None, op0=ALU.add), reads=[bmisc], writes=[bmisc])
                    st.add("vector", lambda e: e.tensor_scalar(out=kf[:], in0=dst[:], scalar1=PI, scalar2=-2 * PI, op0=ALU.is_gt, op1=ALU.mult), reads=[bmisc], writes=[bmisc])
                    st.add("vector", lambda e: e.tensor_tensor(out=dst[:], in0=dst[:], in1=kf[:], op=ALU.add), reads=[bmisc], writes=[bmisc])
                    st.add("vector", lambda e: e.tensor_scalar(out=kf[:], in0=dst[:], scalar1=-PI, scalar2=2 * PI, op0=ALU.is_lt, op1=ALU.mult), reads=[bmisc], writes=[bmisc])
                    st.add("vector", lambda e: e.tensor_tensor(out=dst[:], in0=dst[:], in1=kf[:], op=ALU.add), reads=[bmisc], writes=[bmisc])
                    st.add("vector", lambda e: e.tensor_scalar(out=dst[:], in0=dst[:], scalar1=3.1415925, scalar2=-3.1415925, op0=ALU.min, op1=ALU.max), reads=[bmisc], writes=[bmisc])

                CL_ = 3.1415925
                st.add("vector", lambda e: e.tensor_scalar(out=rr[:], in0=ang[:], scalar1=CL_, scalar2=-CL_, op0=ALU.min, op1=ALU.max), reads=[bmisc], writes=[bmisc])
                st.add("scalar", lambda e: e.activation(out=SN[:], in_=rr[:], func=AF.Sin), reads=[bmisc], writes=[bmisc])
                st.add("vector", lambda e: e.tensor_scalar(out=kf[:], in0=rr[:], scalar1=-1.0, scalar2=None, op0=ALU.mult), reads=[bmisc], writes=[bmisc])
                st.add("vector", lambda e: e.tensor_tensor(out=kf[:], in0=kf[:], in1=rr[:], op=ALU.max), reads=[bmisc], writes=[bmisc])
                st.add("vector", lambda e: e.tensor_scalar(out=kf[:], in0=kf[:], scalar1=-1.0, scalar2=PI / 2, op0=ALU.mult, op1=ALU.add), reads=[bmisc], writes=[bmisc])
                st.add("scalar", lambda e: e.activation(out=CS[:], in_=kf[:], func=AF.Sin), reads=[bmisc], writes=[bmisc])
                st.add("vector", lambda e: e.tensor_scalar(out=SN[0:32, :], in0=SN[0:32, :], scalar1=-1.0, scalar2=None, op0=ALU.mult), reads=[bmisc], writes=[bmisc])
                st.add("scalar", lambda e: e.dma_start(out=CSd[:, :], in_=CS[:]), reads=[bmisc], writes=[Buf()], dma=True)
                st.add("scalar", lambda e: e.dma_start(out=SNd[:, :], in_=SN[:]), reads=[bmisc], writes=[Buf()], dma=True)
                k = 0
                for n in gnames:
                    nl = gshapes[n][0] // NCORES
                    Fc = gshapes[n][2]
                    for c in range(nl):
                        s = k % 2
                        st.add("sync", lambda e, n=n, c=c, s=s, Fc=Fc: e.dma_start(out=stg[s][:, 0:Fc], in_=wsh[n][c]), writes=[bstg[s]], dma=True)
                        ce = ("vector", "scalar", "gpsimd")[k % 3]
                        if ce == "scalar":
                            st.add(ce, lambda e, s=s, Fc=Fc: e.activation(out=stb[s][:, 0:Fc], in_=stg[s][:, 0:Fc], func=AF.Copy), reads=[bstg[s]], writes=[bstb[s]])
                        else:
                            st.add(ce, lambda e, s=s, Fc=Fc: e.tensor_copy(out=stb[s][:, 0:Fc], in_=stg[s][:, 0:Fc]), reads=[bstg[s]], writes=[bstb[s]])
                        st.add("sync", lambda e, n=n, c=c, s=s, Fc=Fc: e.dma_start(out=ib_chunk(n, c), in_=stb[s][:, 0:Fc]), reads=[bstb[s]], writes=[bib[n]], dma=True)
                        k += 1
                    if n in GATHER_IN_PREP:
                        issue_gather(st, n, [bib[n]])
                import os
                if os.environ.get("KWAITAG"):
                    st.add("gpsimd", lambda e: e.memset(tmp[:, 0:1], 0.0), extra=[(ag_sem[n], 2 * len(ag_chunks(n))) for n in gnames])
                st.emit()

        class Ring:
            def __init__(self, es, name, n, width, dtype=BF16):
                self.t = [es.enter_context(sbt(f"{name}{i}", [128, width], dtype)) for i in range(n)]
                self.b = [Buf() for _ in range(n)]
                self.k = 0

            def nxt(self):
                i = self.k % len(self.t)
                self.k += 1
                return self.t[i], self.b[i]

        class Banks:
            def __init__(self, ids):
                self.ids = ids
                self.b = {i: PSB[i] for i in ids}
                self.k = 0

            def nxt(self):
                i = self.ids[self.k % len(self.ids)]
                self.k += 1
                return PS[i], self.b[i]

        def load_w(st, name, c, wt, wb, ncols, first):
            src, need = ob_chunk(name, c)
            st.add("sync", lambda e: e.dma_start(out=wt[:, 0:ncols], in_=src), writes=[wb], dma=True, extra=[(ag_sem[name], need)])

        def linear(st, name, chunks, KT, rhs, rhs_bufs, ring, banks, evac, M=128, nth=2, first_flag=[True]):
            Fc = gshapes[name][2]
            wcw = Fc // KT
            for i, c in enumerate(chunks):
                wt, wb = ring.nxt()
                load_w(st, name, c, wt, wb, Fc, i == 0)
                pss = [banks.nxt() for _ in range(nth)]
                for kt in range(KT):
                    for th in range(nth):
                        ps, pb = pss[th]
                        r_ap = rhs(kt, th)
                        st.add("tensor", lambda e, ps=ps, wt=wt, kt=kt, th=th, r_ap=r_ap: e.matmul(ps[0:M, :], wt[:, kt * wcw:kt * wcw + M], r_ap, start=(kt == 0), stop=(kt == KT - 1)),
                               reads=[wb] + rhs_bufs(kt, th), writes=[pb])
                for th in range(nth):
                    evac(i, c, th, pss[th][0], pss[th][1])

        def rstd_from(st, ps, pb, out_ap, outbuf, nfeat, tmp, tb, mul=1.0):
            st.add("scalar", lambda e: e.activation(out=tmp, in_=ps, func=AF.Sqrt, scale=1.0 / nfeat, bias=1e-6), reads=[pb], writes=[tb])
            st.add("vector", lambda e: e.reciprocal(out=out_ap, in_=tmp), reads=[tb], writes=[outbuf])
            if mul != 1.0:
                st.add("vector", lambda e: e.tensor_scalar(out=out_ap, in0=out_ap, scalar1=float(mul), scalar2=None, op0=ALU.mult), reads=[outbuf], writes=[outbuf])

        def stage_epi(name, src, ysrc, gcoef, gout, hdst, ymul_applied=True):
            st = Stage(nc, name)
            with contextlib.ExitStack() as es:
                alloc_ps(es)
                HNEW = es.enter_context(sbt("HNEW", [128, ND, TH], F32))
                ystg = Ring(es, "ystg", 3, TH, F32)
                sq = Ring(es, "sq", 2, TH, F32)
                rt = es.enter_context(sbt("rt", [128, TH], F32))
                rtb = Buf()
                bh = [Buf() for _ in range(ND)]
                bhn = Buf()
                brs = Buf()
                bd = Buf()
                banks = Banks([0, 1])
                for th in range(2):
                    tsl = slice(th * TH, (th + 1) * TH)
                    ps, pb = banks.nxt()
                    for dt in range(ND):
                        st.add("sync", lambda e, dt=dt, tsl=tsl: e.dma_start(out=HNEW[:, dt, :], in_=src[dt][:, tsl]), writes=[bh[dt]], dma=True)
                        if ysrc is not None:
                            yt, yb = ystg.nxt()
                            st.add("scalar", lambda e, dt=dt, tsl=tsl, yt=yt: e.dma_start(out=yt[:], in_=ysrc[dt][:, tsl]), writes=[yb], dma=True)
                            st.add("gpsimd", lambda e, yt=yt, tsl=tsl: e.tensor_tensor(out=yt[:], in0=yt[:], in1=RSTDY[:, tsl], op=ALU.mult), reads=[yb], writes=[yb])
                            st.add("vector", lambda e, dt=dt, yt=yt: e.scalar_tensor_tensor(out=HNEW[:, dt, :], in0=yt[:], scalar=vcol(gcoef, dt), in1=HNEW[:, dt, :], op0=ALU.mult, op1=ALU.add),
                                   reads=[yb, bh[dt]], writes=[bh[dt]])
                        if hdst is not None and (ysrc is not None or hdst is not src):
                            st.add("scalar", lambda e, dt=dt, tsl=tsl: e.dma_start(out=hdst[dt][:, tsl], in_=HNEW[:, dt, :]), reads=[bh[dt]], writes=[bd], dma=True)
                        if gout is not None:
                            qt, qb = sq.nxt()
                            st.add("scalar", lambda e, dt=dt, qt=qt: e.activation(out=qt[:], in_=HNEW[:, dt, :], func=AF.Square), reads=[bh[dt]], writes=[qb])
                            st.add("tensor", lambda e, dt=dt, qt=qt, ps=ps: e.matmul(ps[:, :], ONES[:, :], qt[:], start=(dt == 0), stop=(dt == ND - 1)), reads=[qb], writes=[pb])
                    if gout is not None:
                        rstd_from(st, ps[:, :], pb, RSTD[:, tsl], brs, float(D), rt[:], rtb)
                        for dt in range(ND):
                            st.add("vector", lambda e, dt=dt, tsl=tsl: e.scalar_tensor_tensor(out=HN[:, dt, tsl], in0=HNEW[:, dt, :], scalar=vcol(gout, dt), in1=RSTD[:, tsl], op0=ALU.mult, op1=ALU.mult),
                                   reads=[bh[dt], brs], writes=[bhn])
                st.emit()

        def stage_renorm(name, gout):
            st = Stage(nc, name)
            with contextlib.ExitStack() as es:
                hs = Ring(es, "hs", 3, T, F32)
                bhn = Buf()
                for dt in range(ND):
                    ht, hb = hs.nxt()
                    st.add("sync", lambda e, dt=dt, ht=ht: e.dma_start(out=ht[:], in_=Hd[dt]), writes=[hb], dma=True)
                    st.add("vector", lambda e, dt=dt, ht=ht: e.scalar_tensor_tensor(out=HN[:, dt, :], in0=ht[:], scalar=vcol(gout, dt), in1=RSTD[:, :], op0=ALU.mult, op1=ALU.mult),
                           reads=[hb], writes=[bhn])
                st.emit()

        class YOut:
            def __init__(self, st, es, mul):
                self.st = st
                self.ystg = Ring(es, "yo", 3, TH, F32)
                self.sq = Ring(es, "ysq", 2, TH, F32)
                self.ssb = Banks([6, 7])
                self.ss = [self.ssb.nxt() for _ in range(2)]
                self.rt = es.enter_context(sbt("yrt", [128, TH], F32))
                self.rtb = Buf()
                self.mul = mul
                self.bd = Buf()
                self.brs = Buf()

            def evac(self, i, dt, th, ps, pb, ntot, pre=None):
                st = self.st
                tsl = slice(th * TH, (th + 1) * TH)
                yt, yb = self.ystg.nxt()
                if pre is None:
                    st.add("vector", lambda e: e.tensor_copy(out=yt[:], in_=ps[:, :]), reads=[pb], writes=[yb])
                else:
                    pt, ptb = pre
                    st.add("vector", lambda e: e.tensor_tensor(out=yt[:], in0=ps[:, :], in1=pt[:], op=ALU.add), reads=[pb, ptb], writes=[yb])
                st.add("scalar", lambda e: e.dma_start(out=Yd[dt][:, tsl], in_=yt[:]), reads=[yb], writes=[self.bd], dma=True)
                qt, qb = self.sq.nxt()
                st.add("scalar", lambda e: e.activation(out=qt[:], in_=yt[:], func=AF.Square), reads=[yb], writes=[qb])
                sps, spb = self.ss[th]
                st.add("tensor", lambda e: e.matmul(sps[:, :], ONES[:, :], qt[:], start=(i == 0), stop=(i == ntot - 1)), reads=[qb], writes=[spb])
                if i == ntot - 1:
                    rstd_from(st, sps[:, :], spb, RSTDY[:, tsl], self.brs, float(D), self.rt[:], self.rtb, mul=self.mul)

        def stage_ffn(q):
            st = Stage(nc, f"ffn{q}")
            for n_ in GATHER_AT_FFN[q]:
                issue_gather(st, n_, [])
            with contextlib.ExitStack() as es:
                alloc_ps(es)
                ACT = es.enter_context(sbt("ACT", [128, NFH, T], BF16))
                ring = Ring(es, "wr", 3, max(ND, NFH) * 128)
                sg = Ring(es, "sg", 3, TH, F32)
                yo = YOut(st, es, 0.5)
                yp = Ring(es, "yp", 3, TH, F32)
                bact = [Buf() for _ in range(NFH)]
                bhn = Buf()
                bdy = [[Buf() for _ in range(2)] for _ in range(ND)]
                NP = len(cfg.PARTS)
                for hf, (p0, pn) in enumerate(cfg.PARTS):
                    banks = Banks([0, 1, 2, 3, 4, 5])
                    sgs = {}

                    def evac_gu(i, c, th, ps, pb):
                        j = i // 2
                        if i % 2 == 0:
                            t_, b_ = sg.nxt()
                            sgs[(j, th)] = (t_, b_)
                            st.add("scalar", lambda e: e.activation(out=t_[:], in_=ps[:, :], func=AF.Silu), reads=[pb], writes=[b_])
                        else:
                            t_, b_ = sgs.pop((j, th))
                            st.add("vector", lambda e: e.tensor_tensor(out=ACT[:, j, th * TH:(th + 1) * TH], in0=ps[:, :], in1=t_[:], op=ALU.mult), reads=[pb, b_], writes=[bact[j]])

                    linear(st, f"gu{q}{hf}", list(range(2 * pn)), ND, lambda kt, th: HN[:, kt, th * TH:(th + 1) * TH], lambda kt, th: [bhn], ring, banks, evac_gu)

                    def evac_dn(i, c, th, ps, pb):
                        tsl = slice(th * TH, (th + 1) * TH)
                        if hf == 0:
                            yt, yb = yo.ystg.nxt()
                            st.add("vector", lambda e: e.tensor_copy(out=yt[:], in_=ps[:, :]), reads=[pb], writes=[yb])
                            st.add("scalar", lambda e: e.dma_start(out=Yd[c][:, tsl], in_=yt[:]), reads=[yb], writes=[bdy[c][th]], dma=True)
                        elif hf < NP - 1:
                            pt, ptb = yp.nxt()
                            st.add("scalar", lambda e: e.dma_start(out=pt[:], in_=Yd[c][:, tsl]), reads=[bdy[c][th]], writes=[ptb], dma=True)
                            yt, yb = yo.ystg.nxt()
                            st.add("vector", lambda e: e.tensor_tensor(out=yt[:], in0=ps[:, :], in1=pt[:], op=ALU.add), reads=[pb, ptb], writes=[yb])
                            st.add("scalar", lambda e: e.dma_start(out=Yd[c][:, tsl], in_=yt[:]), reads=[yb], writes=[bdy[c][th]], dma=True)
                        else:
                            pt, ptb = yp.nxt()
                            st.add("scalar", lambda e: e.dma_start(out=pt[:], in_=Yd[c][:, tsl]), reads=[bdy[c][th]], writes=[ptb], dma=True)
                            yo.evac(i, c, th, ps, pb, ND, pre=(pt, ptb))

                    import os as _o3
                    _dbg = int(_o3.environ.get("KFFNDBG", "9"))
                    if _dbg == 1 or (_dbg == 2 and hf > 0):
                        break
                    linear(st, f"dn{q}{hf}", list(range(ND)), pn, lambda kt, th: ACT[:, kt, th * TH:(th + 1) * TH], lambda kt, th: [bact[kt]], ring, Banks([0, 1, 2, 3]), evac_dn)
                st.emit()

        def exchange(st, src_writes, ib, ob, sem, after_bufs):
            def cc(e):
                ins = e.collective_compute("AllGather", ALU.bypass, replica_groups=PAIRS, ins=[ib.ap().opt()], outs=[ob.ap().opt()])
                ins.then_inc(sem)
                return ins
            st.add("gpsimd", cc, reads=after_bufs)

        def stage_lru_a():
            st = Stage(nc, "lruA")
            with contextlib.ExitStack() as es:
                ring = Ring(es, "wr", 3, ND * 128)
                alloc_ps(es)
                og = Ring(es, "og", 4, TH, F32)
                HAL = es.enter_context(sbt("HAL", [128, ND, 4], F32))
                bhal = Buf()
                bhn = Buf()
                bd = Buf()
                banks = Banks([0, 1, 2, 3, 4, 5])
                st.add("vector", lambda e: e.memset(HAL[:], 0.0), writes=[bhal])

                def evac(i, c, th, ps, pb):
                    tsl = slice(th * TH, (th + 1) * TH)
                    t_, b_ = og.nxt()
                    if c < ND:
                        st.add("scalar", lambda e: e.activation(out=t_[:], in_=ps[:, :], func=AF.Gelu_apprx_tanh), reads=[pb], writes=[b_])
                        st.add("scalar", lambda e: e.dma_start(out=GBd[c][:, tsl], in_=t_[:]), reads=[b_], writes=[bd], dma=True)
                    else:
                        ct = c - ND
                        st.add("vector", lambda e: e.tensor_copy(out=t_[:], in_=ps[:, :]), reads=[pb], writes=[b_])
                        st.add("scalar", lambda e: e.dma_start(out=XBd[ct][:, tsl], in_=t_[:]), reads=[b_], writes=[bd], dma=True)
                        if th == 1:
                            st.add("vector", lambda e: e.tensor_copy(out=HAL[:, ct, 0:3], in_=t_[:, TH - 3:TH]), reads=[b_], writes=[bhal])

                linear(st, "win", list(range(2 * ND)), ND, lambda kt, th: HN[:, kt, th * TH:(th + 1) * TH], lambda kt, th: [bhn], ring, banks, evac)
                bx = Buf()
                st.add("scalar", lambda e: e.dma_start(out=halo_i.ap().rearrange("(c p) k -> p c k", p=128), in_=HAL[:]), reads=[bhal], writes=[bx], dma=True)
                exchange(st, None, halo_i, halo_o, x_sem[0], [bx])
                st.emit()

        def stage_lru_b():
            st = Stage(nc, "lruB")
            NBLK = cfg.NBLK
            with contextlib.ExitStack() as es:
                ring = Ring(es, "wr", 4, 256)
                alloc_ps(es)
                XP = Ring(es, "xp", 4, T + 4, F32)
                XC = Ring(es, "xc", 4, T, F32)
                XCB = Ring(es, "xcb", 4, T, BF16)
                RR = Ring(es, "rr", 3, T, F32)
                II = Ring(es, "ii", 3, T, F32)
                AA = Ring(es, "aa", 3, T, F32)
                MM = Ring(es, "mm", 3, T, F32)
                UU = Ring(es, "uu", 3, T, F32)
                HH = Ring(es, "hh", 2, T, F32)
                HAL = es.enter_context(sbt("HALp", [128, ND, 4], F32))
                FIN = es.enter_context(sbt("FIN", [128, ND, 4], F32))
                bhal = Buf()
                bfin = Buf()
                bd = Buf()
                banks = Banks([0, 1, 2, 3, 4, 5, 6, 7])
                st.add("scalar", lambda e: e.dma_start(out=HAL[:], in_=halo_o.ap()[0:ND * 128, :].rearrange("(c p) k -> p c k", p=128)), writes=[bhal], dma=True, extra=[(x_sem[0], 1)])
                st.add("vector", lambda e: e.tensor_scalar(out=HAL[:], in0=HAL[:], scalar1=FLAG[:, 0:1], scalar2=None, op0=ALU.mult), reads=[bhal], writes=[bhal])
                st.add("vector", lambda e: e.memset(FIN[:], 0.0), writes=[bfin])
                first = [True]
                import os as _o4
                KLB = int(_o4.environ.get("KLB", "9"))
                for blk in range(NBLK if KLB > 1 else 0):
                    xcb = []
                    xcs = []
                    for s in range(2):
                        ct = blk * 2 + s
                        xp, xpb = XP.nxt()
                        st.add("sync", lambda e, xp=xp, ct=ct: e.dma_start(out=xp[:, 4:T + 4], in_=XBd[ct]), writes=[xpb], dma=True)
                        st.add("vector", lambda e, xp=xp, ct=ct: e.tensor_copy(out=xp[:, 1:4], in_=HAL[:, ct, 0:3]), reads=[bhal], writes=[xpb])
                        xc, xcbuf = XC.nxt()
                        st.add("gpsimd", lambda e, xp=xp, xc=xc, ct=ct: e.tensor_scalar(out=xc[:], in0=xp[:, 1:T + 1], scalar1=vcol("cw0", ct), scalar2=vcol("cb", ct), op0=ALU.mult, op1=ALU.add), reads=[xpb], writes=[xcbuf])
                        for k in range(1, 4):
                            st.add("vector", lambda e, xp=xp, xc=xc, ct=ct, k=k: e.scalar_tensor_tensor(out=xc[:], in0=xp[:, 1 + k:T + 1 + k], scalar=vcol(f"cw{k}", ct), in1=xc[:], op0=ALU.mult, op1=ALU.add), reads=[xpb, xcbuf], writes=[xcbuf])
                        xb_, xbb = XCB.nxt()
                        st.add("scalar", lambda e, xb_=xb_, xc=xc: e.activation(out=xb_[:], in_=xc[:], func=AF.Copy), reads=[xcbuf], writes=[xbb])
                        xcb.append((xb_, xbb))
                        xcs.append((xc, xcbuf))
                    res = {}
                    if KLB == 2:
                        continue

                    def evac(i, c, th, ps, pb, blk=blk):
                        gi = (c // 2) % 2
                        nt = c % 2
                        ct = blk * 2 + nt
                        key = (gi, nt)
                        if key not in res:
                            res[key] = (RR if gi == 0 else II).nxt()
                        t_, b_ = res[key]
                        st.add("vector", lambda e: e.tensor_scalar(out=t_[:, th * TH:(th + 1) * TH], in0=ps[:, :], scalar1=vcol("bga" if gi == 0 else "bgx", ct), scalar2=None, op0=ALU.add), reads=[pb], writes=[b_])
                        st.add("scalar", lambda e: e.activation(out=t_[:, th * TH:(th + 1) * TH], in_=t_[:, th * TH:(th + 1) * TH], func=AF.Sigmoid), reads=[b_], writes=[b_])

                    linear(st, "gates", [blk * 4 + i for i in range(4)], 2, lambda kt, th: xcb[kt][0][:, th * TH:(th + 1) * TH], lambda kt, th: [xcb[kt][1]], ring, banks, evac, first_flag=first)
                    if KLB == 3:
                        continue
                    for nt in range(2):
                        ct = blk * 2 + nt
                        r_, rb = res[(0, nt)]
                        i_, ib_ = res[(1, nt)]
                        xc, xcbuf = xcs[nt]
                        a_, ab = AA.nxt()
                        m_, mb = MM.nxt()
                        u_, ub = UU.nxt()
                        h_, hb = HH.nxt()
                        st.add("vector", lambda e, m_=m_, r_=r_, ct=ct: e.tensor_scalar(out=m_[:], in0=r_[:], scalar1=CL[:, ct:ct + 1], scalar2=None, op0=ALU.mult), reads=[rb], writes=[mb])
                        st.add("scalar", lambda e, a_=a_, m_=m_: e.activation(out=a_[:], in_=m_[:], func=AF.Exp), reads=[mb], writes=[ab])
                        st.add("scalar", lambda e, m_=m_: e.activation(out=m_[:], in_=m_[:], func=AF.Exp, scale=2.0), reads=[mb], writes=[mb])
                        st.add("scalar", lambda e, m_=m_: e.activation(out=m_[:], in_=m_[:], func=AF.Sqrt, scale=-1.0, bias=1.0), reads=[mb], writes=[mb])
                        st.add("gpsimd", lambda e, u_=u_, i_=i_, xc=xc: e.tensor_tensor(out=u_[:], in0=i_[:], in1=xc[:], op=ALU.mult), reads=[ib_, xcbuf], writes=[ub])
                        st.add("vector", lambda e, u_=u_, m_=m_: e.tensor_tensor(out=u_[:], in0=u_[:], in1=m_[:], op=ALU.mult), reads=[ub, mb], writes=[ub])
                        st.add("scalar", lambda e, a_=a_, ct=ct: e.dma_start(out=Ad[ct], in_=a_[:]), reads=[ab], writes=[bd], dma=True)
                        st.add("scalar", lambda e, u_=u_, ct=ct: e.dma_start(out=Ud[ct], in_=u_[:]), reads=[ub], writes=[bd], dma=True)
                        if KLB == 4:
                            continue
                        st.add("vector", lambda e, h_=h_, a_=a_, u_=u_: e.tensor_tensor_scan(out=h_[:, 0:TH], data0=a_[:, 0:TH], data1=u_[:, 0:TH], initial=0.0, op0=ALU.mult, op1=ALU.add), reads=[ab, ub], writes=[hb])
                        st.add("vector", lambda e, h_=h_, a_=a_, u_=u_: e.tensor_tensor_scan(out=h_[:, TH:T], data0=a_[:, TH:T], data1=u_[:, TH:T], initial=h_[:, TH - 1:TH], op0=ALU.mult, op1=ALU.add), reads=[ab, ub, hb], writes=[hb])
                        st.add("scalar", lambda e, h_=h_, ct=ct: e.dma_start(out=car_i.ap().rearrange("(c p) k -> p c k", p=128)[:, ct, 0:1], in_=h_[:, T - 1:T], allow_slow_non_contiguous=True), reads=[hb], writes=[bfin], dma=True)
                exchange(st, None, car_i, car_o, x_sem[1], [bfin])
                st.emit()

        def stage_lru_c():
            st = Stage(nc, "lruC")
            with contextlib.ExitStack() as es:
                AA = Ring(es, "aa", 3, T, F32)
                UU = Ring(es, "uu", 3, T, F32)
                GG = Ring(es, "gg", 3, T, F32)
                HH = Ring(es, "hh", 3, T, F32)
                alloc_ps(es)
                CAR = es.enter_context(sbt("CAR", [128, ND, 1], F32))
                ring = Ring(es, "wr", 3, ND * 128)
                yo = YOut(st, es, 1.0)
                bcar = Buf()
                byb = [Buf() for _ in range(ND)]
                st.add("scalar", lambda e: e.dma_start(out=CAR[:], in_=car_o.ap()[0:ND * 128, :].rearrange("(c p) k -> p c k", p=128), allow_slow_non_contiguous=True), writes=[bcar], dma=True, extra=[(x_sem[1], 1)])
                st.add("vector", lambda e: e.tensor_scalar(out=CAR[:], in0=CAR[:], scalar1=FLAG[:, 0:1], scalar2=None, op0=ALU.mult), reads=[bcar], writes=[bcar])
                for ct in range(ND):
                    a_, ab = AA.nxt()
                    u_, ub = UU.nxt()
                    g_, gb = GG.nxt()
                    h_, hb = HH.nxt()
                    st.add("sync", lambda e, a_=a_, ct=ct: e.dma_start(out=a_[:], in_=Ad[ct]), writes=[ab], dma=True)
                    st.add("sync", lambda e, u_=u_, ct=ct: e.dma_start(out=u_[:], in_=Ud[ct]), writes=[ub], dma=True)
                    st.add("sync", lambda e, g_=g_, ct=ct: e.dma_start(out=g_[:], in_=GBd[ct]), writes=[gb], dma=True)
                    st.add("vector", lambda e, h_=h_, a_=a_, u_=u_, ct=ct: e.tensor_tensor_scan(out=h_[:, 0:TH], data0=a_[:, 0:TH], data1=u_[:, 0:TH], initial=CAR[:, ct, 0:1], op0=ALU.mult, op1=ALU.add), reads=[ab, ub, bcar], writes=[hb])
                    st.add("vector", lambda e, h_=h_, a_=a_, u_=u_: e.tensor_tensor_scan(out=h_[:, TH:T], data0=a_[:, TH:T], data1=u_[:, TH:T], initial=h_[:, TH - 1:TH], op0=ALU.mult, op1=ALU.add), reads=[ab, ub, hb], writes=[hb])
                    st.add("gpsimd", lambda e, h_=h_, g_=g_, ct=ct: e.tensor_tensor(out=HN[:, ct, :], in0=h_[:], in1=g_[:], op=ALU.mult), reads=[hb, gb], writes=[byb[ct]])

                def evac(i, c, th, ps, pb):
                    yo.evac(i, c, th, ps, pb, ND)

                linear(st, "wout", list(range(ND)), ND, lambda kt, th: HN[:, kt, th * TH:(th + 1) * TH], lambda kt, th: [byb[kt]], ring, Banks([0, 1, 2, 3]), evac)
                st.emit()

        def rope_apply(st, out_ap, outbuf, x_t, xb, xs_t, xsb, np_, csl, tmp_ring, CS, SN, bcs):
            t1, b1 = tmp_ring.nxt()
            st.add("vector", lambda e: e.tensor_tensor(out=t1[0:np_, :], in0=x_t, in1=CS[0:np_, csl], op=ALU.mult), reads=[xb, bcs], writes=[b1])
            t2, b2 = tmp_ring.nxt()
            st.add("gpsimd", lambda e: e.tensor_tensor(out=t2[0:np_, :], in0=xs_t, in1=SN[0:np_, csl], op=ALU.mult), reads=[xsb, bcs], writes=[b2])
            st.add("vector", lambda e: e.tensor_tensor(out=out_ap, in0=t1[0:np_, :], in1=t2[0:np_, :], op=ALU.add), reads=[b1, b2], writes=[outbuf])

        def stage_kv():
            st = Stage(nc, "kv")
            with contextlib.ExitStack() as es:
                ring = Ring(es, "wr", 3, ND * 128)
                alloc_ps(es)
                CF = es.enter_context(sbt("CF", [128, KK + 2, T], F32))
                CB = es.enter_context(sbt("CB", [128, KK + 1, T], BF16))
                sq = Ring(es, "sq", 2, TH, F32)
                tr = Ring(es, "tr", 2, T, F32)
                CS = es.enter_context(sbt("CS", [64, T], F32))
                SN = es.enter_context(sbt("SN", [64, T], F32))
                bcs = Buf()
                st.add("scalar", lambda e: e.dma_start(out=CS[:], in_=CSd[:, :]), writes=[bcs], dma=True)
                st.add("scalar", lambda e: e.dma_start(out=SN[:], in_=SNd[:, :]), writes=[bcs], dma=True)
                rt = es.enter_context(sbt("rt", [128, TH], F32))
                RC = es.enter_context(sbt("RC", [128, T], F32))
                rtb, brc = Buf(), Buf()
                bcf = [Buf() for _ in range(KK + 2)]
                bcb = Buf()
                bhn = Buf()
                ssb = Banks([6, 7])
                ss = [ssb.nxt() for _ in range(2)]

                def evac(i, c, th, ps, pb):
                    tsl = slice(th * TH, (th + 1) * TH)
                    st.add("vector", lambda e: e.tensor_copy(out=CF[:, c, tsl], in_=ps[:, :]), reads=[pb], writes=[bcf[c]])
                    if c < KK:
                        qt, qb = sq.nxt()
                        st.add("scalar", lambda e: e.activation(out=qt[:], in_=CF[:, c, tsl], func=AF.Square), reads=[bcf[c]], writes=[qb])
                        sps, spb = ss[th]
                        st.add("tensor", lambda e: e.matmul(sps[:, :], ONES[:, :], qt[:], start=(c == 0), stop=(c == KK - 1)), reads=[qb], writes=[spb])
                        if c == KK - 1:
                            rstd_from(st, sps[:, :], spb, RC[:, tsl], brc, float(cfg.KL), rt[:], rtb)

                linear(st, "kvd", list(range(KK + 2)), ND, lambda kt, th: HN[:, kt, th * TH:(th + 1) * TH], lambda kt, th: [bhn], ring, Banks([0, 1, 2, 3]), evac)
                for c in range(KK):
                    st.add("vector", lambda e, c=c: e.scalar_tensor_tensor(out=CB[:, c, :], in0=CF[:, c, :], scalar=vcol("kln", c), in1=RC[:, :], op0=ALU.mult, op1=ALU.mult), reads=[bcf[c], brc], writes=[bcb])
                rope_apply(st, CB[0:64, KK, :], bcb, CF[0:64, KK, :], bcf[KK], CF[0:64, KK + 1, :], bcf[KK + 1], 64, slice(0, T), tr, CS, SN, bcs)
                bx = Buf()
                for c in range(KK):
                    st.add("scalar", lambda e, c=c: e.dma_start(out=kv_i[c * 128:(c + 1) * 128, :], in_=CB[:, c, :]), reads=[bcb], writes=[bx], dma=True)
                st.add("scalar", lambda e: e.dma_start(out=kv_i[KK * 128:KK * 128 + 64, :], in_=CB[0:64, KK, :]), reads=[bcb], writes=[bx], dma=True)
                exchange(st, None, kv_i, kv_o, x_sem[2], [bx])
                st.emit()

        def stage_q():
            st = Stage(nc, "qlat")
            with contextlib.ExitStack() as es:
                alloc_ps(es)
                ring = Ring(es, "wr", 3, ND * 128)
                CQF = es.enter_context(sbt("CQF", [128, KQ, T], F32))
                CQ = es.enter_context(sbt("CQ", [128, KQ, T], BF16))
                RQ = es.enter_context(sbt("RQ", [128, T], F32))
                rt = es.enter_context(sbt("rt", [128, TH], F32))
                sq = Ring(es, "sq", 2, TH, F32)
                rtb, brq = Buf(), Buf()
                bhn = Buf()
                bcq = Buf()
                bcqf = [Buf() for _ in range(KQ)]
                ssb = Banks([4, 5])
                ss = [ssb.nxt() for _ in range(2)]

                def evac_cq(i, c, th, ps, pb):
                    tsl = slice(th * TH, (th + 1) * TH)
                    st.add("vector", lambda e: e.tensor_copy(out=CQF[:, c, tsl], in_=ps[:, :]), reads=[pb], writes=[bcqf[c]])
                    qt, qb = sq.nxt()
                    st.add("scalar", lambda e: e.activation(out=qt[:], in_=CQF[:, c, tsl], func=AF.Square), reads=[bcqf[c]], writes=[qb])
                    sps, spb = ss[th]
                    st.add("tensor", lambda e: e.matmul(sps[:, :], ONES[:, :], qt[:], start=(c == 0), stop=(c == KQ - 1)), reads=[qb], writes=[spb])
                    if c == KQ - 1:
                        rstd_from(st, sps[:, :], spb, RQ[:, tsl], brq, float(cfg.QL), rt[:], rtb)

                linear(st, "wdq", list(range(KQ)), ND, lambda kt, th: HN[:, kt, th * TH:(th + 1) * TH], lambda kt, th: [bhn], ring, Banks([0, 1, 2, 3]), evac_cq)
                for c in range(KQ):
                    st.add("vector", lambda e, c=c: e.scalar_tensor_tensor(out=CQ[:, c, :], in0=CQF[:, c, :], scalar=vcol("qn", c), in1=RQ[:, :], op0=ALU.mult, op1=ALU.mult), reads=[bcqf[c], brq], writes=[bcq])
                    st.add("scalar", lambda e, c=c: e.dma_start(out=CQd[c * 128:(c + 1) * 128, :], in_=CQ[:, c, :]), reads=[bcq], writes=[Buf()], dma=True)
                st.emit()

        def stage_attn():
            st = Stage(nc, "attn")
            scale = 192.0 ** -0.5
            with contextlib.ExitStack() as es:
                alloc_ps(es, 6)
                PT = [es.enter_context(pst(f"PT{i}", [128, 1024], BF16)) for i in range(2)]
                ringq = Ring(es, "wq", 4, KQ * 128)
                ringk = Ring(es, "wk", 3, KK * 128)
                CQ = es.enter_context(sbt("CQ", [128, KQ, T], BF16))
                CKV = es.enter_context(sbt("CKV", [128, KK, 2 * T], BF16))
                KR = es.enter_context(sbt("KR", [64, 2 * T], BF16))
                CS = es.enter_context(sbt("CS", [64, T], F32))
                SN = es.enter_context(sbt("SN", [64, T], F32))
                QN = Ring(es, "qn", 2, T, BF16)
                QRF = Ring(es, "qrf", 2, T, F32)
                QR = Ring(es, "qr", 2, T, BF16)
                tr = Ring(es, "tr", 2, T, F32)
                KN = Ring(es, "kn", 2, 2 * T, BF16)
                VV = Ring(es, "vv", 2, 16 * 128, BF16)
                PP = Ring(es, "pp", 2, 2 * T, BF16)
                PTS = Ring(es, "pts", 2, 2 * T, BF16)
                OS = Ring(es, "os", 2, 128, BF16)
                SM = Ring(es, "sm", 4, 8, F32)
                ET = Ring(es, "et", 3, TH, F32)
                bpt = [Buf(), Buf()]
                bhn = Buf()
                bcq = Buf()
                bckv = Buf()
                bcs = Buf()
                bao = [Buf() for _ in range(NH)]
                st.add("scalar", lambda e: e.dma_start(out=CS[:], in_=CSd[:, :]), writes=[bcs], dma=True)
                st.add("scalar", lambda e: e.dma_start(out=SN[:], in_=SNd[:, :]), writes=[bcs], dma=True)
                for c in range(KQ):
                    st.add("scalar", lambda e, c=c: e.dma_start(out=CQ[:, c, :], in_=CQd[c * 128:(c + 1) * 128, :]), writes=[bcq], dma=True)
                KVR_ = KK * 128 + 64
                for c in range(KK):
                    st.add("scalar", lambda e, c=c: e.dma_start(out=CKV[:, c, 0:T], in_=kv_o[c * 128:(c + 1) * 128, :]), writes=[bckv], dma=True, extra=[(x_sem[2], 1)])
                    st.add("scalar", lambda e, c=c: e.dma_start(out=CKV[:, c, T:2 * T], in_=kv_i[c * 128:(c + 1) * 128, :]), writes=[bckv], dma=True)
                st.add("scalar", lambda e: e.dma_start(out=KR[:, 0:T], in_=kv_o[KK * 128:KK * 128 + 64, :]), writes=[bckv], dma=True, extra=[(x_sem[2], 1)])
                st.add("scalar", lambda e: e.dma_start(out=KR[:, T:2 * T], in_=kv_i[KK * 128:KK * 128 + 64, :]), writes=[bckv], dma=True)
                bS = Banks([0, 1, 2, 3])
                SB = [bS.nxt() for _ in range(4)]
                bO = Banks([4, 5])
                bproj = bS
                for h in range(NH):
                    qn_t, qn_b = QN.nxt()
                    qrf = {}

                    def evac_q(i, c, th, ps, pb):
                        tsl = slice(th * TH, (th + 1) * TH)
                        kind = c % 3
                        if kind == 0:
                            st.add("scalar", lambda e: e.activation(out=qn_t[:, tsl], in_=ps[:, :], func=AF.Copy), reads=[pb], writes=[qn_b])
                        else:
                            if kind not in qrf:
                                qrf[kind] = QRF.nxt()
                            t_, b_ = qrf[kind]
                            st.add("vector", lambda e: e.tensor_copy(out=t_[0:64, tsl], in_=ps[0:64, :]), reads=[pb], writes=[b_])

                    linear(st, "wuq", [3 * h, 3 * h + 1, 3 * h + 2], KQ, lambda kt, th: CQ[:, kt, th * TH:(th + 1) * TH], lambda kt, th: [bcq], ringq, bproj, evac_q)
                    qr_t, qr_b = QR.nxt()
                    rope_apply(st, qr_t[0:64, :], qr_b, qrf[1][0][0:64, :], qrf[1][1], qrf[2][0][0:64, :], qrf[2][1], 64, slice(0, T), tr, CS, SN, bcs)
                    kn_t, kn_b = KN.nxt()
                    wt, wb = ringk.nxt()
                    load_w(st, "kvu", 2 * h, wt, wb, KK * 128, h == 0)
                    for kb in range(4):
                        ps, pb = bproj.nxt()
                        for kt in range(KK):
                            st.add("tensor", lambda e, ps=ps, wt=wt, kt=kt, kb=kb: e.matmul(ps[:, :], wt[:, kt * 128:(kt + 1) * 128], CKV[:, kt, kb * TH:(kb + 1) * TH], start=(kt == 0), stop=(kt == KK - 1)), reads=[wb, bckv], writes=[pb])
                        st.add("scalar", lambda e, ps=ps, kb=kb, kn_t=kn_t: e.activation(out=kn_t[:, kb * TH:(kb + 1) * TH], in_=ps[:, :], func=AF.Copy), reads=[pb], writes=[kn_b])
                    vv_t, vv_b = VV.nxt()
                    wt2, wb2 = ringk.nxt()
                    load_w(st, "kvu", 2 * h + 1, wt2, wb2, KK * 128, False)
                    for g4 in range(4):
                        ps, pb = bproj.nxt()
                        for j in range(4):
                            kt16 = g4 * 4 + j
                            for kt in range(KK):
                                st.add("tensor", lambda e, ps=ps, wt2=wt2, kt=kt, kt16=kt16, j=j: e.matmul(ps[:, j * 128:(j + 1) * 128], CKV[:, kt, kt16 * 128:(kt16 + 1) * 128], wt2[:, kt * 128:(kt + 1) * 128], start=(kt == 0), stop=(kt == KK - 1)), reads=[wb2, bckv], writes=[pb])
                        st.add("vector", lambda e, ps=ps, g4=g4, vv_t=vv_t: e.tensor_copy(out=vv_t[:, g4 * 512:(g4 + 1) * 512], in_=ps[:, :]), reads=[pb], writes=[vv_b])
                    for qi in range(8):
                        qsl = slice(qi * 128, (qi + 1) * 128)
                        nown = (qi + 1) * 128
                        blocks = [(0, 0, 512), (1, 512, 512)]
                        o0 = 0
                        while o0 < nown:
                            w = min(512, nown - o0)
                            blocks.append((2 + o0 // 512, T + o0, w))
                            o0 += 512
                        for (bi, k0, w) in blocks:
                            ps, pb = SB[bi]
                            st.add("tensor", lambda e, ps=ps, k0=k0, w=w: e.matmul(ps[:, 0:w], qn_t[:, qsl], kn_t[:, k0:k0 + w], start=True, stop=False), reads=[qn_b, kn_b], writes=[pb])
                            st.add("tensor", lambda e, ps=ps, k0=k0, w=w: e.matmul(ps[:, 0:w], qr_t[0:64, qsl], KR[0:64, k0:k0 + w], start=False, stop=True), reads=[qr_b, bckv], writes=[pb])
                        dbi = 2 + (qi * 128) // 512
                        dps, dpb = SB[dbi]
                        dc = (qi * 128) % 512
                        st.add("vector", lambda e, dps=dps, dc=dc: e.tensor_tensor(out=dps[:, dc:dc + 128], in0=dps[:, dc:dc + 128], in1=CMASK[:, :], op=ALU.add), reads=[dpb], writes=[dpb])
                        sm, smb = SM.nxt()
                        nblk = len(blocks)
                        for ii, (bi, k0, w) in enumerate(blocks):
                            ps, pb = SB[bi]
                            st.add("vector", lambda e, ps=ps, w=w, ii=ii, sm=sm: e.reduce_max(out=sm[:, ii:ii + 1], in_=ps[:, 0:w], axis=AX.X), reads=[pb], writes=[smb])
                        st.add("vector", lambda e, sm=sm, nblk=nblk: e.reduce_max(out=sm[:, 4:5], in_=sm[:, 0:nblk], axis=AX.X), reads=[smb], writes=[smb])
                        st.add("vector", lambda e, sm=sm: e.tensor_scalar(out=sm[:, 5:6], in0=sm[:, 4:5], scalar1=-scale, scalar2=None, op0=ALU.mult), reads=[smb], writes=[smb])
                        st.add("vector", lambda e, sm=sm: e.tensor_tensor(out=sm[:, 6:7], in0=sm[:, 5:6], in1=FLAG[:, 1:2], op=ALU.add), reads=[smb], writes=[smb])
                        pp, ppb = PP.nxt()
                        sm2, sm2b = SM.nxt()
                        for ii, (bi, k0, w) in enumerate(blocks):
                            ps, pb = SB[bi]
                            bcol = 6 if bi < 2 else 5
                            et, etb = ET.nxt()
                            st.add("vector", lambda e, ps=ps, w=w, sm=sm, bcol=bcol, et=et: e.tensor_scalar(out=et[:, 0:w], in0=ps[:, 0:w], scalar1=float(scale), scalar2=sm[:, bcol:bcol + 1], op0=ALU.mult, op1=ALU.add), reads=[pb, smb], writes=[etb])
                            st.add("scalar", lambda e, w=w, k0=k0, ii=ii, pp=pp, sm2=sm2, et=et: e.activation(out=pp[:, k0:k0 + w], in_=et[:, 0:w], func=AF.Exp, accum_out=sm2[:, ii:ii + 1]),
                                   reads=[etb], writes=[ppb, sm2b])
                        st.add("vector", lambda e, sm2=sm2, nblk=nblk: e.reduce_sum(out=sm2[:, 4:5], in_=sm2[:, 0:nblk], axis=AX.X), reads=[sm2b], writes=[sm2b])
                        st.add("vector", lambda e, sm2=sm2: e.reciprocal(out=sm2[:, 5:6], in_=sm2[:, 4:5]), reads=[sm2b], writes=[sm2b])
                        ktiles = list(range(8)) + [8 + i for i in range(qi + 1)]
                        pts, ptsb = PTS.nxt()
                        for g8 in range(0, len(ktiles), 8):
                            grp = ktiles[g8:g8 + 8]
                            pt = PT[(g8 // 8) % 2]
                            ptb = bpt[(g8 // 8) % 2]
                            for j, ktl in enumerate(grp):
                                st.add("tensor", lambda e, pt=pt, j=j, ktl=ktl, pp=pp: e.transpose(pt[:, j * 128:(j + 1) * 128], pp[:, ktl * 128:(ktl + 1) * 128], IDB[:, :]), reads=[ppb], writes=[ptb])
                            n_ = len(grp) * 128
                            k00 = grp[0] * 128
                            st.add("vector" if (g8 // 8) % 2 == 0 else "scalar",
                                   (lambda e, pt=pt, n_=n_, k00=k00, pts=pts: e.tensor_copy(out=pts[:, k00:k00 + n_], in_=pt[:, 0:n_])) if (g8 // 8) % 2 == 0 else
                                   (lambda e, pt=pt, n_=n_, k00=k00, pts=pts: e.activation(out=pts[:, k00:k00 + n_], in_=pt[:, 0:n_], func=AF.Copy)),
                                   reads=[ptb], writes=[ptsb])
                        ops_, opb = bO.nxt()
                        for ii, ktl in enumerate(ktiles):
                            st.add("tensor", lambda e, ops_=ops_, ktl=ktl, ii=ii, pts=pts, vv_t=vv_t, nk=len(ktiles): e.matmul(ops_[:, 0:128], pts[:, ktl * 128:(ktl + 1) * 128], vv_t[:, ktl * 128:(ktl + 1) * 128], start=(ii == 0), stop=(ii == nk - 1)),
                                   reads=[ptsb, vv_b], writes=[opb])
                        os_, osb = OS.nxt()
                        st.add("vector", lambda e, ops_=ops_, os_=os_, sm2=sm2: e.tensor_scalar(out=os_[:, :], in0=ops_[:, 0:128], scalar1=sm2[:, 5:6], scalar2=None, op0=ALU.mult), reads=[opb, sm2b], writes=[osb])
                        pt = PT[0]
                        st.add("tensor", lambda e, pt=pt, os_=os_: e.transpose(pt[:, 0:128], os_[:, :], IDB[:, :]), reads=[osb], writes=[bpt[0]])
                        st.add("vector", lambda e, pt=pt, h=h, qsl=qsl: e.tensor_copy(out=HN[:, h, qsl], in_=pt[:, 0:128]), reads=[bpt[0]], writes=[bao[h], bhn])
                st.emit()
            return

        def stage_wo():
            st = Stage(nc, "wo")
            with contextlib.ExitStack() as es:
                alloc_ps(es)
                ring = Ring(es, "wr", 3, ND * 128)
                yo = YOut(st, es, 1.0)
                bao = Buf()

                def evac(i, c, th, ps, pb):
                    yo.evac(i, c, th, ps, pb, ND)

                linear(st, "wo", list(range(ND)), NH, lambda kt, th: HN[:, kt, th * TH:(th + 1) * TH], lambda kt, th: [bao], ring, Banks([0, 1, 2, 3]), evac)
                st.emit()

        xs = [xT[i] for i in range(ND)]
        hs = [Hd[i] for i in range(ND)]
        ys = [Yd[i] for i in range(ND)]
        os_ = [outT[i] for i in range(ND)]
        import os
        nst = int(os.environ.get("KSTAGES", "99"))
        plan = [
            lambda: stage_prep(),
            lambda: stage_epi("n0", xs, None, None, "g00", None),
            lambda: stage_ffn(0),
            lambda: stage_epi("e0", xs, ys, "g01", "g02", hs),
            lambda: stage_lru_a(),
            lambda: stage_lru_b(),
            lambda: stage_lru_c(),
            lambda: stage_epi("e1", hs, ys, "g03", "g04", hs),
            lambda: stage_ffn(1),
            lambda: stage_epi("e2", hs, ys, "g05", "g10", hs),
            lambda: stage_ffn(2),
            lambda: stage_epi("e3", hs, ys, "g11", "kvn", hs),
            lambda: stage_kv(),
            lambda: stage_renorm("rn", "g12"),
            lambda: stage_q(),
            lambda: stage_attn(),
            lambda: stage_wo(),
            lambda: stage_epi("e4", hs, ys, "g13", "g14", hs),
            lambda: stage_ffn(3),
            lambda: stage_epi("e5", hs, ys, "g15", None, os_),
        ]
        for f in plan[:nst]:
            f()
        if os.environ.get("KDUMP"):
            kd = os.environ["KDUMP"]
            st = Stage(nc, "dump")
            if kd == "CS":
                with contextlib.ExitStack() as es:
                    rg = Ring(es, "dmp", 2, T, F32)
                    for dt, srcd in enumerate([CSd, SNd]):
                        t_, b_ = rg.nxt()
                        st.add("vector", lambda e, t_=t_: e.memset(t_[:], 0.0), writes=[b_])
                        st.add("sync", lambda e, t_=t_, srcd=srcd: e.dma_start(out=t_[0:64, :], in_=srcd[:, :]), reads=[b_], writes=[b_], dma=True)
                        st.add("sync", lambda e, dt=dt, t_=t_: e.dma_start(out=outT[dt], in_=t_[:]), reads=[b_], writes=[Buf()], dma=True)
                    st.emit()
            if kd in ("KV", "CQ"):
                with contextlib.ExitStack() as es:
                    rb_ = Ring(es, "dmpb", 2, T, BF16)
                    rg = Ring(es, "dmp", 2, T, F32)
                    srcb = kv_i if kd == "KV" else CQd
                    nrow = srcb.shape[0]
                    for dt in range(min(ND, (nrow + 127) // 128)):
                        r1 = min(nrow, (dt + 1) * 128)
                        np_ = r1 - dt * 128
                        tb, bb = rb_.nxt()
                        t_, b_ = rg.nxt()
                        st.add("vector", lambda e, t_=t_: e.memset(t_[:], 0.0), writes=[b_])
                        st.add("sync", lambda e, dt=dt, tb=tb, np_=np_, r1=r1: e.dma_start(out=tb[0:np_, :], in_=srcb[dt * 128:r1, :]), writes=[bb], dma=True)
                        st.add("vector", lambda e, t_=t_, tb=tb, np_=np_: e.tensor_copy(out=t_[0:np_, :], in_=tb[0:np_, :]), reads=[bb, b_], writes=[b_])
                        st.add("sync", lambda e, dt=dt, t_=t_: e.dma_start(out=outT[dt], in_=t_[:]), reads=[b_], writes=[Buf()], dma=True)
                    st.emit()
            src_t = {"H": Hd, "Y": Yd}.get(kd)
            with contextlib.ExitStack() as es:
                if src_t is None:
                    es.close()
                rg = Ring(es, "dmp", 2, T, F32)
                for dt in range(ND if src_t is not None else 0):
                    t_, b_ = rg.nxt()
                    st.add("sync", lambda e, dt=dt, t_=t_: e.dma_start(out=t_[:], in_=src_t[dt]), writes=[b_], dma=True)
                    st.add("sync", lambda e, dt=dt, t_=t_: e.dma_start(out=outT[dt], in_=t_[:]), reads=[b_], writes=[Buf()], dma=True)
                if src_t is not None:
                    st.emit()
    return nc


_CACHE = {}


def run(cfg, inp):
    groups = weight_groups(cfg, inp)
    vecs, vcols = small_vectors(cfg, inp)
    gshapes = {n: a.shape for n, a in groups.items()}
    key = (cfg.D, cfg.F, cfg.NH)
    if key not in _CACHE:
        _CACHE[key] = build(cfg, gshapes, vcols, vecs.shape[1])
    nc = _CACHE[key]
    x = np.asarray(inp["x"], np.float32)
    pos = np.asarray(inp["positions"])
    B = x.shape[0]
    invf = (10000.0 ** (-np.arange(0, 64, 2, dtype=np.float32) / 64.0)).astype(np.float32)
    invf = np.concatenate([invf, invf]).reshape(64, 1)
    ident = np.eye(128, dtype=np.float32)
    cmask = np.where(np.arange(128)[None, :] <= np.arange(128)[:, None], 0.0, NEG).astype(np.float32)
    in_maps = []
    for c in range(NCORES):
        b, half = c // 2, c % 2
        xs = x[b, half * T:(half + 1) * T, :]
        xT = np.ascontiguousarray(xs.T).reshape(cfg.ND, 128, T)
        p = np.asarray(pos[b, half * T:(half + 1) * T], np.int32)
        m = {"xT": xT, "vecs": vecs, "pos": np.ascontiguousarray(np.broadcast_to(p[None, :], (64, T))),
             "invf": invf, "ident": ident, "cmask": cmask,
             "flag": np.ascontiguousarray(np.broadcast_to(np.array([[float(half), 0.0 if half else NEGB]], np.float32), (128, 2)))}
        for n, a in groups.items():
            nl = a.shape[0] // NCORES
            m[f"w_{n}"] = a[c * nl:(c + 1) * nl]
        in_maps.append(m)
    res = run_bass_kernel_spmd(nc, in_maps, core_ids=list(range(NCORES)))
    out = np.zeros((B, 2 * T, cfg.D), np.float32)
    for c in range(NCORES):
        b, half = c // 2, c % 2
        o = res.results[c]["outT"].reshape(cfg.D, T)
        out[b, half * T:(half + 1) * T, :] = o.T
    return out


def kernel(**inputs):
    return run(Cfg(), inputs)
```

```python
import contextlib
import numpy as np
import concourse.bass as bass
import concourse.mybir as mybir
from concourse.bass_utils import run_bass_kernel_spmd

F32 = mybir.dt.float32
BF16 = mybir.dt.bfloat16
I32 = mybir.dt.int32
AF = mybir.ActivationFunctionType
ALU = mybir.AluOpType
AX = mybir.AxisListType
NCORES = 8
T = 1024
TH = 512
NEG = -1000.0
NEGB = -80.0


class Cfg:
    def __init__(self, D=4096, F=11008, NH=32, QL=1024, KL=512, NBLK=16):
        self.D, self.F, self.NH, self.QL, self.KL, self.NBLK = D, F, NH, QL, KL, NBLK
        self.ND = D // 128
        self.NF = F // 128
        base = self.NF // 3
        rem = self.NF - 3 * base
        self.PARTS = []
        o = 0
        for i in range(3):
            n = base + (1 if i < rem else 0)
            self.PARTS.append((o, n))
            o += n
        self.NFH = max(n for _, n in self.PARTS)
        self.KQ = QL // 128
        self.KK = KL // 128


class Buf:
    __slots__ = ("name", "last_w", "readers")

    def __init__(self, name=""):
        self.name = name
        self.last_w = None
        self.readers = []


class Op:
    __slots__ = ("eng", "fn", "idx", "dma", "deps", "signaled", "count", "dsem", "dval", "extra")

    def __init__(self, eng, fn, idx, dma):
        self.eng, self.fn, self.idx, self.dma = eng, fn, idx, dma
        self.deps = []
        self.signaled = False
        self.count = 0
        self.dsem = None
        self.dval = 0
        self.extra = []


ENGS = ("tensor", "vector", "scalar", "gpsimd", "sync")
NDSEM = 6


def _freeze(fn):
    import types
    if getattr(fn, "__closure__", None) is None:
        return fn
    cells = []
    for c in fn.__closure__:
        try:
            cells.append(types.CellType(c.cell_contents))
        except ValueError:
            cells.append(c)
    g = types.FunctionType(fn.__code__, fn.__globals__, fn.__name__, fn.__defaults__, tuple(cells))
    g.__kwdefaults__ = fn.__kwdefaults__
    return g


class Stage:
    def __init__(self, nc, name):
        self.nc = nc
        self.name = name
        self.ops = {e: [] for e in ENGS}
        self.ndma = {e: 0 for e in ENGS}

    def add(self, eng, fn, reads=(), writes=(), dma=False, extra=()):
        op = Op(eng, _freeze(fn), len(self.ops[eng]), dma)
        op.extra = list(extra)
        deps = {}

        def dep(o):
            if o is None or o is op:
                return
            if o.dma:
                deps[("d", id(o))] = o
            else:
                if o.eng == "tensor" and eng == "tensor" and not dma:
                    return
                k = ("c", o.eng)
                if k not in deps or deps[k].idx < o.idx:
                    deps[k] = o

        for b in reads:
            dep(b.last_w)
        for b in writes:
            dep(b.last_w)
            for r in b.readers:
                dep(r)
        op.deps = list(deps.values())
        for o in op.deps:
            o.signaled = True
        for b in reads:
            if not dma:
                b.readers = [r for r in b.readers if r.dma or r.eng != eng]
            b.readers.append(op)
        for b in writes:
            b.last_w = op
            b.readers = []
        if dma:
            n = self.ndma[eng]
            self.ndma[eng] = n + 1
            op.dsem = n % NDSEM
            op.dval = 16 * (n // NDSEM + 1)
        self.ops[eng].append(op)
        return op

    SEMS = None

    def emit(self):
        nc = self.nc
        with contextlib.ExitStack() as es:
            csem, dsem_all, extra_clear = Stage.SEMS
            dsem = {e: dsem_all[e] for e in ENGS if self.ndma[e] > 0}
            allsems = list(csem.values()) + [x for e in ENGS for x in dsem_all[e]] + list(extra_clear)
            extra_clear.clear()
            with nc.Block(f"{self.name}_clr") as cb:
                def clr(eng):
                    for sm_ in allsems:
                        eng.sem_clear(sm_)
                cb.vector(clr)
            for e in ENGS:
                c = 0
                for op in self.ops[e]:
                    if op.signaled and not op.dma:
                        c += 1
                        op.count = c
            block = es.enter_context(nc.Block(f"{self.name}"))

            def body(ename):
                def f(eng):
                    seen = {}

                    def wait(sem, val):
                        k = id(sem)
                        if seen.get(k, 0) >= val:
                            return
                        seen[k] = val
                        eng.wait_ge(sem, val)

                    for op in self.ops[ename]:
                        for (s, v) in op.extra:
                            wait(s, v)
                        for d in op.deps:
                            if d.dma:
                                wait(dsem[d.eng][d.dsem], d.dval)
                            else:
                                wait(csem[d.eng], d.count)
                        if op.dma:
                            if op.dval > 16:
                                wait(dsem[ename][op.dsem], op.dval - 16)
                            op.fn(eng).then_inc(dsem[ename][op.dsem], 16)
                        else:
                            ins = op.fn(eng)
                            if op.signaled:
                                ins.then_inc(csem[ename], 1)
                    if ename in dsem:
                        n = self.ndma[ename]
                        for i in range(NDSEM):
                            cnt = len(range(i, n, NDSEM))
                            if cnt:
                                wait(dsem[ename][i], 16 * cnt)
                return f

            for e in ENGS:
                if self.ops[e]:
                    getattr(block, e)(body(e))


def pack_cols(W, col_lists, wc):
    K, N = W.shape
    KT = K // 128
    nch = len(col_lists)
    idx = np.zeros((nch, wc), np.int64)
    valid = np.zeros((nch, wc), bool)
    for c, cl in enumerate(col_lists):
        idx[c, :len(cl)] = cl
        valid[c, :len(cl)] = True
    Wr = W.reshape(KT, 128, N)
    out = Wr[:, :, idx.reshape(-1)].reshape(KT, 128, nch, wc)
    if not valid.all():
        out = np.where(valid[None, None], out, np.float32(0))
    out = np.ascontiguousarray(out.transpose(2, 1, 0, 3)).reshape(nch, 128, KT * wc)
    npad = (-nch) % NCORES
    if npad:
        out = np.concatenate([out, np.zeros((npad, 128, KT * wc), np.float32)], 0)
    return out


def weight_groups(cfg, inp):
    D, F, NH = cfg.D, cfg.F, cfg.NH
    g = {}
    ar = np.arange

    def ffn(q):
        l, s = q // 2, q % 2
        Wgu = np.asarray(inp["ffn_w_gate_up"][l, s])
        Wdn = np.asarray(inp["ffn_w_down"][l, s])
        for hf, (p0, pn) in enumerate(cfg.PARTS):
            cl = []
            for j in range(pn):
                f0 = (p0 + j) * 128
                cl.append(ar(f0, f0 + 128))
                cl.append(ar(F + f0, F + f0 + 128))
            g[f"gu{q}{hf}"] = pack_cols(Wgu, cl, 128)
            r0 = p0 * 128
            g[f"dn{q}{hf}"] = pack_cols(Wdn[r0:r0 + pn * 128], [ar(i * 128, i * 128 + 128) for i in range(cfg.ND)], 128)

    ffn(0)
    g["win"] = pack_cols(np.asarray(inp["lru_w_in"][0]), [ar(i * 128, i * 128 + 128) for i in range(2 * cfg.ND)], 128)
    wa, wx = np.asarray(inp["lru_w_gate_a"][0]), np.asarray(inp["lru_w_gate_x"][0])
    ch = []
    for b in range(cfg.NBLK):
        for w in (wa, wx):
            for nt in range(2):
                ch.append(pack_cols(w[b], [ar(nt * 128, nt * 128 + 128)], 128)[0])
    gt = np.stack(ch)
    npad = (-len(ch)) % NCORES
    if npad:
        gt = np.concatenate([gt, np.zeros((npad,) + gt.shape[1:], np.float32)])
    g["gates"] = gt
    g["wout"] = pack_cols(np.asarray(inp["lru_w_out"][0]), [ar(i * 128, i * 128 + 128) for i in range(cfg.ND)], 128)
    ffn(1)
    ffn(2)
    KL = cfg.KL
    cl = [ar(i * 128, i * 128 + 128) for i in range(cfg.KK)]
    cl.append(ar(KL, KL + 64))
    cl.append(np.concatenate([ar(KL + 32, KL + 64), ar(KL, KL + 32)]))
    g["kvd"] = pack_cols(np.asarray(inp["kv_w_down"]), cl, 128)
    g["wdq"] = pack_cols(np.asarray(inp["mla_w_dq"][0]), [ar(i * 128, i * 128 + 128) for i in range(cfg.KQ)], 128)
    cl = []
    for h in range(NH):
        b0 = h * 192
        cl.append(ar(b0, b0 + 128))
        cl.append(ar(b0 + 128, b0 + 192))
        cl.append(np.concatenate([ar(b0 + 160, b0 + 192), ar(b0 + 128, b0 + 160)]))
    g["wuq"] = pack_cols(np.asarray(inp["mla_w_uq"][0]), cl, 128)
    cl = []
    for h in range(NH):
        cl.append(ar(h * 256, h * 256 + 128))
        cl.append(ar(h * 256 + 128, h * 256 + 256))
    g["kvu"] = pack_cols(np.asarray(inp["kv_w_up"]), cl, 128)
    g["wo"] = pack_cols(np.asarray(inp["mla_w_o"][0]), [ar(i * 128, i * 128 + 128) for i in range(cfg.ND)], 128)
    ffn(3)
    return g


def small_vectors(cfg, inp):
    cols = {}
    tabs = []

    def put(name, v):
        v = np.asarray(v, np.float32).reshape(-1)
        n = (len(v) + 127) // 128
        vv = np.zeros(n * 128, np.float32)
        vv[:len(v)] = v
        cols[name] = sum(t.shape[1] for t in tabs)
        tabs.append(vv.reshape(n, 128).T)

    ng = np.asarray(inp["norm_gains"])
    for l in range(2):
        for i in range(6):
            put(f"g{l}{i}", ng[l, i])
    for k in range(4):
        put(f"cw{k}", np.asarray(inp["lru_conv_w"])[0, k])
    put("cb", np.asarray(inp["lru_conv_b"])[0])
    put("bga", np.asarray(inp["lru_b_gate_a"])[0])
    put("bgx", np.asarray(inp["lru_b_gate_x"])[0])
    put("lam", np.asarray(inp["lru_lambda"])[0])
    put("kvn", inp["kv_norm_in"])
    put("kln", inp["kv_latent_norm"])
    put("qn", np.asarray(inp["mla_q_norm"])[0])
    return np.ascontiguousarray(np.concatenate(tabs, 1)), cols


def build(cfg, gshapes, vcols, NV):
    nc = bass.Bass("TRN2", target_bir_lowering=False)
    ND, NFH, NH, KQ, KK = cfg.ND, cfg.NFH, cfg.NH, cfg.KQ, cfg.KK
    D = cfg.D
    dt_ = nc.dram_tensor
    _uid = [0]

    def sbt(name, shape, dtype):
        _uid[0] += 1
        return nc.sbuf_tensor(f"{name}_{_uid[0]}", shape, dtype)

    def pst(name, shape, dtype):
        _uid[0] += 1
        return nc.psum_tensor(f"{name}_{_uid[0]}", shape, dtype)
    xT = dt_("xT", [ND, 128, T], F32, kind="ExternalInput")
    outT = dt_("outT", [ND, 128, T], F32, kind="ExternalOutput")
    vecs_d = dt_("vecs", [128, NV], F32, kind="ExternalInput")
    pos_d = dt_("pos", [64, T], I32, kind="ExternalInput")
    CSd = dt_("CSd", [64, T], F32)
    SNd = dt_("SNd", [64, T], F32)
    CQd = dt_("CQd", [cfg.KQ * 128, T], BF16)
    invf_d = dt_("invf", [64, 1], F32, kind="ExternalInput")
    flag_d = dt_("flag", [128, 2], F32, kind="ExternalInput")
    ident_d = dt_("ident", [128, 128], F32, kind="ExternalInput")
    mask_d = dt_("cmask", [128, 128], F32, kind="ExternalInput")
    wsh = {n: dt_(f"w_{n}", [s[0] // NCORES, 128, s[2]], F32, kind="ExternalInput") for n, s in gshapes.items()}
    CHMAX = 512 * 1024
    gn_ = list(gshapes.keys())
    ginfo = {}
    for n in gn_:
        nl = gshapes[n][0] // NCORES
        ce = 128 * gshapes[n][2]
        wpa = max(1, CHMAX // ce)
        ginfo[n] = (nl, ce, wpa)
    ib_t = {n: dt_(f"ib_{n}", [ginfo[n][0], ginfo[n][1]], BF16) for n in gn_}
    o4_t = {n: dt_(f"o4_{n}", [4 * ginfo[n][0], ginfo[n][1]], BF16) for n in gn_}
    o8_t = {n: dt_(f"o8_{n}", [8 * ginfo[n][0], ginfo[n][1]], BF16) for n in gn_}

    def ib_chunk(n, cl):
        nl, ce, wpa = ginfo[n]
        return ib_t[n][cl].rearrange("(p f) -> p f", p=128)

    def ag_chunks(n):
        nl, ce, wpa = ginfo[n]
        out = []
        a0 = 0
        while a0 < nl:
            w = min(wpa, nl - a0)
            out.append((a0, w))
            a0 += w
        return out

    def ob_chunk(n, c):
        nl, ce, wpa = ginfo[n]
        r, cl = c // nl, c % nl
        a = cl // wpa
        a0 = a * wpa
        w = min(wpa, nl - a0)
        z = w * ce
        q, h, i2 = r // 4, (r % 4) // 2, r % 2
        row = 8 * a0 + h * 4 * w + q * 2 * w + i2 * w + (cl - a0)
        return o8_t[n][row].rearrange("(p f) -> p f", p=128), 2 * a + h + 1
    Hd = dt_("Hd", [ND, 128, T], F32)
    Yd = dt_("Yd", [ND, 128, T], F32)
    GBd = dt_("GBd", [ND, 128, T], F32)
    XBd = dt_("XBd", [ND, 128, T], F32)
    Ad = dt_("Ad", [ND, 128, T], F32)
    Ud = dt_("Ud", [ND, 128, T], F32)
    halo_i = dt_("halo_i", [ND * 128, 4], F32)
    halo_o = dt_("halo_o", [2 * ND * 128, 4], F32)
    car_i = dt_("car_i", [ND * 128, 1], F32)
    car_o = dt_("car_o", [2 * ND * 128, 1], F32)
    KVR = KK * 128 + 64
    kv_i = dt_("kv_i", [KVR, T], BF16)
    kv_o = dt_("kv_o", [2 * KVR, T], BF16)
    PAIRS = [[0, 1], [2, 3], [4, 5], [6, 7]]
    gnames = list(gshapes.keys())

    with contextlib.ExitStack() as top:
        E = top.enter_context
        ag_sem = {n: E(nc.semaphore(f"ag_{n}")) for n in gnames}
        g_csem = {e: E(nc.semaphore(f"c_{e}")) for e in ENGS}
        g_dsem = {e: [E(nc.semaphore(f"d_{e}{i}")) for i in range(NDSEM)] for e in ENGS}
        x_sem = [E(nc.semaphore(f"xs{i}")) for i in range(4)]
        Stage.SEMS = (g_csem, g_dsem, list(ag_sem.values()) + x_sem)
        HN = E(sbt("HN", [128, ND, T], BF16))
        RSTD = E(sbt("RSTD", [128, T], F32))
        RSTDY = E(sbt("RSTDY", [128, T], F32))
        VEC = E(sbt("VEC", [128, NV], F32))
        ONES = E(sbt("ONES", [128, 128], F32))
        IDB = E(sbt("IDB", [128, 128], BF16))
        IDF = E(sbt("IDF", [128, 128], F32))
        CMASK = E(sbt("CMASK", [128, 128], F32))
        FLAG = E(sbt("FLAG", [128, 2], F32))
        CL = E(sbt("CL", [128, ND], F32))
        CL2 = E(sbt("CL2", [128, ND], F32))
        PS = [None] * 8

        PSB = {}

        def alloc_ps(es, n=8):
            PSB.clear()
            for i in range(8):
                PSB[i] = Buf()
            for i in range(n):
                PS[i] = es.enter_context(pst(f"PS{i}", [128, TH], F32))

        def vcol(name, i=0):
            c = vcols[name] + i
            return VEC[:, c:c + 1]

        QUADS = [[0, 1, 2, 3], [4, 5, 6, 7]]
        XP4 = [[0, 4], [1, 5], [2, 6], [3, 7]]
        GATHER_IN_PREP = [n for n in gnames if n in ("gu00", "dn00")]
        rest = [n for n in gnames if n not in GATHER_IN_PREP]
        GATHER_AT_FFN = {
            0: [n for n in rest if n.startswith("gu0") or n.startswith("dn0")]
               + [n for n in rest if n in ("win", "gates", "wout") or n.startswith("gu1") or n.startswith("dn1")],
            1: [n for n in rest if n.startswith("gu2") or n.startswith("dn2")],
            2: [n for n in rest if n in ("kvd", "wdq", "wuq", "kvu", "wo") or n.startswith("gu3") or n.startswith("dn3")],
            3: [],
        }
        assert sorted(GATHER_IN_PREP + sum(GATHER_AT_FFN.values(), [])) == sorted(gnames)

        def issue_gather(st, n, deps):
            first_cc = True
            for (a0, w) in ag_chunks(n):
                def cc1(e, n=n, a0=a0, w=w):
                    ins = e.collective_compute("AllGather", ALU.bypass, replica_groups=QUADS,
                                               ins=[ib_t[n][a0:a0 + w].opt()], outs=[o4_t[n][4 * a0:4 * a0 + 4 * w].opt()])
                    ins.then_inc(x_sem[3], 1)
                    return ins
                st.add("gpsimd", cc1, reads=deps if first_cc else [])
                first_cc = False
                for h in range(2):
                    def cc2(e, n=n, a0=a0, w=w, h=h):
                        ins = e.collective_compute("AllGather", ALU.bypass, replica_groups=XP4,
                                                   ins=[o4_t[n][4 * a0 + h * 2 * w:4 * a0 + (h + 1) * 2 * w].opt()],
                                                   outs=[o8_t[n][8 * a0 + h * 4 * w:8 * a0 + (h + 1) * 4 * w].opt()])
                        ins.then_inc(ag_sem[n], 1)
                        return ins
                    st.add("gpsimd", cc2)

        def stage_prep():
            st = Stage(nc, "prep")
            with contextlib.ExitStack() as es:
                FM = max(s[2] for s in gshapes.values())
                stg = [es.enter_context(sbt(f"stg{i}", [128, FM], F32)) for i in range(2)]
                stb = [es.enter_context(sbt(f"stb{i}", [128, FM], BF16)) for i in range(2)]
                tmp = es.enter_context(sbt("ptmp", [128, 2 * ND], F32))
                ang = es.enter_context(sbt("ang", [64, T], F32))
                kf = es.enter_context(sbt("kf", [64, T], F32))
                ki = es.enter_context(sbt("ki", [64, T], I32))
                invf = es.enter_context(sbt("invf_s", [64, 1], F32))
                CS = es.enter_context(sbt("CSp", [64, T], F32))
                SN = es.enter_context(sbt("SNp", [64, T], F32))
                rr = es.enter_context(sbt("rrp", [64, T], F32))
                bstg = [Buf() for _ in range(2)]
                bstb = [Buf() for _ in range(2)]
                bmisc = Buf()
                bib = {n: Buf() for n in gnames}
                st.add("scalar", lambda e: e.dma_start(out=VEC[:], in_=vecs_d[:, :]), writes=[bmisc], dma=True)
                st.add("scalar", lambda e: e.dma_start(out=IDF[:], in_=ident_d[:, :]), writes=[bmisc], dma=True)
                st.add("scalar", lambda e: e.dma_start(out=CMASK[:], in_=mask_d[:, :]), writes=[bmisc], dma=True)
                st.add("scalar", lambda e: e.dma_start(out=FLAG[:], in_=flag_d[:, :]), writes=[bmisc], dma=True)
                st.add("scalar", lambda e: e.dma_start(out=ki[:], in_=pos_d[:, :]), writes=[bmisc], dma=True)
                st.add("scalar", lambda e: e.activation(out=ang[:], in_=ki[:], func=AF.Copy), reads=[bmisc], writes=[bmisc])
                st.add("scalar", lambda e: e.dma_start(out=invf[:], in_=invf_d[:, :]), writes=[bmisc], dma=True)
                st.add("vector", lambda e: e.memset(ONES[:], 1.0), writes=[bmisc])
                st.add("vector", lambda e: e.tensor_copy(out=IDB[:], in_=IDF[:]), reads=[bmisc], writes=[bmisc])
                lam = VEC[:, vcols["lam"]:vcols["lam"] + ND]
                st.add("scalar", lambda e: e.activation(out=tmp[:, 0:ND], in_=lam, func=AF.Exp, scale=-1.0), reads=[bmisc], writes=[bmisc])
                st.add("scalar", lambda e: e.activation(out=tmp[:, ND:2 * ND], in_=tmp[:, 0:ND], func=AF.Ln, bias=1.0), reads=[bmisc], writes=[bmisc])
                st.add("vector", lambda e: e.tensor_scalar(out=CL[:], in0=tmp[:, ND:2 * ND], scalar1=-8.0, scalar2=None, op0=ALU.mult), reads=[bmisc], writes=[bmisc])
                st.add("vector", lambda e: e.tensor_scalar(out=CL2[:], in0=tmp[:, ND:2 * ND], scalar1=-16.0, scalar2=None, op0=ALU.mult), reads=[bmisc], writes=[bmisc])
                TWO_PI = 2.0 * np.pi
                c1 = float(np.float32(6.28125))
                c2 = float(np.float32(TWO_PI - 6.28125))
                c3 = float(TWO_PI - 6.28125 - float(np.float32(TWO_PI - 6.28125)))
                st.add("vector", lambda e: e.tensor_scalar(out=ang[:], in0=ang[:], scalar1=invf[:, 0:1], scalar2=None, op0=ALU.mult), reads=[bmisc], writes=[bmisc])
                st.add("vector", lambda e: e.tensor_scalar(out=kf[:], in0=ang[:], scalar1=float(1.0 / TWO_PI), scalar2=None, op0=ALU.mult), reads=[bmisc], writes=[bmisc])
                MAGIC = 12582912.0
                st.add("vector", lambda e: e.tensor_scalar(out=kf[:], in0=kf[:], scalar1=MAGIC, scalar2=None, op0=ALU.add), reads=[bmisc], writes=[bmisc])
                st.add("vector", lambda e: e.tensor_scalar(out=kf[:], in0=kf[:], scalar1=-MAGIC, scalar2=None, op0=ALU.add), reads=[bmisc], writes=[bmisc])
                for cc_ in (c1, c2, c3):
                    st.add("vector", lambda e, cc_=cc_: e.tensor_scalar(out=rr[:], in0=kf[:], scalar1=float(-cc_), scalar2=None, op0=ALU.mult), reads=[bmisc], writes=[bmisc])
                    st.add("vector", lambda e: e.tensor_tensor(out=ang[:], in0=ang[:], in1=rr[:], op=ALU.add), reads=[bmisc], writes=[bmisc])
                PI = float(np.pi)

                def wrap(dst, src, shift):
                    st.add("vector", lambda e: e.tensor_scalar(out=dst[:], in0=src[:], scalar1=float(shift), scalar2=None, op0=ALU.add), reads=[bmisc], writes=[bmisc])
                    st.add("vector", lambda e: e.tensor_scalar(out=kf[:], in0=dst[:], scalar1=PI, scalar2=-2 * PI, op0=ALU.is_gt, op1=ALU.mult), reads=[bmisc], writes=[bmisc])
                    st.add("vector", lambda e: e.tensor_tensor(out=dst[:], in0=dst[:], in1=kf[:], op=ALU.add), reads=[bmisc], writes=[bmisc])
                    st.add("vector", lambda e: e.tensor_scalar(out=kf[:], in0=dst[:], scalar1=-PI, scalar2=2 * PI, op0=ALU.is_lt, op1=ALU.mult), reads=[bmisc], writes=[bmisc])
                    st.add("vector", lambda e: e.tensor_tensor(out=dst[:], in0=dst[:], in1=kf[:], op=ALU.add), reads=[bmisc], writes=[bmisc])
                    st.add("vector", lambda e: e.tensor_scalar(out=dst[:], in0=dst[:], scalar1=3.1415925, scalar2=-3.1415925, op0=ALU.min, op1=ALU.max), reads=[bmisc], writes=[bmisc])

                CL_ = 3.1415925
                st.add("vector", lambda e: e.tensor_scalar(out=rr[:], in0=ang[:], scalar1=CL_, scalar2=-CL_, op0=ALU.min, op1=ALU.max), reads=[bmisc], writes=[bmisc])
                st.add("scalar", lambda e: e.activation(out=SN[:], in_=rr[:], func=AF.Sin), reads=[bmisc], writes=[bmisc])
                st.add("vector", lambda e: e.tensor_scalar(out=kf[:], in0=rr[:], scalar1=-1.0, scalar2=None, op0=ALU.mult), reads=[bmisc], writes=[bmisc])
                st.add("vector", lambda e: e.tensor_tensor(out=kf[:], in0=kf[:], in1=rr[:], op=ALU.max), reads=[bmisc], writes=[bmisc])
                st.add("vector", lambda e: e.tensor_scalar(out=kf[:], in0=kf[:], scalar1=-1.0, scalar2=PI / 2, op0=ALU.mult, op1=ALU.add), reads=[bmisc], writes=[bmisc])
                st.add("scalar", lambda e: e.activation(out=CS[:], in_=kf[:], func=AF.Sin), reads=[bmisc], writes=[bmisc])
                st.add("vector", lambda e: e.tensor_scalar(out=SN[0:32, :], in0=SN[0:32, :], scalar1=-1.0, scalar2=None, op0=ALU.mult), reads=[bmisc], writes=[bmisc])
                st.add("scalar", lambda e: e.dma_start(out=CSd[:, :], in_=CS[:]), reads=[bmisc], writes=[Buf()], dma=True)
                st.add("scalar", lambda e: e.dma_start(out=SNd[:, :], in_=SN[:]), reads=[bmisc], writes=[Buf()], dma=True)
                k = 0
                for n in gnames:
                    nl = gshapes[n][0] // NCORES
                    Fc = gshapes[n][2]
                    for c in range(nl):
                        s = k % 2
                        st.add("sync", lambda e, n=n, c=c, s=s, Fc=Fc: e.dma_start(out=stg[s][:, 0:Fc], in_=wsh[n][c]), writes=[bstg[s]], dma=True)
                        ce = ("vector", "scalar", "gpsimd")[k % 3]
                        if ce == "scalar":
                            st.add(ce, lambda e, s=s, Fc=Fc: e.activation(out=stb[s][:, 0:Fc], in_=stg[s][:, 0:Fc], func=AF.Copy), reads=[bstg[s]], writes=[bstb[s]])
                        else:
                            st.add(ce, lambda e, s=s, Fc=Fc: e.tensor_copy(out=stb[s][:, 0:Fc], in_=stg[s][:, 0:Fc]), reads=[bstg[s]], writes=[bstb[s]])
                        st.add("sync", lambda e, n=n, c=c, s=s, Fc=Fc: e.dma_start(out=ib_chunk(n, c), in_=stb[s][:, 0:Fc]), reads=[bstb[s]], writes=[bib[n]], dma=True)
                        k += 1
                    if n in GATHER_IN_PREP:
                        issue_gather(st, n, [bib[n]])
                import os
                if os.environ.get("KWAITAG"):
                    st.add("gpsimd", lambda e: e.memset(tmp[:, 0:1], 0.0), extra=[(ag_sem[n], 2 * len(ag_chunks(n))) for n in gnames])
                st.emit()

        class Ring:
            def __init__(self, es, name, n, width, dtype=BF16):
                self.t = [es.enter_context(sbt(f"{name}{i}", [128, width], dtype)) for i in range(n)]
                self.b = [Buf() for _ in range(n)]
                self.k = 0

            def nxt(self):
                i = self.k % len(self.t)
                self.k += 1
                return self.t[i], self.b[i]

        class Banks:
            def __init__(self, ids):
                self.ids = ids
                self.b = {i: PSB[i] for i in ids}
                self.k = 0

            def nxt(self):
                i = self.ids[self.k % len(self.ids)]
                self.k += 1
                return PS[i], self.b[i]

        def load_w(st, name, c, wt, wb, ncols, first):
            src, need = ob_chunk(name, c)
            st.add("sync", lambda e: e.dma_start(out=wt[:, 0:ncols], in_=src), writes=[wb], dma=True, extra=[(ag_sem[name], need)])

        def linear(st, name, chunks, KT, rhs, rhs_bufs, ring, banks, evac, M=128, nth=2, first_flag=[True]):
            Fc = gshapes[name][2]
            wcw = Fc // KT
            for i, c in enumerate(chunks):
                wt, wb = ring.nxt()
                load_w(st, name, c, wt, wb, Fc, i == 0)
                pss = [banks.nxt() for _ in range(nth)]
                for kt in range(KT):
                    for th in range(nth):
                        ps, pb = pss[th]
                        r_ap = rhs(kt, th)
                        st.add("tensor", lambda e, ps=ps, wt=wt, kt=kt, th=th, r_ap=r_ap: e.matmul(ps[0:M, :], wt[:, kt * wcw:kt * wcw + M], r_ap, start=(kt == 0), stop=(kt == KT - 1)),
                               reads=[wb] + rhs_bufs(kt, th), writes=[pb])
                for th in range(nth):
                    evac(i, c, th, pss[th][0], pss[th][1])

        def rstd_from(st, ps, pb, out_ap, outbuf, nfeat, tmp, tb, mul=1.0):
            st.add("scalar", lambda e: e.activation(out=tmp, in_=ps, func=AF.Sqrt, scale=1.0 / nfeat, bias=1e-6), reads=[pb], writes=[tb])
            st.add("vector", lambda e: e.reciprocal(out=out_ap, in_=tmp), reads=[tb], writes=[outbuf])
            if mul != 1.0:
                st.add("vector", lambda e: e.tensor_scalar(out=out_ap, in0=out_ap, scalar1=float(mul), scalar2=None, op0=ALU.mult), reads=[outbuf], writes=[outbuf])

        def stage_epi(name, src, ysrc, gcoef, gout, hdst, ymul_applied=True):
            st = Stage(nc, name)
            with contextlib.ExitStack() as es:
                alloc_ps(es)
                HNEW = es.enter_context(sbt("HNEW", [128, ND, TH], F32))
                ystg = Ring(es, "ystg", 3, TH, F32)
                sq = Ring(es, "sq", 2, TH, F32)
                rt = es.enter_context(sbt("rt", [128, TH], F32))
                rtb = Buf()
                bh = [Buf() for _ in range(ND)]
                bhn = Buf()
                brs = Buf()
                bd = Buf()
                banks = Banks([0, 1])
                for th in range(2):
                    tsl = slice(th * TH, (th + 1) * TH)
                    ps, pb = banks.nxt()
                    for dt in range(ND):
                        st.add("sync", lambda e, dt=dt, tsl=tsl: e.dma_start(out=HNEW[:, dt, :], in_=src[dt][:, tsl]), writes=[bh[dt]], dma=True)
                        if ysrc is not None:
                            yt, yb = ystg.nxt()
                            st.add("scalar", lambda e, dt=dt, tsl=tsl, yt=yt: e.dma_start(out=yt[:], in_=ysrc[dt][:, tsl]), writes=[yb], dma=True)
                            st.add("gpsimd", lambda e, yt=yt, tsl=tsl: e.tensor_tensor(out=yt[:], in0=yt[:], in1=RSTDY[:, tsl], op=ALU.mult), reads=[yb], writes=[yb])
                            st.add("vector", lambda e, dt=dt, yt=yt: e.scalar_tensor_tensor(out=HNEW[:, dt, :], in0=yt[:], scalar=vcol(gcoef, dt), in1=HNEW[:, dt, :], op0=ALU.mult, op1=ALU.add),
                                   reads=[yb, bh[dt]], writes=[bh[dt]])
                        if hdst is not None and (ysrc is not None or hdst is not src):
                            st.add("scalar", lambda e, dt=dt, tsl=tsl: e.dma_start(out=hdst[dt][:, tsl], in_=HNEW[:, dt, :]), reads=[bh[dt]], writes=[bd], dma=True)
                        if gout is not None:
                            qt, qb = sq.nxt()
                            st.add("scalar", lambda e, dt=dt, qt=qt: e.activation(out=qt[:], in_=HNEW[:, dt, :], func=AF.Square), reads=[bh[dt]], writes=[qb])
                            st.add("tensor", lambda e, dt=dt, qt=qt, ps=ps: e.matmul(ps[:, :], ONES[:, :], qt[:], start=(dt == 0), stop=(dt == ND - 1)), reads=[qb], writes=[pb])
                    if gout is not None:
                        rstd_from(st, ps[:, :], pb, RSTD[:, tsl], brs, float(D), rt[:], rtb)
                        for dt in range(ND):
                            st.add("vector", lambda e, dt=dt, tsl=tsl: e.scalar_tensor_tensor(out=HN[:, dt, tsl], in0=HNEW[:, dt, :], scalar=vcol(gout, dt), in1=RSTD[:, tsl], op0=ALU.mult, op1=ALU.mult),
                                   reads=[bh[dt], brs], writes=[bhn])
                st.emit()

        def stage_renorm(name, gout):
            st = Stage(nc, name)
            with contextlib.ExitStack() as es:
                hs = Ring(es, "hs", 3, T, F32)
                bhn = Buf()
                for dt in range(ND):
                    ht, hb = hs.nxt()
                    st.add("sync", lambda e, dt=dt, ht=ht: e.dma_start(out=ht[:], in_=Hd[dt]), writes=[hb], dma=True)
                    st.add("vector", lambda e, dt=dt, ht=ht: e.scalar_tensor_tensor(out=HN[:, dt, :], in0=ht[:], scalar=vcol(gout, dt), in1=RSTD[:, :], op0=ALU.mult, op1=ALU.mult),
                           reads=[hb], writes=[bhn])
                st.emit()

        class YOut:
            def __init__(self, st, es, mul):
                self.st = st
                self.ystg = Ring(es, "yo", 3, TH, F32)
                self.sq = Ring(es, "ysq", 2, TH, F32)
                self.ssb = Banks([6, 7])
                self.ss = [self.ssb.nxt() for _ in range(2)]
                self.rt = es.enter_context(sbt("yrt", [128, TH], F32))
                self.rtb = Buf()
                self.mul = mul
                self.bd = Buf()
                self.brs = Buf()

            def evac(self, i, dt, th, ps, pb, ntot, pre=None):
                st = self.st
                tsl = slice(th * TH, (th + 1) * TH)
                yt, yb = self.ystg.nxt()
                if pre is None:
                    st.add("vector", lambda e: e.tensor_copy(out=yt[:], in_=ps[:, :]), reads=[pb], writes=[yb])
                else:
                    pt, ptb = pre
                    st.add("vector", lambda e: e.tensor_tensor(out=yt[:], in0=ps[:, :], in1=pt[:], op=ALU.add), reads=[pb, ptb], writes=[yb])
                st.add("scalar", lambda e: e.dma_start(out=Yd[dt][:, tsl], in_=yt[:]), reads=[yb], writes=[self.bd], dma=True)
                qt, qb = self.sq.nxt()
                st.add("scalar", lambda e: e.activation(out=qt[:], in_=yt[:], func=AF.Square), reads=[yb], writes=[qb])
                sps, spb = self.ss[th]
                st.add("tensor", lambda e: e.matmul(sps[:, :], ONES[:, :], qt[:], start=(i == 0), stop=(i == ntot - 1)), reads=[qb], writes=[spb])
                if i == ntot - 1:
                    rstd_from(st, sps[:, :], spb, RSTDY[:, tsl], self.brs, float(D), self.rt[:], self.rtb, mul=self.mul)

        def stage_ffn(q):
            st = Stage(nc, f"ffn{q}")
            for n_ in GATHER_AT_FFN[q]:
                issue_gather(st, n_, [])
            with contextlib.ExitStack() as es:
                alloc_ps(es)
                ACT = es.enter_context(sbt("ACT", [128, NFH, T], BF16))
                ring = Ring(es, "wr", 3, max(ND, NFH) * 128)
                sg = Ring(es, "sg", 3, TH, F32)
                yo = YOut(st, es, 0.5)
                yp = Ring(es, "yp", 3, TH, F32)
                bact = [Buf() for _ in range(NFH)]
                bhn = Buf()
                bdy = [[Buf() for _ in range(2)] for _ in range(ND)]
                NP = len(cfg.PARTS)
                for hf, (p0, pn) in enumerate(cfg.PARTS):
                    banks = Banks([0, 1, 2, 3, 4, 5])
                    sgs = {}

                    def evac_gu(i, c, th, ps, pb):
                        j = i // 2
                        if i % 2 == 0:
                            t_, b_ = sg.nxt()
                            sgs[(j, th)] = (t_, b_)
                            st.add("scalar", lambda e: e.activation(out=t_[:], in_=ps[:, :], func=AF.Silu), reads=[pb], writes=[b_])
                        else:
                            t_, b_ = sgs.pop((j, th))
                            st.add("vector", lambda e: e.tensor_tensor(out=ACT[:, j, th * TH:(th + 1) * TH], in0=ps[:, :], in1=t_[:], op=ALU.mult), reads=[pb, b_], writes=[bact[j]])

                    linear(st, f"gu{q}{hf}", list(range(2 * pn)), ND, lambda kt, th: HN[:, kt, th * TH:(th + 1) * TH], lambda kt, th: [bhn], ring, banks, evac_gu)

                    def evac_dn(i, c, th, ps, pb):
                        tsl = slice(th * TH, (th + 1) * TH)
                        if hf == 0:
                            yt, yb = yo.ystg.nxt()
                            st.add("vector", lambda e: e.tensor_copy(out=yt[:], in_=ps[:, :]), reads=[pb], writes=[yb])
                            st.add("scalar", lambda e: e.dma_start(out=Yd[c][:, tsl], in_=yt[:]), reads=[yb], writes=[bdy[c][th]], dma=True)
                        elif hf < NP - 1:
                            pt, ptb = yp.nxt()
                            st.add("scalar", lambda e: e.dma_start(out=pt[:], in_=Yd[c][:, tsl]), reads=[bdy[c][th]], writes=[ptb], dma=True)
                            yt, yb = yo.ystg.nxt()
                            st.add("vector", lambda e: e.tensor_tensor(out=yt[:], in0=ps[:, :], in1=pt[:], op=ALU.add), reads=[pb, ptb], writes=[yb])
                            st.add("scalar", lambda e: e.dma_start(out=Yd[c][:, tsl], in_=yt[:]), reads=[yb], writes=[bdy[c][th]], dma=True)
                        else:
                            pt, ptb = yp.nxt()
                            st.add("scalar", lambda e: e.dma_start(out=pt[:], in_=Yd[c][:, tsl]), reads=[bdy[c][th]], writes=[ptb], dma=True)
                            yo.evac(i, c, th, ps, pb, ND, pre=(pt, ptb))

                    import os as _o3
                    _dbg = int(_o3.environ.get("KFFNDBG", "9"))
                    if _dbg == 1 or (_dbg == 2 and hf > 0):
                        break
                    linear(st, f"dn{q}{hf}", list(range(ND)), pn, lambda kt, th: ACT[:, kt, th * TH:(th + 1) * TH], lambda kt, th: [bact[kt]], ring, Banks([0, 1, 2, 3]), evac_dn)
                st.emit()

        def exchange(st, src_writes, ib, ob, sem, after_bufs):
            def cc(e):
                ins = e.collective_compute("AllGather", ALU.bypass, replica_groups=PAIRS, ins=[ib.ap().opt()], outs=[ob.ap().opt()])
                ins.then_inc(sem)
                return ins
            st.add("gpsimd", cc, reads=after_bufs)

        def stage_lru_a():
            st = Stage(nc, "lruA")
            with contextlib.ExitStack() as es:
                ring = Ring(es, "wr", 3, ND * 128)
                alloc_ps(es)
                og = Ring(es, "og", 4, TH, F32)
                HAL = es.enter_context(sbt("HAL", [128, ND, 4], F32))
                bhal = Buf()
                bhn = Buf()
                bd = Buf()
                banks = Banks([0, 1, 2, 3, 4, 5])
                st.add("vector", lambda e: e.memset(HAL[:], 0.0), writes=[bhal])

                def evac(i, c, th, ps, pb):
                    tsl = slice(th * TH, (th + 1) * TH)
                    t_, b_ = og.nxt()
                    if c < ND:
                        st.add("scalar", lambda e: e.activation(out=t_[:], in_=ps[:, :], func=AF.Gelu_apprx_tanh), reads=[pb], writes=[b_])
                        st.add("scalar", lambda e: e.dma_start(out=GBd[c][:, tsl], in_=t_[:]), reads=[b_], writes=[bd], dma=True)
                    else:
                        ct = c - ND
                        st.add("vector", lambda e: e.tensor_copy(out=t_[:], in_=ps[:, :]), reads=[pb], writes=[b_])
                        st.add("scalar", lambda e: e.dma_start(out=XBd[ct][:, tsl], in_=t_[:]), reads=[b_], writes=[bd], dma=True)
                        if th == 1:
                            st.add("vector", lambda e: e.tensor_copy(out=HAL[:, ct, 0:3], in_=t_[:, TH - 3:TH]), reads=[b_], writes=[bhal])

                linear(st, "win", list(range(2 * ND)), ND, lambda kt, th: HN[:, kt, th * TH:(th + 1) * TH], lambda kt, th: [bhn], ring, banks, evac)
                bx = Buf()
                st.add("scalar", lambda e: e.dma_start(out=halo_i.ap().rearrange("(c p) k -> p c k", p=128), in_=HAL[:]), reads=[bhal], writes=[bx], dma=True)
                exchange(st, None, halo_i, halo_o, x_sem[0], [bx])
                st.emit()

        def stage_lru_b():
            st = Stage(nc, "lruB")
            NBLK = cfg.NBLK
            with contextlib.ExitStack() as es:
                ring = Ring(es, "wr", 4, 256)
                alloc_ps(es)
                XP = Ring(es, "xp", 4, T + 4, F32)
                XC = Ring(es, "xc", 4, T, F32)
                XCB = Ring(es, "xcb", 4, T, BF16)
                RR = Ring(es, "rr", 3, T, F32)
                II = Ring(es, "ii", 3, T, F32)
                AA = Ring(es, "aa", 3, T, F32)
                MM = Ring(es, "mm", 3, T, F32)
                UU = Ring(es, "uu", 3, T, F32)
                HH = Ring(es, "hh", 2, T, F32)
                HAL = es.enter_context(sbt("HALp", [128, ND, 4], F32))
                FIN = es.enter_context(sbt("FIN", [128, ND, 4], F32))
                bhal = Buf()
                bfin = Buf()
                bd = Buf()
                banks = Banks([0, 1, 2, 3, 4, 5, 6, 7])
                st.add("scalar", lambda e: e.dma_start(out=HAL[:], in_=halo_o.ap()[0:ND * 128, :].rearrange("(c p) k -> p c k", p=128)), writes=[bhal], dma=True, extra=[(x_sem[0], 1)])
                st.add("vector", lambda e: e.tensor_scalar(out=HAL[:], in0=HAL[:], scalar1=FLAG[:, 0:1], scalar2=None, op0=ALU.mult), reads=[bhal], writes=[bhal])
                st.add("vector", lambda e: e.memset(FIN[:], 0.0), writes=[bfin])
                first = [True]
                import os as _o4
                KLB = int(_o4.environ.get("KLB", "9"))
                for blk in range(NBLK if KLB > 1 else 0):
                    xcb = []
                    xcs = []
                    for s in range(2):
                        ct = blk * 2 + s
                        xp, xpb = XP.nxt()
                        st.add("sync", lambda e, xp=xp, ct=ct: e.dma_start(out=xp[:, 4:T + 4], in_=XBd[ct]), writes=[xpb], dma=True)
                        st.add("vector", lambda e, xp=xp, ct=ct: e.tensor_copy(out=xp[:, 1:4], in_=HAL[:, ct, 0:3]), reads=[bhal], writes=[xpb])
                        xc, xcbuf = XC.nxt()
                        st.add("gpsimd", lambda e, xp=xp, xc=xc, ct=ct: e.tensor_scalar(out=xc[:], in0=xp[:, 1:T + 1], scalar1=vcol("cw0", ct), scalar2=vcol("cb", ct), op0=ALU.mult, op1=ALU.add), reads=[xpb], writes=[xcbuf])
                        for k in range(1, 4):
                            st.add("vector", lambda e, xp=xp, xc=xc, ct=ct, k=k: e.scalar_tensor_tensor(out=xc[:], in0=xp[:, 1 + k:T + 1 + k], scalar=vcol(f"cw{k}", ct), in1=xc[:], op0=ALU.mult, op1=ALU.add), reads=[xpb, xcbuf], writes=[xcbuf])
                        xb_, xbb = XCB.nxt()
                        st.add("scalar", lambda e, xb_=xb_, xc=xc: e.activation(out=xb_[:], in_=xc[:], func=AF.Copy), reads=[xcbuf], writes=[xbb])
                        xcb.append((xb_, xbb))
                        xcs.append((xc, xcbuf))
                    res = {}
                    if KLB == 2:
                        continue

                    def evac(i, c, th, ps, pb, blk=blk):
                        gi = (c // 2) % 2
                        nt = c % 2
                        ct = blk * 2 + nt
                        key = (gi, nt)
                        if key not in res:
                            res[key] = (RR if gi == 0 else II).nxt()
                        t_, b_ = res[key]
                        st.add("vector", lambda e: e.tensor_scalar(out=t_[:, th * TH:(th + 1) * TH], in0=ps[:, :], scalar1=vcol("bga" if gi == 0 else "bgx", ct), scalar2=None, op0=ALU.add), reads=[pb], writes=[b_])
                        st.add("scalar", lambda e: e.activation(out=t_[:, th * TH:(th + 1) * TH], in_=t_[:, th * TH:(th + 1) * TH], func=AF.Sigmoid), reads=[b_], writes=[b_])

                    linear(st, "gates", [blk * 4 + i for i in range(4)], 2, lambda kt, th: xcb[kt][0][:, th * TH:(th + 1) * TH], lambda kt, th: [xcb[kt][1]], ring, banks, evac, first_flag=first)
                    if KLB == 3:
                        continue
                    for nt in range(2):
                        ct = blk * 2 + nt
                        r_, rb = res[(0, nt)]
                        i_, ib_ = res[(1, nt)]
                        xc, xcbuf = xcs[nt]
                        a_, ab = AA.nxt()
                        m_, mb = MM.nxt()
                        u_, ub = UU.nxt()
                        h_, hb = HH.nxt()
                        st.add("vector", lambda e, m_=m_, r_=r_, ct=ct: e.tensor_scalar(out=m_[:], in0=r_[:], scalar1=CL[:, ct:ct + 1], scalar2=None, op0=ALU.mult), reads=[rb], writes=[mb])
                        st.add("scalar", lambda e, a_=a_, m_=m_: e.activation(out=a_[:], in_=m_[:], func=AF.Exp), reads=[mb], writes=[ab])
                        st.add("scalar", lambda e, m_=m_: e.activation(out=m_[:], in_=m_[:], func=AF.Exp, scale=2.0), reads=[mb], writes=[mb])
                        st.add("scalar", lambda e, m_=m_: e.activation(out=m_[:], in_=m_[:], func=AF.Sqrt, scale=-1.0, bias=1.0), reads=[mb], writes=[mb])
                        st.add("gpsimd", lambda e, u_=u_, i_=i_, xc=xc: e.tensor_tensor(out=u_[:], in0=i_[:], in1=xc[:], op=ALU.mult), reads=[ib_, xcbuf], writes=[ub])
                        st.add("vector", lambda e, u_=u_, m_=m_: e.tensor_tensor(out=u_[:], in0=u_[:], in1=m_[:], op=ALU.mult), reads=[ub, mb], writes=[ub])
                        st.add("scalar", lambda e, a_=a_, ct=ct: e.dma_start(out=Ad[ct], in_=a_[:]), reads=[ab], writes=[bd], dma=True)
                        st.add("scalar", lambda e, u_=u_, ct=ct: e.dma_start(out=Ud[ct], in_=u_[:]), reads=[ub], writes=[bd], dma=True)
                        if KLB == 4:
                            continue
                        st.add("vector", lambda e, h_=h_, a_=a_, u_=u_: e.tensor_tensor_scan(out=h_[:, 0:TH], data0=a_[:, 0:TH], data1=u_[:, 0:TH], initial=0.0, op0=ALU.mult, op1=ALU.add), reads=[ab, ub], writes=[hb])
                        st.add("vector", lambda e, h_=h_, a_=a_, u_=u_: e.tensor_tensor_scan(out=h_[:, TH:T], data0=a_[:, TH:T], data1=u_[:, TH:T], initial=h_[:, TH - 1:TH], op0=ALU.mult, op1=ALU.add), reads=[ab, ub, hb], writes=[hb])
                        st.add("scalar", lambda e, h_=h_, ct=ct: e.dma_start(out=car_i.ap().rearrange("(c p) k -> p c k", p=128)[:, ct, 0:1], in_=h_[:, T - 1:T], allow_slow_non_contiguous=True), reads=[hb], writes=[bfin], dma=True)
                exchange(st, None, car_i, car_o, x_sem[1], [bfin])
                st.emit()

        def stage_lru_c():
            st = Stage(nc, "lruC")
            with contextlib.ExitStack() as es:
                AA = Ring(es, "aa", 3, T, F32)
                UU = Ring(es, "uu", 3, T, F32)
                GG = Ring(es, "gg", 3, T, F32)
                HH = Ring(es, "hh", 3, T, F32)
                alloc_ps(es)
                CAR = es.enter_context(sbt("CAR", [128, ND, 1], F32))
                ring = Ring(es, "wr", 3, ND * 128)
                yo = YOut(st, es, 1.0)
                bcar = Buf()
                byb = [Buf() for _ in range(ND)]
                st.add("scalar", lambda e: e.dma_start(out=CAR[:], in_=car_o.ap()[0:ND * 128, :].rearrange("(c p) k -> p c k", p=128), allow_slow_non_contiguous=True), writes=[bcar], dma=True, extra=[(x_sem[1], 1)])
                st.add("vector", lambda e: e.tensor_scalar(out=CAR[:], in0=CAR[:], scalar1=FLAG[:, 0:1], scalar2=None, op0=ALU.mult), reads=[bcar], writes=[bcar])
                for ct in range(ND):
                    a_, ab = AA.nxt()
                    u_, ub = UU.nxt()
                    g_, gb = GG.nxt()
                    h_, hb = HH.nxt()
                    st.add("sync", lambda e, a_=a_, ct=ct: e.dma_start(out=a_[:], in_=Ad[ct]), writes=[ab], dma=True)
                    st.add("sync", lambda e, u_=u_, ct=ct: e.dma_start(out=u_[:], in_=Ud[ct]), writes=[ub], dma=True)
                    st.add("sync", lambda e, g_=g_, ct=ct: e.dma_start(out=g_[:], in_=GBd[ct]), writes=[gb], dma=True)
                    st.add("vector", lambda e, h_=h_, a_=a_, u_=u_, ct=ct: e.tensor_tensor_scan(out=h_[:, 0:TH], data0=a_[:, 0:TH], data1=u_[:, 0:TH], initial=CAR[:, ct, 0:1], op0=ALU.mult, op1=ALU.add), reads=[ab, ub, bcar], writes=[hb])
                    st.add("vector", lambda e, h_=h_, a_=a_, u_=u_: e.tensor_tensor_scan(out=h_[:, TH:T], data0=a_[:, TH:T], data1=u_[:, TH:T], initial=h_[:, TH - 1:TH], op0=ALU.mult, op1=ALU.add), reads=[ab, ub, hb], writes=[hb])
                    st.add("gpsimd", lambda e, h_=h_, g_=g_, ct=ct: e.tensor_tensor(out=HN[:, ct, :], in0=h_[:], in1=g_[:], op=ALU.mult), reads=[hb, gb], writes=[byb[ct]])

                def evac(i, c, th, ps, pb):
                    yo.evac(i, c, th, ps, pb, ND)

                linear(st, "wout", list(range(ND)), ND, lambda kt, th: HN[:, kt, th * TH:(th + 1) * TH], lambda kt, th: [byb[kt]], ring, Banks([0, 1, 2, 3]), evac)
                st.emit()

        def rope_apply(st, out_ap, outbuf, x_t, xb, xs_t, xsb, np_, csl, tmp_ring, CS, SN, bcs):
            t1, b1 = tmp_ring.nxt()
            st.add("vector", lambda e: e.tensor_tensor(out=t1[0:np_, :], in0=x_t, in1=CS[0:np_, csl], op=ALU.mult), reads=[xb, bcs], writes=[b1])
            t2, b2 = tmp_ring.nxt()
            st.add("gpsimd", lambda e: e.tensor_tensor(out=t2[0:np_, :], in0=xs_t, in1=SN[0:np_, csl], op=ALU.mult), reads=[xsb, bcs], writes=[b2])
            st.add("vector", lambda e: e.tensor_tensor(out=out_ap, in0=t1[0:np_, :], in1=t2[0:np_, :], op=ALU.add), reads=[b1, b2], writes=[outbuf])

        def stage_kv():
            st = Stage(nc, "kv")
            with contextlib.ExitStack() as es:
                ring = Ring(es, "wr", 3, ND * 128)
                alloc_ps(es)
                CF = es.enter_context(sbt("CF", [128, KK + 2, T], F32))
                CB = es.enter_context(sbt("CB", [128, KK + 1, T], BF16))
                sq = Ring(es, "sq", 2, TH, F32)
                tr = Ring(es, "tr", 2, T, F32)
                CS = es.enter_context(sbt("CS", [64, T], F32))
                SN = es.enter_context(sbt("SN", [64, T], F32))
                bcs = Buf()
                st.add("scalar", lambda e: e.dma_start(out=CS[:], in_=CSd[:, :]), writes=[bcs], dma=True)
                st.add("scalar", lambda e: e.dma_start(out=SN[:], in_=SNd[:, :]), writes=[bcs], dma=True)
                rt = es.enter_context(sbt("rt", [128, TH], F32))
                RC = es.enter_context(sbt("RC", [128, T], F32))
                rtb, brc = Buf(), Buf()
                bcf = [Buf() for _ in range(KK + 2)]
                bcb = Buf()
                bhn = Buf()
                ssb = Banks([6, 7])
                ss = [ssb.nxt() for _ in range(2)]

                def evac(i, c, th, ps, pb):
                    tsl = slice(th * TH, (th + 1) * TH)
                    st.add("vector", lambda e: e.tensor_copy(out=CF[:, c, tsl], in_=ps[:, :]), reads=[pb], writes=[bcf[c]])
                    if c < KK:
                        qt, qb = sq.nxt()
                        st.add("scalar", lambda e: e.activation(out=qt[:], in_=CF[:, c, tsl], func=AF.Square), reads=[bcf[c]], writes=[qb])
                        sps, spb = ss[th]
                        st.add("tensor", lambda e: e.matmul(sps[:, :], ONES[:, :], qt[:], start=(c == 0), stop=(c == KK - 1)), reads=[qb], writes=[spb])
                        if c == KK - 1:
                            rstd_from(st, sps[:, :], spb, RC[:, tsl], brc, float(cfg.KL), rt[:], rtb)

                linear(st, "kvd", list(range(KK + 2)), ND, lambda kt, th: HN[:, kt, th * TH:(th + 1) * TH], lambda kt, th: [bhn], ring, Banks([0, 1, 2, 3]), evac)
                for c in range(KK):
                    st.add("vector", lambda e, c=c: e.scalar_tensor_tensor(out=CB[:, c, :], in0=CF[:, c, :], scalar=vcol("kln", c), in1=RC[:, :], op0=ALU.mult, op1=ALU.mult), reads=[bcf[c], brc], writes=[bcb])
                rope_apply(st, CB[0:64, KK, :], bcb, CF[0:64, KK, :], bcf[KK], CF[0:64, KK + 1, :], bcf[KK + 1], 64, slice(0, T), tr, CS, SN, bcs)
                bx = Buf()
                for c in range(KK):
                    st.add("scalar", lambda e, c=c: e.dma_start(out=kv_i[c * 128:(c + 1) * 128, :], in_=CB[:, c, :]), reads=[bcb], writes=[bx], dma=True)
                st.add("scalar", lambda e: e.dma_start(out=kv_i[KK * 128:KK * 128 + 64, :], in_=CB[0:64, KK, :]), reads=[bcb], writes=[bx], dma=True)
                exchange(st, None, kv_i, kv_o, x_sem[2], [bx])
                st.emit()

        def stage_q():
            st = Stage(nc, "qlat")
            with contextlib.ExitStack() as es:
                alloc_ps(es)
                ring = Ring(es, "wr", 3, ND * 128)
                CQF = es.enter_context(sbt("CQF", [128, KQ, T], F32))
                CQ = es.enter_context(sbt("CQ", [128, KQ, T], BF16))
                RQ = es.enter_context(sbt("RQ", [128, T], F32))
                rt = es.enter_context(sbt("rt", [128, TH], F32))
                sq = Ring(es, "sq", 2, TH, F32)
                rtb, brq = Buf(), Buf()
                bhn = Buf()
                bcq = Buf()
                bcqf = [Buf() for _ in range(KQ)]
                ssb = Banks([4, 5])
                ss = [ssb.nxt() for _ in range(2)]

                def evac_cq(i, c, th, ps, pb):
                    tsl = slice(th * TH, (th + 1) * TH)
                    st.add("vector", lambda e: e.tensor_copy(out=CQF[:, c, tsl], in_=ps[:, :]), reads=[pb], writes=[bcqf[c]])
                    qt, qb = sq.nxt()
                    st.add("scalar", lambda e: e.activation(out=qt[:], in_=CQF[:, c, tsl], func=AF.Square), reads=[bcqf[c]], writes=[qb])
                    sps, spb = ss[th]
                    st.add("tensor", lambda e: e.matmul(sps[:, :], ONES[:, :], qt[:], start=(c == 0), stop=(c == KQ - 1)), reads=[qb], writes=[spb])
                    if c == KQ - 1:
                        rstd_from(st, sps[:, :], spb, RQ[:, tsl], brq, float(cfg.QL), rt[:], rtb)

                linear(st, "wdq", list(range(KQ)), ND, lambda kt, th: HN[:, kt, th * TH:(th + 1) * TH], lambda kt, th: [bhn], ring, Banks([0, 1, 2, 3]), evac_cq)
                for c in range(KQ):
                    st.add("vector", lambda e, c=c: e.scalar_tensor_tensor(out=CQ[:, c, :], in0=CQF[:, c, :], scalar=vcol("qn", c), in1=RQ[:, :], op0=ALU.mult, op1=ALU.mult), reads=[bcqf[c], brq], writes=[bcq])
                    st.add("scalar", lambda e, c=c: e.dma_start(out=CQd[c * 128:(c + 1) * 128, :], in_=CQ[:, c, :]), reads=[bcq], writes=[Buf()], dma=True)
                st.emit()

        def stage_attn():
            st = Stage(nc, "attn")
            scale = 192.0 ** -0.5
            with contextlib.ExitStack() as es:
                alloc_ps(es, 6)
                PT = [es.enter_context(pst(f"PT{i}", [128, 1024], BF16)) for i in range(2)]
                ringq = Ring(es, "wq", 4, KQ * 128)
                ringk = Ring(es, "wk", 3, KK * 128)
                CQ = es.enter_context(sbt("CQ", [128, KQ, T], BF16))
                CKV = es.enter_context(sbt("CKV", [128, KK, 2 * T], BF16))
                KR = es.enter_context(sbt("KR", [64, 2 * T], BF16))
                CS = es.enter_context(sbt("CS", [64, T], F32))
                SN = es.enter_context(sbt("SN", [64, T], F32))
                QN = Ring(es, "qn", 2, T, BF16)
                QRF = Ring(es, "qrf", 2, T, F32)
                QR = Ring(es, "qr", 2, T, BF16)
                tr = Ring(es, "tr", 2, T, F32)
                KN = Ring(es, "kn", 2, 2 * T, BF16)
                VV = Ring(es, "vv", 2, 16 * 128, BF16)
                PP = Ring(es, "pp", 2, 2 * T, BF16)
                PTS = Ring(es, "pts", 2, 2 * T, BF16)
                OS = Ring(es, "os", 2, 128, BF16)
                SM = Ring(es, "sm", 4, 8, F32)
                ET = Ring(es, "et", 3, TH, F32)
                bpt = [Buf(), Buf()]
                bhn = Buf()
                bcq = Buf()
                bckv = Buf()
                bcs = Buf()
                bao = [Buf() for _ in range(NH)]
                st.add("scalar", lambda e: e.dma_start(out=CS[:], in_=CSd[:, :]), writes=[bcs], dma=True)
                st.add("scalar", lambda e: e.dma_start(out=SN[:], in_=SNd[:, :]), writes=[bcs], dma=True)
                for c in range(KQ):
                    st.add("scalar", lambda e, c=c: e.dma_start(out=CQ[:, c, :], in_=CQd[c * 128:(c + 1) * 128, :]), writes=[bcq], dma=True)
                KVR_ = KK * 128 + 64
                for c in range(KK):
                    st.add("scalar", lambda e, c=c: e.dma_start(out=CKV[:, c, 0:T], in_=kv_o[c * 128:(c + 1) * 128, :]), writes=[bckv], dma=True, extra=[(x_sem[2], 1)])
                    st.add("scalar", lambda e, c=c: e.dma_start(out=CKV[:, c, T:2 * T], in_=kv_i[c * 128:(c + 1) * 128, :]), writes=[bckv], dma=True)
                st.add("scalar", lambda e: e.dma_start(out=KR[:, 0:T], in_=kv_o[KK * 128:KK * 128 + 64, :]), writes=[bckv], dma=True, extra=[(x_sem[2], 1)])
                st.add("scalar", lambda e: e.dma_start(out=KR[:, T:2 * T], in_=kv_i[KK * 128:KK * 128 + 64, :]), writes=[bckv], dma=True)
                bS = Banks([0, 1, 2, 3])
                SB = [bS.nxt() for _ in range(4)]
                bO = Banks([4, 5])
                bproj = bS
                for h in range(NH):
                    qn_t, qn_b = QN.nxt()
                    qrf = {}

                    def evac_q(i, c, th, ps, pb):
                        tsl = slice(th * TH, (th + 1) * TH)
                        kind = c % 3
                        if kind == 0:
                            st.add("scalar", lambda e: e.activation(out=qn_t[:, tsl], in_=ps[:, :], func=AF.Copy), reads=[pb], writes=[qn_b])
                        else:
                            if kind not in qrf:
                                qrf[kind] = QRF.nxt()
                            t_, b_ = qrf[kind]
                            st.add("vector", lambda e: e.tensor_copy(out=t_[0:64, tsl], in_=ps[0:64, :]), reads=[pb], writes=[b_])

                    linear(st, "wuq", [3 * h, 3 * h + 1, 3 * h + 2], KQ, lambda kt, th: CQ[:, kt, th * TH:(th + 1) * TH], lambda kt, th: [bcq], ringq, bproj, evac_q)
                    qr_t, qr_b = QR.nxt()
                    rope_apply(st, qr_t[0:64, :], qr_b, qrf[1][0][0:64, :], qrf[1][1], qrf[2][0][0:64, :], qrf[2][1], 64, slice(0, T), tr, CS, SN, bcs)
                    kn_t, kn_b = KN.nxt()
                    wt, wb = ringk.nxt()
                    load_w(st, "kvu", 2 * h, wt, wb, KK * 128, h == 0)
                    for kb in range(4):
                        ps, pb = bproj.nxt()
                        for kt in range(KK):
                            st.add("tensor", lambda e, ps=ps, wt=wt, kt=kt, kb=kb: e.matmul(ps[:, :], wt[:, kt * 128:(kt + 1) * 128], CKV[:, kt, kb * TH:(kb + 1) * TH], start=(kt == 0), stop=(kt == KK - 1)), reads=[wb, bckv], writes=[pb])
                        st.add("scalar", lambda e, ps=ps, kb=kb, kn_t=kn_t: e.activation(out=kn_t[:, kb * TH:(kb + 1) * TH], in_=ps[:, :], func=AF.Copy), reads=[pb], writes=[kn_b])
                    vv_t, vv_b = VV.nxt()
                    wt2, wb2 = ringk.nxt()
                    load_w(st, "kvu", 2 * h + 1, wt2, wb2, KK * 128, False)
                    for g4 in range(4):
                        ps, pb = bproj.nxt()
                        for j in range(4):
                            kt16 = g4 * 4 + j
                            for kt in range(KK):
                                st.add("tensor", lambda e, ps=ps, wt2=wt2, kt=kt, kt16=kt16, j=j: e.matmul(ps[:, j * 128:(j + 1) * 128], CKV[:, kt, kt16 * 128:(kt16 + 1) * 128], wt2[:, kt * 128:(kt + 1) * 128], start=(kt == 0), stop=(kt == KK - 1)), reads=[wb2, bckv], writes=[pb])
                        st.add("vector", lambda e, ps=ps, g4=g4, vv_t=vv_t: e.tensor_copy(out=vv_t[:, g4 * 512:(g4 + 1) * 512], in_=ps[:, :]), reads=[pb], writes=[vv_b])
                    for qi in range(8):
                        qsl = slice(qi * 128, (qi + 1) * 128)
                        nown = (qi + 1) * 128
                        blocks = [(0, 0, 512), (1, 512, 512)]
                        o0 = 0
                        while o0 < nown:
                            w = min(512, nown - o0)
                            blocks.append((2 + o0 // 512, T + o0, w))
                            o0 += 512
                        for (bi, k0, w) in blocks:
                            ps, pb = SB[bi]
                            st.add("tensor", lambda e, ps=ps, k0=k0, w=w: e.matmul(ps[:, 0:w], qn_t[:, qsl], kn_t[:, k0:k0 + w], start=True, stop=False), reads=[qn_b, kn_b], writes=[pb])
                            st.add("tensor", lambda e, ps=ps, k0=k0, w=w: e.matmul(ps[:, 0:w], qr_t[0:64, qsl], KR[0:64, k0:k0 + w], start=False, stop=True), reads=[qr_b, bckv], writes=[pb])
                        dbi = 2 + (qi * 128) // 512
                        dps, dpb = SB[dbi]
                        dc = (qi * 128) % 512
                        st.add("vector", lambda e, dps=dps, dc=dc: e.tensor_tensor(out=dps[:, dc:dc + 128], in0=dps[:, dc:dc + 128], in1=CMASK[:, :], op=ALU.add), reads=[dpb], writes=[dpb])
                        sm, smb = SM.nxt()
                        nblk = len(blocks)
                        for ii, (bi, k0, w) in enumerate(blocks):
                            ps, pb = SB[bi]
                            st.add("vector", lambda e, ps=ps, w=w, ii=ii, sm=sm: e.reduce_max(out=sm[:, ii:ii + 1], in_=ps[:, 0:w], axis=AX.X), reads=[pb], writes=[smb])
                        st.add("vector", lambda e, sm=sm, nblk=nblk: e.reduce_max(out=sm[:, 4:5], in_=sm[:, 0:nblk], axis=AX.X), reads=[smb], writes=[smb])
                        st.add("vector", lambda e, sm=sm: e.tensor_scalar(out=sm[:, 5:6], in0=sm[:, 4:5], scalar1=-scale, scalar2=None, op0=ALU.mult), reads=[smb], writes=[smb])
                        st.add("vector", lambda e, sm=sm: e.tensor_tensor(out=sm[:, 6:7], in0=sm[:, 5:6], in1=FLAG[:, 1:2], op=ALU.add), reads=[smb], writes=[smb])
                        pp, ppb = PP.nxt()
                        sm2, sm2b = SM.nxt()
                        for ii, (bi, k0, w) in enumerate(blocks):
                            ps, pb = SB[bi]
                            bcol = 6 if bi < 2 else 5
                            et, etb = ET.nxt()
                            st.add("vector", lambda e, ps=ps, w=w, sm=sm, bcol=bcol, et=et: e.tensor_scalar(out=et[:, 0:w], in0=ps[:, 0:w], scalar1=float(scale), scalar2=sm[:, bcol:bcol + 1], op0=ALU.mult, op1=ALU.add), reads=[pb, smb], writes=[etb])
                            st.add("scalar", lambda e, w=w, k0=k0, ii=ii, pp=pp, sm2=sm2, et=et: e.activation(out=pp[:, k0:k0 + w], in_=et[:, 0:w], func=AF.Exp, accum_out=sm2[:, ii:ii + 1]),
                                   reads=[etb], writes=[ppb, sm2b])
                        st.add("vector", lambda e, sm2=sm2, nblk=nblk: e.reduce_sum(out=sm2[:, 4:5], in_=sm2[:, 0:nblk], axis=AX.X), reads=[sm2b], writes=[sm2b])
                        st.add("vector", lambda e, sm2=sm2: e.reciprocal(out=sm2[:, 5:6], in_=sm2[:, 4:5]), reads=[sm2b], writes=[sm2b])
                        ktiles = list(range(8)) + [8 + i for i in range(qi + 1)]
                        pts, ptsb = PTS.nxt()
                        for g8 in range(0, len(ktiles), 8):
                            grp = ktiles[g8:g8 + 8]
                            pt = PT[(g8 // 8) % 2]
                            ptb = bpt[(g8 // 8) % 2]
                            for j, ktl in enumerate(grp):
                                st.add("tensor", lambda e, pt=pt, j=j, ktl=ktl, pp=pp: e.transpose(pt[:, j * 128:(j + 1) * 128], pp[:, ktl * 128:(ktl + 1) * 128], IDB[:, :]), reads=[ppb], writes=[ptb])
                            n_ = len(grp) * 128
                            k00 = grp[0] * 128
                            st.add("vector" if (g8 // 8) % 2 == 0 else "scalar",
                                   (lambda e, pt=pt, n_=n_, k00=k00, pts=pts: e.tensor_copy(out=pts[:, k00:k00 + n_], in_=pt[:, 0:n_])) if (g8 // 8) % 2 == 0 else
                                   (lambda e, pt=pt, n_=n_, k00=k00, pts=pts: e.activation(out=pts[:, k00:k00 + n_], in_=pt[:, 0:n_], func=AF.Copy)),
                                   reads=[ptb], writes=[ptsb])
                        ops_, opb = bO.nxt()
                        for ii, ktl in enumerate(ktiles):
                            st.add("tensor", lambda e, ops_=ops_, ktl=ktl, ii=ii, pts=pts, vv_t=vv_t, nk=len(ktiles): e.matmul(ops_[:, 0:128], pts[:, ktl * 128:(ktl + 1) * 128], vv_t[:, ktl * 128:(ktl + 1) * 128], start=(ii == 0), stop=(ii == nk - 1)),
                                   reads=[ptsb, vv_b], writes=[opb])
                        os_, osb = OS.nxt()
                        st.add("vector", lambda e, ops_=ops_, os_=os_, sm2=sm2: e.tensor_scalar(out=os_[:, :], in0=ops_[:, 0:128], scalar1=sm2[:, 5:6], scalar2=None, op0=ALU.mult), reads=[opb, sm2b], writes=[osb])
                        pt = PT[0]
                        st.add("tensor", lambda e, pt=pt, os_=os_: e.transpose(pt[:, 0:128], os_[:, :], IDB[:, :]), reads=[osb], writes=[bpt[0]])
                        st.add("vector", lambda e, pt=pt, h=h, qsl=qsl: e.tensor_copy(out=HN[:, h, qsl], in_=pt[:, 0:128]), reads=[bpt[0]], writes=[bao[h], bhn])
                st.emit()
            return

        def stage_wo():
            st = Stage(nc, "wo")
            with contextlib.ExitStack() as es:
                alloc_ps(es)
                ring = Ring(es, "wr", 3, ND * 128)
                yo = YOut(st, es, 1.0)
                bao = Buf()

                def evac(i, c, th, ps, pb):
                    yo.evac(i, c, th, ps, pb, ND)

                linear(st, "wo", list(range(ND)), NH, lambda kt, th: HN[:, kt, th * TH:(th + 1) * TH], lambda kt, th: [bao], ring, Banks([0, 1, 2, 3]), evac)
                st.emit()

        xs = [xT[i] for i in range(ND)]
        hs = [Hd[i] for i in range(ND)]
        ys = [Yd[i] for i in range(ND)]
        os_ = [outT[i] for i in range(ND)]
        import os
        nst = int(os.environ.get("KSTAGES", "99"))
        plan = [
            lambda: stage_prep(),
            lambda: stage_epi("n0", xs, None, None, "g00", None),
            lambda: stage_ffn(0),
            lambda: stage_epi("e0", xs, ys, "g01", "g02", hs),
            lambda: stage_lru_a(),
            lambda: stage_lru_b(),
            lambda: stage_lru_c(),
            lambda: stage_epi("e1", hs, ys, "g03", "g04", hs),
            lambda: stage_ffn(1),
            lambda: stage_epi("e2", hs, ys, "g05", "g10", hs),
            lambda: stage_ffn(2),
            lambda: stage_epi("e3", hs, ys, "g11", "kvn", hs),
            lambda: stage_kv(),
            lambda: stage_renorm("rn", "g12"),
            lambda: stage_q(),
            lambda: stage_attn(),
            lambda: stage_wo(),
            lambda: stage_epi("e4", hs, ys, "g13", "g14", hs),
            lambda: stage_ffn(3),
            lambda: stage_epi("e5", hs, ys, "g15", None, os_),
        ]
        for f in plan[:nst]:
            f()
        if os.environ.get("KDUMP"):
            kd = os.environ["KDUMP"]
            st = Stage(nc, "dump")
            if kd == "CS":
                with contextlib.ExitStack() as es:
                    rg = Ring(es, "dmp", 2, T, F32)
                    for dt, srcd in enumerate([CSd, SNd]):
                        t_, b_ = rg.nxt()
                        st.add("vector", lambda e, t_=t_: e.memset(t_[:], 0.0), writes=[b_])
                        st.add("sync", lambda e, t_=t_, srcd=srcd: e.dma_start(out=t_[0:64, :], in_=srcd[:, :]), reads=[b_], writes=[b_], dma=True)
                        st.add("sync", lambda e, dt=dt, t_=t_: e.dma_start(out=outT[dt], in_=t_[:]), reads=[b_], writes=[Buf()], dma=True)
                    st.emit()
            if kd in ("KV", "CQ"):
                with contextlib.ExitStack() as es:
                    rb_ = Ring(es, "dmpb", 2, T, BF16)
                    rg = Ring(es, "dmp", 2, T, F32)
                    srcb = kv_i if kd == "KV" else CQd
                    nrow = srcb.shape[0]
                    for dt in range(min(ND, (nrow + 127) // 128)):
                        r1 = min(nrow, (dt + 1) * 128)
                        np_ = r1 - dt * 128
                        tb, bb = rb_.nxt()
                        t_, b_ = rg.nxt()
                        st.add("vector", lambda e, t_=t_: e.memset(t_[:], 0.0), writes=[b_])
                        st.add("sync", lambda e, dt=dt, tb=tb, np_=np_, r1=r1: e.dma_start(out=tb[0:np_, :], in_=srcb[dt * 128:r1, :]), writes=[bb], dma=True)
                        st.add("vector", lambda e, t_=t_, tb=tb, np_=np_: e.tensor_copy(out=t_[0:np_, :], in_=tb[0:np_, :]), reads=[bb, b_], writes=[b_])
                        st.add("sync", lambda e, dt=dt, t_=t_: e.dma_start(out=outT[dt], in_=t_[:]), reads=[b_], writes=[Buf()], dma=True)
                    st.emit()
            src_t = {"H": Hd, "Y": Yd}.get(kd)
            with contextlib.ExitStack() as es:
                if src_t is None:
                    es.close()
                rg = Ring(es, "dmp", 2, T, F32)
                for dt in range(ND if src_t is not None else 0):
                    t_, b_ = rg.nxt()
                    st.add("sync", lambda e, dt=dt, t_=t_: e.dma_start(out=t_[:], in_=src_t[dt]), writes=[b_], dma=True)
                    st.add("sync", lambda e, dt=dt, t_=t_: e.dma_start(out=outT[dt], in_=t_[:]), reads=[b_], writes=[Buf()], dma=True)
                if src_t is not None:
                    st.emit()
    return nc


_CACHE = {}


def run(cfg, inp):
    groups = weight_groups(cfg, inp)
    vecs, vcols = small_vectors(cfg, inp)
    gshapes = {n: a.shape for n, a in groups.items()}
    key = (cfg.D, cfg.F, cfg.NH)
    if key not in _CACHE:
        _CACHE[key] = build(cfg, gshapes, vcols, vecs.shape[1])
    nc = _CACHE[key]
    x = np.asarray(inp["x"], np.float32)
    pos = np.asarray(inp["positions"])
    B = x.shape[0]
    invf = (10000.0 ** (-np.arange(0, 64, 2, dtype=np.float32) / 64.0)).astype(np.float32)
    invf = np.concatenate([invf, invf]).reshape(64, 1)
    ident = np.eye(128, dtype=np.float32)
    cmask = np.where(np.arange(128)[None, :] <= np.arange(128)[:, None], 0.0, NEG).astype(np.float32)
    in_maps = []
    for c in range(NCORES):
        b, half = c // 2, c % 2
        xs = x[b, half * T:(half + 1) * T, :]
        xT = np.ascontiguousarray(xs.T).reshape(cfg.ND, 128, T)
        p = np.asarray(pos[b, half * T:(half + 1) * T], np.int32)
        m = {"xT": xT, "vecs": vecs, "pos": np.ascontiguousarray(np.broadcast_to(p[None, :], (64, T))),
             "invf": invf, "ident": ident, "cmask": cmask,
             "flag": np.ascontiguousarray(np.broadcast_to(np.array([[float(half), 0.0 if half else NEGB]], np.float32), (128, 2)))}
        for n, a in groups.items():
            nl = a.shape[0] // NCORES
            m[f"w_{n}"] = a[c * nl:(c + 1) * nl]
        in_maps.append(m)
    res = run_bass_kernel_spmd(nc, in_maps, core_ids=list(range(NCORES)))
    out = np.zeros((B, 2 * T, cfg.D), np.float32)
    for c in range(NCORES):
        b, half = c // 2, c % 2
        o = res.results[c]["outT"].reshape(cfg.D, T)
        out[b, half * T:(half + 1) * T, :] = o.T
    return out


def kernel(**inputs):
    return run(Cfg(), inputs)
```
